# Optimizing a Trainium2 kernel written in Bass

```python
import jax, jax.numpy as jnp
from jax import lax
import numpy as np

D_MODEL = 2048
BATCH = 8
SEQ = 2048
DEPTH = 2

BRANCH_WIDTH = D_MODEL // 2
N_BRANCH = 4
SSD_HEAD_DIM = 64
SSD_HEADS = BRANCH_WIDTH // SSD_HEAD_DIM
SSD_GROUPS = 4
SSD_STATE = 128
SSD_CONV = 4
SSD_CHUNK = 128
SSD_XBC = BRANCH_WIDTH + 2 * SSD_GROUPS * SSD_STATE
ATTN_HEAD_DIM = 64
ATTN_Q_HEADS = BRANCH_WIDTH // ATTN_HEAD_DIM
ATTN_KV_HEADS = 4
KV_WIDTH = ATTN_KV_HEADS * ATTN_HEAD_DIM
WINDOW = 128
ROPE_THETA = 10000.0
SCONV_WIDTH = 3
LRU_WIDTH = BRANCH_WIDTH
LRU_BLOCKS = 16
LRU_BLOCK_DIM = LRU_WIDTH // LRU_BLOCKS
LRU_CONV = 4
LRU_C = 8.0
LN_EPS = 1e-5
RMS_EPS = 1e-5
DEEPNORM_ALPHA = (2.0 * DEPTH) ** 0.25
DEEPNORM_BETA = (8.0 * DEPTH) ** -0.25
ADA_INIT = 0.5

COL_SIZES = (
    BRANCH_WIDTH, SSD_XBC, SSD_HEADS,
    BRANCH_WIDTH, KV_WIDTH, KV_WIDTH, BRANCH_WIDTH,
    BRANCH_WIDTH, BRANCH_WIDTH, BRANCH_WIDTH, BRANCH_WIDTH,
    LRU_WIDTH, LRU_WIDTH,
    N_BRANCH * D_MODEL,
)
IN_COLS = sum(COL_SIZES)
SPLIT_POINTS = tuple(sum(COL_SIZES[:i + 1]) for i in range(len(COL_SIZES) - 1))

kernel_name = "hybrid_gated_parallel_mixers_deepnorm_adaln"


def layer_norm(x, w, b):
    xf = x.astype(jnp.float32)
    mu = xf.mean(-1, keepdims=True)
    var = jnp.square(xf - mu).mean(-1, keepdims=True)
    y = (xf - mu) * lax.rsqrt(var + LN_EPS) * w.astype(jnp.float32) + b.astype(jnp.float32)
    return y.astype(x.dtype)


def rms_norm(x):
    xf = x.astype(jnp.float32)
    return xf * lax.rsqrt(jnp.square(xf).mean(-1, keepdims=True) + RMS_EPS)


def causal_dwconv(x, w, b=None):
    k = w.shape[0]
    y = lax.conv_general_dilated(
        x, w[:, None, :].astype(x.dtype), window_strides=(1,), padding=[(k - 1, 0)],
        dimension_numbers=('NWC', 'WIO', 'NWC'), feature_group_count=x.shape[-1])
    return y if b is None else y + b.astype(x.dtype)


def rope(x, pos):
    half = x.shape[-1] // 2
    inv_freq = ROPE_THETA ** (-jnp.arange(half, dtype=jnp.float32) / half)
    ang = pos.astype(jnp.float32)[..., None] * inv_freq
    cos, sin = jnp.cos(ang)[:, :, None, :], jnp.sin(ang)[:, :, None, :]
    xf = x.astype(jnp.float32)
    x1, x2 = xf[..., :half], xf[..., half:]
    return jnp.concatenate([x1 * cos - x2 * sin, x2 * cos + x1 * sin], -1).astype(x.dtype)


def ssd_branch(z, xbc, dt, conv_w, conv_b, dt_bias, a_log, d_skip, norm_w):
    f32 = jnp.float32
    b, l, _ = z.shape
    nc, q, g, r = l // SSD_CHUNK, SSD_CHUNK, SSD_GROUPS, SSD_HEADS // SSD_GROUPS
    xbc = jax.nn.silu(causal_dwconv(xbc, conv_w, conv_b))
    xs, bm, cm = jnp.split(xbc, [BRANCH_WIDTH, BRANCH_WIDTH + SSD_GROUPS * SSD_STATE], axis=-1)
    xs = xs.astype(f32).reshape(b, l, SSD_HEADS, SSD_HEAD_DIM)
    dt = jax.nn.softplus(dt.astype(f32) + dt_bias.astype(f32))
    a = -jnp.exp(a_log.astype(f32))
    xdt = (xs * dt[..., None]).reshape(b, nc, q, g, r, SSD_HEAD_DIM)
    da = (dt * a).reshape(b, nc, q, g, r)
    bm = bm.astype(f32).reshape(b, nc, q, g, SSD_STATE)
    cm = cm.astype(f32).reshape(b, nc, q, g, SSD_STATE)
    cum = jnp.cumsum(da, axis=2)
    causal = jnp.tril(jnp.ones((q, q), bool))[None, None, :, :, None, None]
    seg = cum[:, :, :, None] - cum[:, :, None, :]
    decay = jnp.exp(jnp.where(causal, seg, -jnp.inf))
    cb = jnp.einsum('bclgn,bcsgn->bclsg', cm, bm)
    y_diag = jnp.einsum('bclsg,bclsgr,bcsgrp->bclgrp', cb, decay, xdt)
    decay_to_end = jnp.exp(cum[:, :, -1:] - cum)
    states = jnp.einsum('bcsgn,bcsgr,bcsgrp->bcgrpn', bm, decay_to_end, xdt)
    chunk_decay = jnp.exp(cum[:, :, -1])

    def step(h, inp):
        s, dcy = inp
        return h * dcy[..., None, None] + s, h

    h0 = jnp.zeros((b, g, r, SSD_HEAD_DIM, SSD_STATE), f32)
    _, prev = lax.scan(step, h0, (jnp.moveaxis(states, 1, 0), jnp.moveaxis(chunk_decay, 1, 0)))
    prev = jnp.moveaxis(prev, 0, 1)
    y_off = jnp.einsum('bclgn,bcgrpn,bclgr->bclgrp', cm, prev, jnp.exp(cum))
    y = (y_diag + y_off).reshape(b, l, SSD_HEADS, SSD_HEAD_DIM) + xs * d_skip.astype(f32)[:, None]
    y = y.reshape(b, l, BRANCH_WIDTH) * jax.nn.silu(z.astype(f32))
    return (rms_norm(y) * norm_w.astype(f32)).astype(z.dtype)


def swa_branch(q, k, v, gate, pos, sinks):
    f32 = jnp.float32
    b, l, _ = q.shape
    nb, r = l // WINDOW, ATTN_Q_HEADS // ATTN_KV_HEADS
    q = rope(q.reshape(b, l, ATTN_Q_HEADS, ATTN_HEAD_DIM), pos)
    k = rope(k.reshape(b, l, ATTN_KV_HEADS, ATTN_HEAD_DIM), pos)
    v = v.reshape(b, l, ATTN_KV_HEADS, ATTN_HEAD_DIM)
    qb = q.reshape(b, nb, WINDOW, ATTN_KV_HEADS, r, ATTN_HEAD_DIM)

    def with_prev(t):
        tb = t.reshape(b, nb, WINDOW, ATTN_KV_HEADS, ATTN_HEAD_DIM)
        prev = jnp.pad(tb, ((0, 0), (1, 0), (0, 0), (0, 0), (0, 0)))[:, :-1]
        return jnp.concatenate([prev, tb], axis=2)

    kb, vb = with_prev(k), with_prev(v)
    s = jnp.einsum('bnqkrd,bnskd->bnkrqs', qb, kb).astype(f32) * (ATTN_HEAD_DIM ** -0.5)
    qi = jnp.arange(WINDOW)[:, None]
    sj = jnp.arange(2 * WINDOW)[None, :]
    rel = qi + WINDOW - sj
    band = (rel >= 0) & (rel < WINDOW)
    blk = jnp.arange(nb)[:, None, None]
    valid = band[None] & ((blk > 0) | (sj[None] >= WINDOW))
    s = jnp.where(valid[None, :, None, None], s, -jnp.inf)
    sink = sinks.astype(f32).reshape(ATTN_KV_HEADS, r)[None, None, :, :, None, None]
    m = jnp.maximum(s.max(-1, keepdims=True), sink)
    e = jnp.exp(s - m)
    p = e / (e.sum(-1, keepdims=True) + jnp.exp(sink - m))
    o = jnp.einsum('bnkrqs,bnskd->bnqkrd', p.astype(vb.dtype), vb).reshape(b, l, BRANCH_WIDTH)
    return o * jax.nn.silu(gate)


def shortconv_branch(bg, cg, xs, gate, conv_w):
    return bg * causal_dwconv(cg * xs, conv_w) * jax.nn.silu(gate)


def rglru_branch(xs, gate, conv_w, conv_b, w_a, b_a, w_x, b_x, lam):
    f32 = jnp.float32
    b, l, _ = xs.shape
    xs = causal_dwconv(xs, conv_w, conv_b)
    xh = xs.reshape(b, l, LRU_BLOCKS, LRU_BLOCK_DIM)
    rg = jax.nn.sigmoid(jnp.einsum('blhi,hij->blhj', xh, w_a).reshape(b, l, LRU_WIDTH) + b_a)
    ig = jax.nn.sigmoid(jnp.einsum('blhi,hij->blhj', xh, w_x).reshape(b, l, LRU_WIDTH) + b_x)
    log_a = -LRU_C * rg.astype(f32) * jax.nn.softplus(-lam.astype(f32))
    a = jnp.exp(log_a)
    u = jnp.sqrt(-jnp.expm1(2.0 * log_a)) * (ig * xs).astype(f32)

    def combine(left, right):
        a1, b1 = left
        a2, b2 = right
        return a1 * a2, a2 * b1 + b2

    _, h = lax.associative_scan(combine, (a, u), axis=1)
    return (h * jax.nn.silu(gate.astype(f32))).astype(xs.dtype)


def hybrid_layer(x, pos, c_act, w_ada, b_ada, w_in, b_gate, ssd_conv_w, ssd_conv_b, ssd_dt_bias,
                 ssd_a_log, ssd_d, ssd_norm_w, attn_sinks, sconv_w, lru_conv_w, lru_conv_b,
                 lru_w_a, lru_b_a, lru_w_x, lru_b_x, lru_lambda, w_branch, w_out, ln_w, ln_b):
    b, l, d = x.shape
    ada = c_act @ w_ada + b_ada
    shift, scale, gate = jnp.split(ada, 3, axis=-1)
    h = x * (1 + scale[:, None]) + shift[:, None]
    proj = h @ w_in
    (a_z, a_xbc, a_dt, b_q, b_k, b_v, b_g, c_b, c_c, c_x, c_g, d_x, d_g, merge) = jnp.split(
        proj, SPLIT_POINTS, axis=-1)
    ys = (
        ssd_branch(a_z, a_xbc, a_dt, ssd_conv_w, ssd_conv_b, ssd_dt_bias, ssd_a_log, ssd_d, ssd_norm_w),
        swa_branch(b_q, b_k, b_v, b_g, pos, attn_sinks),
        shortconv_branch(c_b, c_c, c_x, c_g, sconv_w),
        rglru_branch(d_x, d_g, lru_conv_w, lru_conv_b, lru_w_a, lru_b_a, lru_w_x, lru_b_x, lru_lambda),
    )
    gates = jax.nn.sigmoid(merge.reshape(b, l, N_BRANCH, d) + b_gate)
    m = gates[:, :, 0] * (ys[0] @ w_branch[0])
    for i in range(1, N_BRANCH):
        m = m + gates[:, :, i] * (ys[i] @ w_branch[i])
    out = m @ w_out
    return layer_norm(DEEPNORM_ALPHA * x + gate[:, None] * out, ln_w, ln_b)


def setup_inputs(seed: int = 0) -> dict:
    key = jax.random.key(seed)
    ks = jax.random.split(key, 32)
    f32 = jnp.float32

    def nrm(k, shape, std):
        return std * jax.random.normal(k, shape, f32)

    dt0 = jnp.exp(jax.random.uniform(ks[8], (DEPTH, SSD_HEADS), f32, np.log(1e-3), np.log(1e-1)))
    a_pow = jax.random.uniform(ks[19], (DEPTH, LRU_WIDTH), f32, 0.9, 0.999)
    sig = a_pow ** (1.0 / LRU_C)
    offsets = jax.random.randint(ks[2], (BATCH, 1), 0, SEQ, jnp.int32)
    return {
        "x": nrm(ks[0], (BATCH, SEQ, D_MODEL), 1.0),
        "c": nrm(ks[1], (BATCH, D_MODEL), 1.0),
        "positions": offsets + jnp.arange(SEQ, dtype=jnp.int32)[None, :],
        "w_ada": nrm(ks[3], (DEPTH, D_MODEL, 3 * D_MODEL), ADA_INIT * D_MODEL ** -0.5),
        "b_ada": nrm(ks[4], (DEPTH, 3 * D_MODEL), 0.01),
        "w_in": nrm(ks[5], (DEPTH, D_MODEL, IN_COLS), D_MODEL ** -0.5),
        "b_gate": nrm(ks[6], (DEPTH, N_BRANCH, D_MODEL), 0.01),
        "ssd_conv_w": nrm(ks[7], (DEPTH, SSD_CONV, SSD_XBC), SSD_CONV ** -0.5),
        "ssd_conv_b": nrm(ks[9], (DEPTH, SSD_XBC), 0.01),
        "ssd_dt_bias": dt0 + jnp.log(-jnp.expm1(-dt0)),
        "ssd_a_log": jnp.log(jax.random.uniform(ks[10], (DEPTH, SSD_HEADS), f32, 1.0, 16.0)),
        "ssd_d": 1.0 + nrm(ks[11], (DEPTH, SSD_HEADS), 0.1),
        "ssd_norm_w": 1.0 + nrm(ks[12], (DEPTH, BRANCH_WIDTH), 0.1),
        "attn_sinks": nrm(ks[13], (DEPTH, ATTN_Q_HEADS), 1.0),
        "sconv_w": nrm(ks[14], (DEPTH, SCONV_WIDTH, BRANCH_WIDTH), SCONV_WIDTH ** -0.5),
        "lru_conv_w": nrm(ks[15], (DEPTH, LRU_CONV, LRU_WIDTH), LRU_CONV ** -0.5),
        "lru_conv_b": nrm(ks[16], (DEPTH, LRU_WIDTH), 0.01),
        "lru_w_a": nrm(ks[17], (DEPTH, LRU_BLOCKS, LRU_BLOCK_DIM, LRU_BLOCK_DIM), LRU_BLOCK_DIM ** -0.5),
        "lru_b_a": nrm(ks[18], (DEPTH, LRU_WIDTH), 0.01),
        "lru_w_x": nrm(ks[20], (DEPTH, LRU_BLOCKS, LRU_BLOCK_DIM, LRU_BLOCK_DIM), LRU_BLOCK_DIM ** -0.5),
        "lru_b_x": nrm(ks[21], (DEPTH, LRU_WIDTH), 0.01),
        "lru_lambda": jnp.log(sig) - jnp.log1p(-sig),
        "w_branch": nrm(ks[22], (DEPTH, N_BRANCH, BRANCH_WIDTH, D_MODEL), DEEPNORM_BETA * BRANCH_WIDTH ** -0.5),
        "w_out": nrm(ks[23], (DEPTH, D_MODEL, D_MODEL), DEEPNORM_BETA * D_MODEL ** -0.5),
        "ln_w": 1.0 + nrm(ks[24], (DEPTH, D_MODEL), 0.1),
        "ln_b": nrm(ks[25], (DEPTH, D_MODEL), 0.01),
    }


def reference(x, c, positions, w_ada, b_ada, w_in, b_gate, ssd_conv_w, ssd_conv_b, ssd_dt_bias,
              ssd_a_log, ssd_d, ssd_norm_w, attn_sinks, sconv_w, lru_conv_w, lru_conv_b, lru_w_a,
              lru_b_a, lru_w_x, lru_b_x, lru_lambda, w_branch, w_out, ln_w, ln_b):
    c_act = jax.nn.silu(c)
    for i in range(DEPTH):
        x = hybrid_layer(
            x, positions, c_act, w_ada[i], b_ada[i], w_in[i], b_gate[i], ssd_conv_w[i], ssd_conv_b[i],
            ssd_dt_bias[i], ssd_a_log[i], ssd_d[i], ssd_norm_w[i], attn_sinks[i], sconv_w[i],
            lru_conv_w[i], lru_conv_b[i], lru_w_a[i], lru_b_a[i], lru_w_x[i], lru_b_x[i], lru_lambda[i],
            w_branch[i], w_out[i], ln_w[i], ln_b[i])
    return x
```

```python
import numpy as np
import concourse.bass as bass
import concourse.mybir as mybir
from concourse.bass_utils import run_bass_kernel_spmd

F32 = mybir.dt.float32
BF16 = mybir.dt.bfloat16
I32 = mybir.dt.int32
U8 = mybir.dt.uint8
AF = mybir.ActivationFunctionType
ALU = mybir.AluOpType
AX = mybir.AxisListType

D = 2048
L = 2048
DEPTH = 2
NKC = 16
BW = 1024
IN_COLS = 19984
A_Z, A_X, A_B, A_C, A_DT = 0, 1024, 2048, 2560, 3072
B_Q, B_K, B_V, B_G = 3088, 4112, 4368, 4624
C_B, C_C, C_X, C_G = 5648, 6672, 7696, 8720
D_X, D_G = 9744, 10768
MERGE = 11792
ALPHA = (2.0 * DEPTH) ** 0.25
LN_EPS = 1e-5
RMS_EPS = 1e-5

CP = {}
_o = 0
for _n, _w in [("ssd_cw", 64), ("ssd_cb", 16), ("ssd_nw", 8), ("ssd_d", 8), ("sconv_w", 24), ("lru_cw", 32),
               ("lru_cb", 8), ("lru_ba", 8), ("lru_bx", 8), ("lru_lam", 8), ("b_gate", 64), ("sinks", 16),
               ("dt_bias", 16), ("a_log", 16)]:
    CP[_n] = (_o, _w)
    _o += _w
NCP = _o
KO = {}
_o = 0
for _n, _w in [("ident", 128), ("triu", 128), ("ones", 128), ("rot", 128), ("invf", 1), ("amask", 256),
               ("smask", 128), ("amask0", 256), ("halfpi", 1)]:
    KO[_n] = (_o, _w)
    _o += _w
NKO = _o


class Res:
    __slots__ = ("name", "w", "r", "excl")

    def __init__(self, name, excl=False):
        self.name = name
        self.w = {}
        self.r = {}
        self.excl = excl


class Sched:
    NDS = 12

    def __init__(self, nc):
        self.nc = nc
        self.eng = {"pe": nc.tensor, "act": nc.scalar, "dve": nc.vector, "pool": nc.gpsimd, "sp": nc.sync}
        self.sem = {k: nc.alloc_semaphore("s_" + k) for k in self.eng}
        self.cnt = {k: 0 for k in self.eng}
        self.seen = {k: {} for k in self.eng}
        self.dsems = {q: [[nc.alloc_semaphore("d_%s_%d" % (q, i)), 0] for i in range(self.NDS)]
                      for q in ("sp", "pool", "act")}
        self.dptr = {q: 0 for q in self.dsems}
        self.all_dma = {}
        self.ninst = 0

    SAME_ENGINE_WAITS = ("pool", "pe", "sp", "act", "dve")

    def _wait(self, e, deps):
        need = {}
        own = self.sem[e].name
        for d in deps:
            for nm, (sem, val) in d.items():
                if nm == own and e not in self.SAME_ENGINE_WAITS:
                    continue
                if self.seen[e].get(nm, 0) < val and need.get(nm, (None, 0))[1] < val:
                    need[nm] = (sem, val)
        for nm, (sem, val) in need.items():
            self.eng[e].wait_ge(sem, val)
            self.seen[e][nm] = val

    def _deps(self, reads, writes, acc):
        deps = []
        for r in reads:
            deps.append(r.w)
            if r.excl:
                deps.append(r.r)
        if not acc:
            for w in writes:
                deps.append(w.w)
                deps.append(w.r)
        return deps

    def _commit(self, reads, writes, sem, val):
        nm = sem.name
        for w in writes:
            w.w = {nm: (sem, val)}
            w.r = {}
        for r in reads:
            r.r[nm] = (sem, val)

    def op(self, e, fn, reads=(), writes=(), acc=False):
        self._wait(e, self._deps(reads, writes, acc))
        ins = fn(self.eng[e])
        self.cnt[e] += 1
        ins.then_inc(self.sem[e], 1)
        self.seen[e][self.sem[e].name] = max(self.seen[e].get(self.sem[e].name, 0), 0)
        self._commit(reads, writes, self.sem[e], self.cnt[e])
        self.ninst += 1

    def dma(self, q, out, in_, reads=(), writes=(), **kw):
        slot = self.dsems[q][self.dptr[q] % self.NDS]
        self.dptr[q] += 1
        sem, uses = slot
        deps = self._deps(reads, writes, False)
        if uses > 0:
            deps.append({sem.name: (sem, 16 * uses)})
        self._wait(q, deps)
        self.eng[q].dma_start(out=out, in_=in_, **kw).then_inc(sem, 16)
        slot[1] = uses + 1
        self._commit(reads, writes, sem, 16 * (uses + 1))
        self.all_dma[sem.name] = (sem, 16 * (uses + 1))
        self.ninst += 1
        return sem, 16 * (uses + 1)

    def barrier(self):
        allt = {}
        for k in self.eng:
            if self.cnt[k] > 0:
                allt[self.sem[k].name] = (self.sem[k], self.cnt[k])
        allt.update(self.all_dma)
        for k in self.eng:
            self._wait(k, [allt])

    def finish(self, e="sp"):
        allt = dict(self.all_dma)
        for k in self.eng:
            if self.cnt[k] > 0:
                allt[self.sem[k].name] = (self.sem[k], self.cnt[k])
        self._wait(e, [allt])


def run_gens(gens):
    live = [[g, n] for g, n in gens]
    while live:
        for item in list(live):
            g, n = item
            for _ in range(n):
                try:
                    next(g)
                except StopIteration:
                    live.remove(item)
                    break


class Arena:
    def __init__(self, nc, nbytes):
        self.t = nc.alloc_sbuf_tensor("arena", [128, nbytes], U8)
        self.ap = self.t.ap()
        self.nbytes = nbytes

    def view(self, off, shape, dtype):
        esz = mybir.dt.size(dtype)
        n = 1
        for s in shape[1:]:
            n *= s
        assert off % 4 == 0 and off + n * esz <= self.nbytes, (off, shape, self.nbytes)
        v = self.ap[:, off:off + n * esz].bitcast(dtype)
        if len(shape) == 3:
            v = v.rearrange("p (a b) -> p a b", b=shape[2])
        elif len(shape) == 4:
            v = v.rearrange("p (a b c) -> p a b c", b=shape[2], c=shape[3])
        return v


ATTN_STOP = 9


def build_program(n_layers=DEPTH, branches=(0, 1, 2, 3), debug=None):
    nc = bass.Bass("TRN2", target_bir_lowering=False)
    dt_in = lambda name, shape, dt=F32: nc.dram_tensor(name, list(shape), dt, kind="ExternalInput").ap()
    x_in = dt_in("x", [L, D])
    c_in = dt_in("c", [128, NKC])
    pos_in = dt_in("pos", [1, L], I32)
    w_ada = dt_in("w_ada", [DEPTH, D, 3 * D])
    b_ada = dt_in("b_ada", [DEPTH, 1, 3 * D])
    w_in = dt_in("w_in", [DEPTH, D, IN_COLS])
    w_br = dt_in("w_branch", [DEPTH, 4, BW, D])
    w_out = dt_in("w_out", [DEPTH, D, D])
    ln_wb = dt_in("ln_wb", [DEPTH, 2, 1, D])
    cp_in = dt_in("cp", [DEPTH, 128, NCP])
    lruw_in = dt_in("lruw", [DEPTH, 2, 8, 128, 128])
    ko_in = dt_in("konst", [128, NKO])
    y_out = nc.dram_tensor("y", [L, D], F32, kind="ExternalOutput").ap()
    x1 = nc.dram_tensor("x1_scr", [L, D], F32).ap()
    gscr = nc.dram_tensor("gate_scr", [128, D], F32).ap()
    mscr = nc.dram_tensor("m_scr", [NKC, 128, L], F32).ap()
    dbg = None
    if debug is not None:
        dbg = nc.dram_tensor("dbg", list(debug[1]), F32, kind="ExternalOutput").ap()

    S = Sched(nc)
    AR = Arena(nc, 206 * 1024)
    psum_t = nc.alloc_psum_tensor("ps", [128, 4096], F32)
    PS = psum_t.ap()
    PB = [PS[:, b * 512:(b + 1) * 512] for b in range(8)]
    PR = [Res("psum%d" % b, excl=True) for b in range(8)]

    o = 0
    def take(n):
        nonlocal o
        r = o
        o += (n + 31) // 32 * 32
        return r
    O_KO = take(NKO * 4)
    O_CP = take(NCP * 4)
    O_SMALL = take(4096)
    O_HT = take(NKC * L * 2)
    O_Y = take(8 * L * 2)
    O_W = take(4 * NKC * 128 * 2)
    O_WB = take(3 * 8 * 128 * 2)
    O_T = take(0)
    T_BYTES = AR.nbytes - O_T
    assert T_BYTES >= 64 * 1024, T_BYTES

    KOt = AR.view(O_KO, [128, NKO], F32)
    CPt = AR.view(O_CP, [128, NCP], F32)
    SM = AR.view(O_SMALL, [128, 1024], F32)
    HT = AR.view(O_HT, [128, NKC, L], BF16)
    YT = AR.view(O_Y, [128, 8, L], BF16)
    WR = [AR.view(O_W + i * NKC * 128 * 2, [128, NKC, 128], BF16) for i in range(4)]
    WRr = [Res("wr%d" % i) for i in range(4)]
    WBR = [AR.view(O_WB + i * 8 * 128 * 2, [128, 8, 128], BF16) for i in range(3)]
    WBRr = [Res("wbr%d" % i) for i in range(3)]
    rKO, rCP, rSM = Res("ko"), Res("cp"), Res("sm")
    rHT = [Res("ht%d" % i) for i in range(4)]
    rMT = [Res("mt%d" % i) for i in range(NKC)]
    rYT = [Res("yt%d" % i) for i in range(8)]
    rT = [Res("t%d" % i) for i in range(8)]
    rX1, rG, rDBG, rOUT = Res("x1"), Res("gscr"), Res("dbg"), Res("out")

    def ko(name, rows=128):
        a, w = KO[name]
        return KOt[0:rows, a:a + w]

    def cp(name, j=None, w=None):
        a, ww = CP[name]
        if j is None:
            return CPt[:, a:a + ww]
        w = w or 1
        return CPt[:, a + j * w:a + (j + 1) * w]

    SH = SM[:, 0:16]
    SC1 = SM[:, 16:32]
    C8 = SM[:, 32:40]
    CACT = SM[:, 40:56]
    SMT = SM[:, 64:256]

    S.dma("sp", KOt, ko_in, writes=[rKO])
    S.dma("sp", CACT, c_in, writes=[rSM])
    S.op("act", lambda e: e.activation(out=CACT, in_=CACT, func=AF.Silu), reads=[rSM], writes=[rSM])

    wr_i = [0]

    def proj_fm(l, col0, ncols, consumer, banks=(0, 1, 2, 3), wload=None):
        i = wr_i[0] % 4
        wr_i[0] += 1
        wt, wres = WR[i], WRr[i]
        if wload is None:
            S.dma("pool", wt[:, :, 0:ncols], w_in[l, :, col0:col0 + ncols].rearrange("(kc p) c -> p kc c", p=128),
                  writes=[wres])
        else:
            wload(wt, wres)
        for tt in range(4):
            b = banks[tt % len(banks)]
            for kc in range(NKC):
                S.op("pe", lambda e, kc=kc, tt=tt, b=b: e.matmul(
                    PB[b][0:ncols, :], lhsT=wt[:, kc, 0:ncols], rhs=HT[:, kc, tt * 512:(tt + 1) * 512],
                    start=(kc == 0), stop=(kc == NKC - 1)),
                    reads=[wres, rHT[tt]], writes=[PR[b]], acc=(kc > 0))
            consumer(tt, PB[b][0:ncols, :], PR[b])

    def softplus_small(dst, src, tmp, n, reads, writes):
        t0, t1, t2 = tmp[:, 0:n], tmp[:, n:2 * n], tmp[:, 2 * n:3 * n]
        rw = dict(reads=reads, writes=writes)
        S.op("dve", lambda e: e.tensor_scalar_mul(out=t1, in0=src, scalar1=-1.0), **rw)
        S.op("dve", lambda e: e.tensor_max(out=t0, in0=src, in1=t1), **rw)
        S.op("act", lambda e: e.activation(out=t0, in_=t0, func=AF.Exp, scale=-1.0), **rw)
        S.op("dve", lambda e: e.tensor_scalar_add(out=t1, in0=t0, scalar1=2.0), **rw)
        S.op("dve", lambda e: e.reciprocal(out=t1, in_=t1), **rw)
        S.op("dve", lambda e: e.tensor_mul(out=t0, in0=t0, in1=t1), **rw)
        S.op("dve", lambda e: e.tensor_mul(out=t1, in0=t0, in1=t0), **rw)
        S.op("dve", lambda e: e.tensor_scalar(out=t2, in0=t1, scalar1=1.0 / 9.0, scalar2=1.0 / 7.0,
                                              op0=ALU.mult, op1=ALU.add), **rw)
        for cst in (1.0 / 5.0, 1.0 / 3.0, 1.0):
            S.op("dve", lambda e: e.tensor_mul(out=t2, in0=t2, in1=t1), **rw)
            S.op("dve", lambda e, cst=cst: e.tensor_scalar_add(out=t2, in0=t2, scalar1=cst), **rw)
        S.op("dve", lambda e: e.tensor_mul(out=t2, in0=t2, in1=t0), **rw)
        S.op("dve", lambda e: e.tensor_scalar_max(out=t0, in0=src, scalar1=0.0), **rw)
        S.op("dve", lambda e: e.scalar_tensor_tensor(out=dst, in0=t2, scalar=2.0, in1=t0,
                                                     op0=ALU.mult, op1=ALU.add), **rw)

    for l in range(n_layers):
        xin = x_in if l == 0 else x1
        xout = y_out if l == n_layers - 1 else x1
        S.barrier()
        S.dma("sp", CPt, cp_in[l], writes=[rCP])

        CB = AR.view(O_T, [128, NKC, 128], F32)
        ADA = AR.view(O_T + 8192, [128, 3 * D], F32)
        WA = [AR.view(O_T + 8192 + 24576 + i * 16384, [128, 8, 512], F32) for i in range(2)]
        rCB = Res("cb")
        rADAb = [Res("ada%d" % nb) for nb in range(12)]
        rWA = [Res("wa0"), Res("wa1")]
        S.op("dve", lambda e: e.tensor_copy(out=CB, in_=CACT.unsqueeze(2).to_broadcast([128, NKC, 128])),
             reads=[rSM], writes=[rCB])
        S.dma("sp", ADA, b_ada[l].partition_broadcast(128), writes=rADAb)
        XT = [AR.view(O_Y + i * 8192, [128, D], F32) for i in range(2)]
        HB = [AR.view(O_Y + 16384 + i * 4096, [128, D], BF16) for i in range(2)]
        IDB1 = AR.view(O_Y + 24576, [128, 128], BF16)
        rXT = [Res("xt0"), Res("xt1")]
        rHB = [Res("hb0"), Res("hb1")]
        rIDB1 = Res("idb1")
        S.op("dve", lambda e: e.tensor_copy(out=IDB1, in_=ko("ident")), reads=[rKO], writes=[rIDB1])
        wi = 0
        for nb in range(12):
            b = nb % 2
            for half in range(2):
                wt, wr = WA[wi % 2], rWA[wi % 2]
                wi += 1
                S.dma("sp" if half == 0 else "act", wt,
                      w_ada[l, half * 1024:(half + 1) * 1024, nb * 512:(nb + 1) * 512].rearrange(
                          "(kc p) c -> p kc c", p=128), writes=[wr])
                for k8 in range(8):
                    kc = half * 8 + k8
                    S.op("pe", lambda e, kc=kc, k8=k8, wt=wt, b=b: e.matmul(
                        PB[b], lhsT=CB[:, kc, :], rhs=wt[:, k8, :], start=(kc == 0), stop=(kc == NKC - 1)),
                        reads=[rCB, wr], writes=[PR[b]], acc=(kc > 0))
            if 4 <= nb < 8:
                S.op("dve", lambda e, nb=nb, b=b: e.scalar_tensor_tensor(
                    out=ADA[:, nb * 512:(nb + 1) * 512], in0=PB[b], scalar=1.0, in1=ADA[:, nb * 512:(nb + 1) * 512],
                    op0=ALU.add, op1=ALU.add), reads=[PR[b], rADAb[nb]], writes=[rADAb[nb]])
            else:
                S.op("dve", lambda e, nb=nb, b=b: e.tensor_add(out=ADA[:, nb * 512:(nb + 1) * 512],
                                                               in0=PB[b], in1=ADA[:, nb * 512:(nb + 1) * 512]),
                     reads=[PR[b], rADAb[nb]], writes=[rADAb[nb]])
        S.dma("sp", gscr, ADA[:, 2 * D:3 * D], reads=rADAb[8:12], writes=[rG])

        SHR = ADA[:, 0:D]
        SCR = ADA[:, D:2 * D]
        for t16 in range(16):
            xt, xr = XT[t16 % 2], rXT[t16 % 2]
            hb, hr = HB[t16 % 2], rHB[t16 % 2]
            S.dma("sp", xt, xin[t16 * 128:(t16 + 1) * 128, :], reads=[rX1] if l > 0 else [], writes=[xr])
            S.op("dve", lambda e, xt=xt: e.tensor_mul(out=xt, in0=xt, in1=SCR), reads=[xr] + rADAb[4:8], writes=[xr])
            S.op("dve", lambda e, xt=xt, hb=hb: e.tensor_add(out=hb, in0=xt, in1=SHR), reads=[xr] + rADAb[0:4], writes=[hr])
            pb0 = 2 * (t16 % 2)
            ptp = PS[:, pb0 * 512:(pb0 + 2) * 512].bitcast(BF16)
            for kc in range(NKC):
                S.op("pe", lambda e, kc=kc, hb=hb, ptp=ptp: e.transpose(out=ptp[:, kc * 128:(kc + 1) * 128],
                                                                       in_=hb[:, kc * 128:(kc + 1) * 128], identity=IDB1),
                     reads=[hr, rIDB1], writes=[PR[pb0 + kc // 8]], acc=(kc % 8 > 0))
            for bk in range(2):
                S.op("act", lambda e, bk=bk, ptp=ptp, t16=t16: e.activation(
                    out=HT[:, 8 * bk:8 * bk + 8, t16 * 128:(t16 + 1) * 128],
                    in_=ptp[:, bk * 1024:(bk + 1) * 1024].rearrange("p (a b) -> p a b", b=128), func=AF.Copy),
                    reads=[PR[pb0 + bk]], writes=[rHT[t16 // 4]], acc=True)
        if debug is not None and debug[0] == "ht" and l == debug[2]:
            DT_ = AR.view(O_T, [128, L], F32)
            for kc in range(NKC):
                S.op("dve", lambda e, kc=kc: e.tensor_copy(out=DT_, in_=HT[:, kc, :]), reads=rHT, writes=[rT[0]])
                S.dma("sp", dbg[kc * 128:(kc + 1) * 128, :], DT_, reads=[rT[0]], writes=[rDBG])
        S.barrier()

        TA = AR.view(O_T, [128, L + 32], F32)
        TB = AR.view(O_T + 8320, [128, L + 32], F32)
        TC = AR.view(O_T + 2 * 8320, [128, L + 32], F32)
        rTA, rTB, rTC = rT[0], rT[1], rT[2]
        first_branch = [True]

        def evac_copy(dst, off, res):
            def f(tt, ps, pres):
                S.op("act", lambda e: e.activation(out=dst[:, off + tt * 512: off + (tt + 1) * 512], in_=ps, func=AF.Copy),
                     reads=[pres], writes=[res], acc=(tt > 0))
            return f

        def evac_act(dst, off, res, func, bias=None):
            def f(tt, ps, pres):
                kw = {} if bias is None else {"bias": bias}
                S.op("act", lambda e: e.activation(out=dst[:, off + tt * 512: off + (tt + 1) * 512], in_=ps, func=func, **kw),
                     reads=[pres, rCP], writes=[res], acc=(tt > 0))
            return f

        def evac_mul(dst, doff, src, soff, res):
            def f(tt, ps, pres):
                S.op("dve", lambda e: e.tensor_mul(out=dst[:, doff + tt * 512: doff + (tt + 1) * 512], in0=ps,
                                                   in1=src[:, soff + tt * 512: soff + (tt + 1) * 512]),
                     reads=[pres, res], writes=[res], acc=(tt > 0))
            return f

        MOFF = O_T + 3 * 8320
        SGB = [AR.view(MOFF + i * 2048, [128, 512], F32) for i in range(3)]
        PVB = [AR.view(MOFF + 3 * 2048 + i * 2048, [128, 512], F32) for i in range(3)]
        rSGB = [Res("sg%d" % i) for i in range(3)]
        rPVB = [Res("pv%d" % i) for i in range(3)]

        def merge_branch(k, rstd_row=None, rstd_res=None):
            first = first_branch[0]
            it = 0
            pending = None
            for dc in range(NKC):
                wi_ = wr_i[0] % 4
                wr_i[0] += 1
                wg, wgr = WR[wi_], WRr[wi_]
                S.dma("pool", wg, w_in[l, :, MERGE + k * D + dc * 128: MERGE + k * D + (dc + 1) * 128].rearrange(
                    "(kc p) c -> p kc c", p=128), writes=[wgr])
                wb, wbr = WBR[dc % 3], WBRr[dc % 3]
                S.dma("pool", wb, w_br[l, k, :, dc * 128:(dc + 1) * 128].rearrange("(c p) d -> p c d", p=128),
                      writes=[wbr])
                for tt in range(4):
                    bg, bb = (it % 2) * 2, (it % 2) * 2 + 1
                    sg, sgr = SGB[it % 3], rSGB[it % 3]
                    pv, pvr = PVB[it % 3], rPVB[it % 3]
                    it += 1
                    if not first:
                        S.dma("sp", pv, mscr[dc, :, tt * 512:(tt + 1) * 512], reads=[rMT[dc]], writes=[pvr])
                    for kc in range(NKC):
                        S.op("pe", lambda e, kc=kc, tt=tt, bg=bg, wg=wg: e.matmul(
                            PB[bg], lhsT=wg[:, kc, :], rhs=HT[:, kc, tt * 512:(tt + 1) * 512],
                            start=(kc == 0), stop=(kc == NKC - 1)), reads=[wgr, rHT[tt]], writes=[PR[bg]], acc=(kc > 0))
                    for c in range(8):
                        S.op("pe", lambda e, c=c, tt=tt, bb=bb, wb=wb: e.matmul(
                            PB[bb], lhsT=wb[:, c, :], rhs=YT[:, c, tt * 512:(tt + 1) * 512],
                            start=(c == 0), stop=(c == 7)), reads=[wbr, rYT[c]], writes=[PR[bb]], acc=(c > 0))
                    bgc = cp("b_gate")[:, k * 16 + dc:k * 16 + dc + 1]
                    S.op("act", lambda e, bg=bg, sg=sg, bgc=bgc: e.activation(out=sg, in_=PB[bg], func=AF.Sigmoid, bias=bgc),
                         reads=[PR[bg], rCP], writes=[sgr])
                    if pending is not None:
                        pending()
                    S.op("dve", lambda e, bb=bb, sg=sg: e.tensor_mul(out=sg, in0=sg, in1=PB[bb]),
                         reads=[PR[bb], sgr], writes=[sgr])
                    if rstd_row is not None:
                        S.op("dve", lambda e, sg=sg, tt=tt: e.tensor_mul(out=sg, in0=sg, in1=rstd_row[:, tt * 512:(tt + 1) * 512]),
                             reads=[sgr, rstd_res], writes=[sgr])
                    if not first:
                        S.op("dve", lambda e, sg=sg, pv=pv: e.tensor_add(out=sg, in0=sg, in1=pv), reads=[sgr, pvr], writes=[sgr])
                    pending = (lambda sg=sg, sgr=sgr, dc=dc, tt=tt: S.dma(
                        "sp", mscr[dc, :, tt * 512:(tt + 1) * 512], sg, reads=[sgr], writes=[rMT[dc]]))
            pending()
            first_branch[0] = False


        if 0 in branches:
            S.barrier()
            o2 = [O_T]
            def tk(n):
                r = o2[0]
                o2[0] += (n + 31) // 32 * 32
                return r
            SSQ = AR.view(tk(8192), [128, L], F32)
            DTT = AR.view(tk(1024), [128, 16, 16], F32)
            DA = AR.view(tk(1024), [128, 16, 16], F32)
            CUMC = AR.view(tk(1024), [128, 16, 16], F32)
            CDEC = AR.view(tk(1024), [128, 16, 16], F32)
            DTDE = AR.view(tk(1024), [128, 16, 16], F32)
            SPT = AR.view(tk(3072), [128, 768], F32)
            XS = [AR.view(tk(8192), [128, L], F32) for _ in range(2)]
            CV = AR.view(tk(8320), [128, L + 32], F32)
            BT = AR.view(tk(4096), [128, L], BF16)
            CT = AR.view(tk(4096), [128, L], BF16)
            ZS = [AR.view(tk(4096), [128, L], BF16) for _ in range(2)]
            XDTc = [AR.view(tk(512), [128, 4, 64], BF16) for _ in range(2)]
            XDEc = [AR.view(tk(512), [128, 4, 64], BF16) for _ in range(2)]
            Bc = [AR.view(tk(256), [128, 128], BF16) for _ in range(2)]
            RR = AR.view(tk(2048), [128, 4, 128], F32)
            SEG = AR.view(tk(2048), [128, 4, 128], F32)
            MT2 = [AR.view(tk(1024), [128, 4, 128], BF16) for _ in range(2)]
            EC = AR.view(tk(2048), [128, 4, 128], F32)
            CE2 = [AR.view(tk(1024), [128, 4, 128], BF16) for _ in range(2)]
            rMT2 = [Res("mtc0"), Res("mtc1")]
            rCE2 = [Res("cec0"), Res("cec1")]
            STATE = AR.view(tk(1024), [128, 4, 64], F32)
            STATEB = AR.view(tk(512), [128, 256], BF16)
            IDB = AR.view(tk(256), [128, 128], BF16)
            assert o2[0] <= AR.nbytes, o2[0]
            (rSSQ, rDT, rXS0, rXS1, rCV, rBT, rCT, rZS0, rZS1, rRR, rSEG, rMTc, rEC, rCEc, rST, rSTB, rIDB, rSPT) = (
                Res(n) for n in ("ssq", "dt", "xs0", "xs1", "cv", "bt", "ct", "zs0", "zs1", "rr", "seg", "mtc", "ec", "cec",
                                 "st", "stb", "idb", "spt"))
            rXS = [rXS0, rXS1]
            rZS = [rZS0, rZS1]
            rXDT = [Res("xdt0"), Res("xdt1")]
            rXDE = [Res("xde0"), Res("xde1")]
            rBc = [Res("bc0"), Res("bc1")]
            S.op("dve", lambda e: e.tensor_copy(out=IDB, in_=ko("ident")), reads=[rKO], writes=[rIDB])
            S.op("dve", lambda e: e.memset(SSQ, 0.0), writes=[rSSQ])
            S.op("dve", lambda e: e.memset(CV[:, 0:4], 0.0), writes=[rCV])
            wi_ = wr_i[0] % 4
            wr_i[0] += 1
            wdt, wdtr = WR[wi_], WRr[wi_]
            S.dma("pool", wdt[:, :, 0:16], w_in[l, :, A_DT:A_DT + 16].rearrange("(kc p) c -> p kc c", p=128), writes=[wdtr])
            for c in range(16):
                for kc in range(NKC):
                    S.op("pe", lambda e, c=c, kc=kc: e.matmul(PB[0][:, c * 16:(c + 1) * 16], lhsT=HT[:, kc, c * 128:(c + 1) * 128],
                                                             rhs=wdt[:, kc, 0:16], start=(kc == 0), stop=(kc == NKC - 1)),
                         reads=[wdtr, rHT[c // 4]], writes=[PR[0]], acc=(c > 0 or kc > 0))
            p0v = PB[0][:, 0:256].rearrange("p (c h) -> p c h", h=16)
            S.op("dve", lambda e: e.tensor_add(out=DTT, in0=p0v, in1=cp("dt_bias").unsqueeze(1).to_broadcast([128, 16, 16])),
                 reads=[PR[0], rCP], writes=[rDT])
            DTTf = DTT.rearrange("p c h -> p (c h)")
            DAf = DA.rearrange("p c h -> p (c h)")
            CUMCf = CUMC.rearrange("p c h -> p (c h)")
            CDECf = CDEC.rearrange("p c h -> p (c h)")
            DTDEf = DTDE.rearrange("p c h -> p (c h)")
            softplus_small(DTTf, DTTf, SPT, 256, [rDT, rSPT], [rDT, rSPT])
            S.op("act", lambda e: e.activation(out=SMT[:, 32:48], in_=cp("a_log"), func=AF.Exp), reads=[rCP], writes=[rSM])
            S.op("dve", lambda e: e.scalar_tensor_tensor(out=DA, in0=DTT, scalar=-1.0,
                                                         in1=SMT[:, 32:48].unsqueeze(1).to_broadcast([128, 16, 16]),
                                                         op0=ALU.mult, op1=ALU.mult), reads=[rDT, rSM], writes=[rDT])
            S.op("pe", lambda e: e.matmul(PB[1][:, 0:256], lhsT=ko("triu"), rhs=DAf, start=True, stop=True),
                 reads=[rKO, rDT], writes=[PR[1]])
            S.op("pe", lambda e: e.matmul(PB[2][:, 0:256], lhsT=ko("ones"), rhs=DAf, start=True, stop=True),
                 reads=[rKO, rDT], writes=[PR[2]])
            S.op("dve", lambda e: e.tensor_copy(out=CUMCf, in_=PB[1][:, 0:256]), reads=[PR[1]], writes=[rDT])
            S.op("act", lambda e: e.activation(out=CDECf, in_=PB[2][:, 0:256], func=AF.Exp), reads=[PR[2]], writes=[rDT])
            S.op("dve", lambda e: e.tensor_sub(out=DTDEf, in0=PB[2][:, 0:256], in1=CUMCf), reads=[PR[2], rDT], writes=[rDT])
            S.op("act", lambda e: e.activation(out=DTDEf, in_=DTDEf, func=AF.Exp), reads=[rDT], writes=[rDT])
            S.op("dve", lambda e: e.tensor_mul(out=DTDEf, in0=DTDEf, in1=DTTf), reads=[rDT], writes=[rDT])

            def conv_silu(chunk16, dst, dres, out_bf16):
                cw = cp("ssd_cw")
                acc_t = dst if not out_bf16 else SEGL
                acc_r = dres if not out_bf16 else rSEGL
                S.op("dve", lambda e: e.tensor_scalar(out=acc_t, in0=CV[:, 0:L], scalar1=cw[:, chunk16 * 4:chunk16 * 4 + 1],
                                                      scalar2=cp("ssd_cb")[:, chunk16:chunk16 + 1], op0=ALU.mult, op1=ALU.add),
                     reads=[rCV, rCP], writes=[acc_r])
                for kk in (1, 2, 3):
                    S.op("dve", lambda e, kk=kk: e.scalar_tensor_tensor(
                        out=acc_t, in0=CV[:, kk:kk + L], scalar=cw[:, chunk16 * 4 + kk:chunk16 * 4 + kk + 1], in1=acc_t,
                        op0=ALU.mult, op1=ALU.add), reads=[rCV, acc_r, rCP], writes=[acc_r])
                S.op("act", lambda e: e.activation(out=dst, in_=acc_t, func=AF.Silu), reads=[acc_r], writes=[dres, acc_r])

            SEGL = AR.view(O_T + 8192 + 5 * 1024 + 3072 + 2 * 8192 + 8320 + 16384 + 1536, [128, L], F32) if False else None
            rSEGL = None
            for g in range(4):
                SEGL, rSEGL = XS[1], rXS[1]
                proj_fm(l, A_B + g * 128, 128, evac_copy(CV, 3, rCV))
                conv_silu(8 + g, BT, rBT, True)
                proj_fm(l, A_C + g * 128, 128, evac_copy(CV, 3, rCV))
                conv_silu(12 + g, CT, rCT, True)
                for i in range(2):
                    proj_fm(l, A_X + (2 * g + i) * 128, 128, evac_copy(CV, 3, rCV))
                    conv_silu(2 * g + i, XS[i], rXS[i], False)
                    proj_fm(l, A_Z + (2 * g + i) * 128, 128, evac_act(ZS[i], 0, rZS[i], AF.Silu))
                S.op("dve", lambda e: e.memset(STATE, 0.0), writes=[rST])
                hs4 = slice(4 * g, 4 * g + 4)

                def ssd_P(c):
                    tok = slice(c * 128, (c + 1) * 128)
                    xd, rxd = XDTc[c % 2], rXDT[c % 2]
                    xe, rxe = XDEc[c % 2], rXDE[c % 2]
                    bc, rbc = Bc[c % 2], rBc[c % 2]
                    mt, rmt = MT2[c % 2], rMT2[c % 2]
                    ce, rce = CE2[c % 2], rCE2[c % 2]
                    for i in range(2):
                        b = 4 + i
                        S.op("pe", lambda e, i=i, b=b: e.transpose(out=PB[b][:, 0:128], in_=XS[i][:, tok], identity=ko("ident")),
                             reads=[rXS[i], rKO], writes=[PR[b]])
                        yield
                        pv = PB[b][:, 0:128].rearrange("p (h d) -> p h d", d=64)
                        hs = slice(4 * g + 2 * i, 4 * g + 2 * i + 2)
                        S.op("dve", lambda e, i=i, pv=pv, hs=hs: e.tensor_mul(
                            out=xd[:, 2 * i:2 * i + 2, :], in0=pv, in1=DTT[:, c, hs].unsqueeze(2).to_broadcast([128, 2, 64])),
                            reads=[PR[b], rDT], writes=[rxd], acc=(i > 0))
                        S.op("dve", lambda e, i=i, pv=pv, hs=hs: e.tensor_mul(
                            out=xe[:, 2 * i:2 * i + 2, :], in0=pv, in1=DTDE[:, c, hs].unsqueeze(2).to_broadcast([128, 2, 64])),
                            reads=[PR[b], rDT], writes=[rxe], acc=(i > 0))
                        yield
                    p6 = PB[6].bitcast(BF16)
                    S.op("pe", lambda e: e.transpose(out=p6[:, 0:128], in_=BT[:, tok], identity=IDB),
                         reads=[rBT, rIDB], writes=[PR[6]])
                    S.op("act", lambda e: e.activation(out=bc, in_=p6[:, 0:128], func=AF.Copy), reads=[PR[6]], writes=[rbc])
                    yield
                    S.op("pe", lambda e: e.matmul(PB[7][:, 0:128], lhsT=BT[:, tok], rhs=CT[:, tok], start=True, stop=True),
                         reads=[rBT, rCT], writes=[PR[7]])
                    S.op("dve", lambda e: e.tensor_mul(
                        out=RR, in0=DA[:, c, hs4].unsqueeze(2).to_broadcast([128, 4, 128]),
                        in1=ko("triu").unsqueeze(1).to_broadcast([128, 4, 128])), reads=[rDT, rKO], writes=[rRR])
                    yield
                    S.op("pe", lambda e: e.matmul(PB[3], lhsT=ko("ones"), rhs=RR.rearrange("p h l -> p (h l)"), start=True, stop=True),
                         reads=[rKO, rRR], writes=[PR[3]])
                    p3v = PB[3].rearrange("p (h l) -> p h l", l=128)
                    yield
                    S.op("dve", lambda e: e.tensor_sub(
                        out=SEG, in0=p3v, in1=CUMC[:, c, hs4].unsqueeze(2).to_broadcast([128, 4, 128])),
                        reads=[PR[3], rDT], writes=[rSEG])
                    if c > 0:
                        S.op("act", lambda e: e.activation(out=EC, in_=p3v, func=AF.Exp), reads=[PR[3], rSEG], writes=[rEC])
                    yield
                    S.op("dve", lambda e: e.tensor_add(out=SEG, in0=SEG, in1=ko("smask").unsqueeze(1).to_broadcast([128, 4, 128])),
                         reads=[rSEG, rKO], writes=[rSEG])
                    yield
                    S.op("act", lambda e: e.activation(out=SEG, in_=SEG, func=AF.Exp), reads=[rSEG], writes=[rSEG])
                    if c > 0:
                        S.op("dve", lambda e: e.tensor_mul(out=ce, in0=EC, in1=CT[:, tok].unsqueeze(1).to_broadcast([128, 4, 128])),
                             reads=[rEC, rCT], writes=[rce])
                    yield
                    S.op("dve", lambda e: e.tensor_mul(out=mt, in0=SEG, in1=PB[7][:, 0:128].unsqueeze(1).to_broadcast([128, 4, 128])),
                         reads=[rSEG, PR[7]], writes=[rmt])
                    yield

                def ssd_Q(c):
                    tok = slice(c * 128, (c + 1) * 128)
                    xd, rxd = XDTc[c % 2], rXDT[c % 2]
                    xe, rxe = XDEc[c % 2], rXDE[c % 2]
                    bc, rbc = Bc[c % 2], rBc[c % 2]
                    mt, rmt = MT2[c % 2], rMT2[c % 2]
                    ce, rce = CE2[c % 2], rCE2[c % 2]
                    for i in range(2):
                        b = 0 + i
                        for h2 in range(2):
                            h = 2 * i + h2
                            pr = slice(64 * h2, 64 * h2 + 64)
                            S.op("pe", lambda e, b=b, pr=pr, h=h: e.matmul(
                                PB[b][pr, 0:128], lhsT=xd[:, h, :], rhs=mt[:, h, :], start=True, stop=(c == 0)),
                                reads=[rxd, rmt], writes=[PR[b]], acc=(h2 > 0))
                            if c > 0:
                                S.op("pe", lambda e, b=b, pr=pr, h=h: e.matmul(
                                    PB[b][pr, 0:128], lhsT=STATEB[:, h * 64:(h + 1) * 64], rhs=ce[:, h, :], start=False, stop=True),
                                    reads=[rSTB, rce], writes=[PR[b]], acc=True)
                        yield
                        j = 2 * g + i
                        S.op("dve", lambda e, i=i, b=b, j=j: e.scalar_tensor_tensor(
                            out=XS[i][:, tok], in0=XS[i][:, tok], scalar=cp("ssd_d")[:, j:j + 1], in1=PB[b][:, 0:128],
                            op0=ALU.mult, op1=ALU.add), reads=[PR[b], rXS[i], rCP], writes=[rXS[i]])
                        yield
                        S.op("dve", lambda e, i=i: e.tensor_mul(out=XS[i][:, tok], in0=XS[i][:, tok], in1=ZS[i][:, tok]),
                             reads=[rXS[i], rZS[i]], writes=[rXS[i]])
                        yield
                    if c < 15:
                        S.op("pe", lambda e: e.matmul(PB[2][:, 0:256], lhsT=bc, rhs=xe.rearrange("p h d -> p (h d)"),
                                                      start=True, stop=True), reads=[rbc, rxe], writes=[PR[2]])
                        S.op("dve", lambda e: e.tensor_mul(
                            out=STATE, in0=STATE, in1=CDEC[:, c, hs4].unsqueeze(2).to_broadcast([128, 4, 64])),
                            reads=[rST, rDT], writes=[rST])
                        yield
                        S.op("dve", lambda e: e.tensor_add(out=STATE, in0=STATE, in1=PB[2][:, 0:256].rearrange("p (h d) -> p h d", d=64)),
                             reads=[rST, PR[2]], writes=[rST])
                        yield
                        S.op("act", lambda e: e.activation(out=STATEB, in_=STATE.rearrange("p h d -> p (h d)"), func=AF.Copy),
                             reads=[rST], writes=[rSTB])
                        yield

                run_gens([(ssd_P(0), 1)])
                for c in range(16):
                    gens = [(ssd_Q(c), 1)]
                    if c + 1 < 16:
                        gens.insert(0, (ssd_P(c + 1), 1))
                    run_gens(gens)
                for i in range(2):
                    j = 2 * g + i
                    S.op("act", lambda e, i=i: e.activation(out=CV[:, 4:4 + L], in_=XS[i], func=AF.Square), reads=[rXS[i], rCV], writes=[rCV])
                    for tt in range(4):
                        b = 4 + tt
                        S.op("pe", lambda e, tt=tt, b=b: e.matmul(PB[b], lhsT=ko("ones"), rhs=CV[:, 4 + tt * 512:4 + (tt + 1) * 512],
                                                                 start=True, stop=True), reads=[rKO, rCV], writes=[PR[b]])
                        S.op("dve", lambda e, tt=tt, b=b: e.tensor_add(out=SSQ[:, tt * 512:(tt + 1) * 512], in0=SSQ[:, tt * 512:(tt + 1) * 512],
                                                                      in1=PB[b]), reads=[PR[b], rSSQ], writes=[rSSQ])
                    S.op("dve", lambda e, i=i, j=j: e.tensor_scalar_mul(out=YT[:, j, :], in0=XS[i], scalar1=cp("ssd_nw")[:, j:j + 1]),
                         reads=[rXS[i], rCP], writes=[rYT[j]])
                S.op("dve", lambda e: e.memset(CV[:, 0:4], 0.0), reads=[rCV], writes=[rCV])
            S.op("dve", lambda e: e.tensor_scalar(out=SSQ, in0=SSQ, scalar1=1.0 / BW, scalar2=RMS_EPS, op0=ALU.mult, op1=ALU.add),
                 reads=[rSSQ], writes=[rSSQ])
            S.op("act", lambda e: e.activation(out=SSQ, in_=SSQ, func=AF.Sqrt), reads=[rSSQ], writes=[rSSQ])
            S.op("dve", lambda e: e.reciprocal(out=SSQ, in_=SSQ), reads=[rSSQ], writes=[rSSQ])
            if debug is not None and debug[0] == "ya" and l == debug[2]:
                S.barrier()
                DT_ = AR.view(O_T + 8192, [128, L], F32)
                for j in range(8):
                    S.op("dve", lambda e, j=j: e.tensor_mul(out=DT_, in0=YT[:, j, :], in1=SSQ), reads=[rYT[j], rSSQ], writes=[rT[0]])
                    S.dma("sp", dbg[j * 128:(j + 1) * 128, :], DT_, reads=[rT[0]], writes=[rDBG])
            S.barrier()
            merge_branch(0, rstd_row=SSQ, rstd_res=rSSQ)
            S.barrier()

        if 1 in branches:
            S.barrier()
            LP = L + 128
            COS = AR.view(O_T, [128, L], F32)
            SIN = AR.view(O_T + 8192, [128, L], F32)
            KT2 = AR.view(O_T + 16384, [128, 4, LP], BF16)
            VT = AR.view(O_T + 33792, [128, 17, 256], BF16)
            QF = AR.view(O_T + 42496, [128, L], F32)
            QR = AR.view(O_T + 50688, [128, L], F32)
            QI = AR.view(O_T + 50688, [128, L], I32)
            SMB = AR.view(O_T + 50688, [128, 8, 256], F32)
            WV = AR.view(O_T + 50688, [128, NKC, 256], BF16)
            QT = AR.view(O_T + 58880, [128, L], BF16)
            GS = AR.view(O_T + 62976, [128, L], BF16)
            PBF = AR.view(O_T + 67072, [128, 8, 256], BF16)
            PTS = AR.view(O_T + 71168, [128, 2048], BF16)
            IDB = AR.view(O_T + 75264, [128, 128], BF16)
            assert O_T + 75264 + 256 <= AR.nbytes
            rCOS, rSIN, rKT2, rVT, rQF, rQR, rQT, rGS, rIDB, rPBF, rPTS, rAT, rPBFb = (Res(n) for n in
                ("cos", "sin", "kt2", "vt", "qf", "qr", "qt", "gs", "idb", "pbf", "pts", "at", "pbfb"))
            AT = SM[:, 320:384]
            MX = AT[:, 0:8]
            RS8 = AT[:, 8:16]
            ES = AT[:, 16:24]
            NMX = AT[:, 24:32]
            S.op("dve", lambda e: e.tensor_copy(out=IDB, in_=ko("ident")), reads=[rKO], writes=[rIDB])
            for kh in range(4):
                S.op("dve", lambda e, kh=kh: e.memset(KT2[:, kh, 0:128], 0.0), writes=[rKT2], acc=(kh > 0))
            S.op("dve", lambda e: e.memset(VT[:, 0, :], 0.0), writes=[rVT])
            S.dma("sp", QI, pos_in.partition_broadcast(128), writes=[rQR])
            S.op("dve", lambda e: e.tensor_copy(out=QF, in_=QI), reads=[rQR], writes=[rQF])
            S.op("dve", lambda e: e.tensor_scalar_mul(out=QF, in0=QF, scalar1=ko("invf")), reads=[rQF, rKO], writes=[rQF])
            S.op("dve", lambda e: e.tensor_scalar_mul(out=COS, in0=QF, scalar1=float(1.0 / (2 * np.pi))), reads=[rQF], writes=[rCOS])
            S.op("dve", lambda e: e.tensor_copy(out=QI, in_=COS), reads=[rCOS], writes=[rQR])
            S.op("dve", lambda e: e.tensor_copy(out=COS, in_=QI), reads=[rQR], writes=[rCOS])
            S.op("dve", lambda e: e.scalar_tensor_tensor(out=SIN, in0=COS, scalar=-6.28125, in1=QF, op0=ALU.mult, op1=ALU.add),
                 reads=[rCOS, rQF], writes=[rSIN])
            S.op("dve", lambda e: e.scalar_tensor_tensor(out=SIN, in0=COS, scalar=-0.0019353071795864769, in1=SIN,
                                                         op0=ALU.mult, op1=ALU.add), reads=[rCOS, rSIN], writes=[rSIN])
            S.op("dve", lambda e: e.tensor_scalar(out=SIN, in0=SIN, scalar1=-3.141592, scalar2=3.141592, op0=ALU.max, op1=ALU.min),
                 reads=[rSIN], writes=[rSIN])
            S.op("dve", lambda e: e.tensor_scalar_mul(out=QF, in0=SIN, scalar1=-1.0), reads=[rSIN], writes=[rQF])
            S.op("dve", lambda e: e.tensor_max(out=QF, in0=QF, in1=SIN), reads=[rSIN, rQF], writes=[rQF])
            S.op("act", lambda e: e.activation(out=COS, in_=QF, func=AF.Sin, scale=-1.0, bias=ko("halfpi")),
                 reads=[rQF, rKO], writes=[rCOS])
            S.op("act", lambda e: e.activation(out=SIN, in_=SIN, func=AF.Sin), reads=[rSIN], writes=[rSIN])

            def rope_chunk(dst, dres, qscale):
                for tt in range(4):
                    b = 4 + tt
                    S.op("pe", lambda e, tt=tt, b=b: e.matmul(PB[b], lhsT=ko("rot"), rhs=QF[:, tt * 512:(tt + 1) * 512],
                                                             start=True, stop=True), reads=[rKO, rQF], writes=[PR[b]])
                    S.op("dve", lambda e, tt=tt, b=b: e.scalar_tensor_tensor(
                        out=QR[:, tt * 512:(tt + 1) * 512], in0=PB[b], scalar=qscale, in1=SIN[:, tt * 512:(tt + 1) * 512],
                        op0=ALU.mult, op1=ALU.mult), reads=[PR[b], rSIN], writes=[rQR], acc=(tt > 0))
                S.op("dve", lambda e: e.scalar_tensor_tensor(out=QF, in0=QF, scalar=qscale, in1=COS, op0=ALU.mult, op1=ALU.mult),
                     reads=[rQF, rCOS], writes=[rQF])
                S.op("dve", lambda e: e.tensor_add(out=dst, in0=QF, in1=QR), reads=[rQF, rQR], writes=[dres])

            S.dma("pool", WV, w_in[l, :, B_V:B_V + 256].rearrange("(kc p) c -> p kc c", p=128), reads=[rQR], writes=[rQR])
            for n in range(16):
                b = n % 4
                for kc in range(NKC):
                    S.op("pe", lambda e, kc=kc, n=n, b=b: e.matmul(PB[b][:, 0:256], lhsT=HT[:, kc, n * 128:(n + 1) * 128],
                                                                  rhs=WV[:, kc, :], start=(kc == 0), stop=(kc == NKC - 1)),
                         reads=[rQR, rHT[n // 4]], writes=[PR[b]], acc=(kc > 0))
                S.op("act", lambda e, n=n, b=b: e.activation(out=VT[:, n + 1, :], in_=PB[b][:, 0:256], func=AF.Copy),
                     reads=[PR[b]], writes=[rVT], acc=True)
            for kh in range(4):
                def wl(wt, wres, kh=kh):
                    src = w_in[l, :, B_K + kh * 64:B_K + (kh + 1) * 64].rearrange("(kc p) c -> p kc c", p=128)
                    S.dma("pool", wt[:, :, 0:64], src, writes=[wres])
                    keep_w = dict(wres.w)
                    sem2, val2 = S.dma("pool", wt[:, :, 64:128], src, reads=[wres], writes=[])
                    wres.r = {}
                    wres.w = keep_w
                    wres.w[sem2.name] = (sem2, val2)
                proj_fm(l, 0, 128, evac_copy(QF, 0, rQF), wload=wl)
                rope_chunk(KT2[:, kh, 128:LP], rKT2, 1.0)
            PSG = PS[:, 0:2048].rearrange("p (i s) -> p i s", s=256)
            PTP = PS[:, 2048:3072].bitcast(BF16)
            rPSG = Res("psg")
            rPTP = Res("ptp")

            def s_matmuls(jq, ng):
                kh = jq // 2
                for nbi in range(4):
                    n = ng * 4 + nbi
                    for h2 in range(2):
                        i = h2 * 4 + nbi
                        pr = slice(64 * h2, 64 * h2 + 64)
                        if ATTN_STOP == 2.6:
                            S.op("pe", lambda e, pr=pr, n=n, i=i, kh=kh: e.matmul(
                                PB[i % 4][:, 0:256], lhsT=QT[pr, n * 128:(n + 1) * 128], rhs=KT2[pr, kh, n * 128:n * 128 + 256],
                                start=True, stop=True), reads=[rQT, rKT2], writes=[PR[i % 4]])
                            continue
                        S.op("pe", lambda e, pr=pr, n=n, i=i, kh=kh: e.matmul(
                            PSG[:, i, :], lhsT=QT[pr, n * 128:(n + 1) * 128], rhs=KT2[pr, kh, n * 128:n * 128 + 256],
                            start=True, stop=True), reads=[rQT, rKT2], writes=[PR[0], PR[1], PR[2], PR[3]], acc=(nbi > 0 or h2 > 0))

            for jq in range(8 if ATTN_STOP >= 2 else 0):
                kh = jq // 2
                proj_fm(l, B_Q + jq * 128, 128, evac_copy(QF, 0, rQF))
                rope_chunk(QT, rQT, 0.125)
                proj_fm(l, B_G + jq * 128, 128, evac_act(GS, 0, rGS, AF.Silu))
                sinkb = cp("sinks")[:, 2 * jq:2 * jq + 2].unsqueeze(2).to_broadcast([128, 2, 4])
                v42 = lambda t: t.rearrange("p (a b) -> p a b", b=4)
                PBF2 = [PBF, AR.view(O_T + 75520, [128, 8, 256], BF16)]
                rPBF2 = [rPBF, rPBFb]
                assert O_T + 75520 + 4096 <= AR.nbytes

                def att_H1(ng, jq=jq, sinkb=sinkb, v42=v42):
                    pbf, rpbf = PBF2[ng % 2], rPBF2[ng % 2]
                    for bk in range(4):
                        S.op("dve", lambda e, bk=bk: e.tensor_add(out=SMB[:, 2 * bk:2 * bk + 2, :], in0=PSG[:, 2 * bk:2 * bk + 2, :],
                                                                 in1=ko("amask").unsqueeze(1).to_broadcast([128, 2, 256])),
                             reads=[PR[bk], rKO], writes=[rQR], acc=(bk > 0))
                        if bk % 2 == 1:
                            yield
                    if ng < 3:
                        s_matmuls(jq, ng + 1)
                    yield
                    if ng == 0:
                        for i0 in (0, 4):
                            S.op("dve", lambda e, i0=i0: e.tensor_scalar_add(out=SMB[:, i0, 0:128], in0=SMB[:, i0, 0:128], scalar1=-30000.0),
                                 reads=[rQR], writes=[rQR])
                    S.op("dve", lambda e: e.reduce_max(out=MX, in_=SMB, axis=AX.X), reads=[rQR], writes=[rAT])
                    yield
                    S.op("dve", lambda e: e.tensor_max(out=v42(MX), in0=v42(MX), in1=sinkb), reads=[rAT, rCP], writes=[rAT])
                    S.op("dve", lambda e: e.tensor_scalar_mul(out=NMX, in0=MX, scalar1=-1.0), reads=[rAT], writes=[rAT])
                    yield
                    for i in range(8):
                        S.op("act", lambda e, i=i: e.activation(out=SMB[:, i, :], in_=SMB[:, i, :], func=AF.Exp, bias=NMX[:, i:i + 1],
                                                               accum_out=RS8[:, i:i + 1]), reads=[rQR, rAT], writes=[rQR, rAT], acc=(i > 0))
                        if i % 2 == 1:
                            yield
                    S.op("dve", lambda e: e.tensor_sub(out=v42(ES), in0=sinkb, in1=v42(MX)), reads=[rAT, rCP], writes=[rAT])
                    S.op("act", lambda e: e.activation(out=ES, in_=ES, func=AF.Exp), reads=[rAT], writes=[rAT])
                    yield
                    S.op("dve", lambda e: e.tensor_add(out=RS8, in0=RS8, in1=ES), reads=[rAT], writes=[rAT])
                    S.op("dve", lambda e: e.reciprocal(out=RS8, in_=RS8), reads=[rAT], writes=[rAT])
                    yield
                    S.op("dve", lambda e: e.tensor_mul(out=pbf, in0=SMB, in1=RS8.unsqueeze(2).to_broadcast([128, 8, 256])),
                         reads=[rQR, rAT], writes=[rpbf])
                    yield

                def att_H2(ng, jq=jq, kh=kh):
                    pbf, rpbf = PBF2[ng % 2], rPBF2[ng % 2]
                    bo = 6 + (ng % 2)
                    for i in range(8):
                        for blk in range(2):
                            S.op("pe", lambda e, i=i, blk=blk: e.transpose(
                                out=PTP[:, (i * 2 + blk) * 128:(i * 2 + blk + 1) * 128], in_=pbf[:, i, blk * 128:(blk + 1) * 128],
                                identity=IDB), reads=[rpbf, rIDB], writes=[PR[4], PR[5]], acc=(i > 0 or blk > 0))
                        if i % 2 == 1:
                            yield
                    for bk in range(2):
                        S.op("act", lambda e, bk=bk: e.activation(out=PTS[:, bk * 1024:(bk + 1) * 1024], in_=PTP[:, bk * 1024:(bk + 1) * 1024],
                                                                 func=AF.Copy), reads=[PR[4 + bk]], writes=[rPTS], acc=(bk > 0))
                    yield
                    for nbi in range(4):
                        n = ng * 4 + nbi
                        for h2 in range(2):
                            i = h2 * 4 + nbi
                            pr = slice(64 * h2, 64 * h2 + 64)
                            for blk in range(2):
                                S.op("pe", lambda e, i=i, blk=blk, pr=pr, n=n, nbi=nbi: e.matmul(
                                    PB[bo][pr, nbi * 128:(nbi + 1) * 128], lhsT=VT[:, n + blk, kh * 64:(kh + 1) * 64],
                                    rhs=PTS[:, (i * 2 + blk) * 128:(i * 2 + blk + 1) * 128], start=(blk == 0), stop=(blk == 1)),
                                    reads=[rVT, rPTS], writes=[PR[bo]], acc=(nbi > 0 or h2 > 0 or blk > 0))
                        yield
                    S.op("dve", lambda e: e.tensor_mul(out=YT[:, jq, ng * 512:(ng + 1) * 512], in0=PB[bo],
                                                       in1=GS[:, ng * 512:(ng + 1) * 512]),
                         reads=[PR[bo], rGS], writes=[rYT[jq]], acc=True)
                    yield

                s_matmuls(jq, 0)
                run_gens([(att_H1(0), 1)])
                for ng in range(4):
                    gens = [(att_H2(ng), 1)]
                    if ng < 3:
                        gens.insert(0, (att_H1(ng + 1), 1))
                    run_gens(gens)
            if debug is not None and debug[0] == "yb" and l == debug[2]:
                S.barrier()
                DT_ = AR.view(O_T, [128, L], F32)
                for j in range(8):
                    S.op("dve", lambda e, j=j: e.tensor_copy(out=DT_, in_=YT[:, j, :]), reads=[rYT[j]], writes=[rT[0]])
                    S.dma("sp", dbg[j * 128:(j + 1) * 128, :], DT_, reads=[rT[0]], writes=[rDBG])
            S.barrier()
            merge_branch(1)
            S.barrier()

        if 2 in branches:
            S.op("dve", lambda e: e.memset(TB[:, 0:4], 0.0), writes=[rTB])
            for j in range(8):
                proj_fm(l, C_C + j * 128, 128, evac_copy(TA, 0, rTA))
                proj_fm(l, C_X + j * 128, 128, evac_mul(TB, 2, TA, 0, rTB) if False else
                        (lambda tt, ps, pres: S.op("dve", lambda e: e.tensor_mul(
                            out=TB[:, 2 + tt * 512: 2 + (tt + 1) * 512], in0=ps, in1=TA[:, tt * 512:(tt + 1) * 512]),
                            reads=[pres, rTA], writes=[rTB], acc=(tt > 0))))
                wv = cp("sconv_w")
                S.op("dve", lambda e, j=j: e.tensor_scalar_mul(out=TA[:, 0:L], in0=TB[:, 0:L], scalar1=wv[:, j * 3:j * 3 + 1]),
                     reads=[rTB, rCP], writes=[rTA])
                for kk in (1, 2):
                    S.op("dve", lambda e, j=j, kk=kk: e.scalar_tensor_tensor(
                        out=TA[:, 0:L], in0=TB[:, kk:kk + L], scalar=wv[:, j * 3 + kk:j * 3 + kk + 1], in1=TA[:, 0:L],
                        op0=ALU.mult, op1=ALU.add), reads=[rTB, rTA, rCP], writes=[rTA])
                proj_fm(l, C_B + j * 128, 128, lambda tt, ps, pres: S.op("dve", lambda e: e.tensor_mul(
                    out=TA[:, tt * 512:(tt + 1) * 512], in0=ps, in1=TA[:, tt * 512:(tt + 1) * 512]),
                    reads=[pres, rTA], writes=[rTA], acc=(tt > 0)))
                proj_fm(l, C_G + j * 128, 128, evac_act(TC, 0, rTC, AF.Silu))
                S.op("dve", lambda e, j=j: e.tensor_mul(out=YT[:, j, :], in0=TA[:, 0:L], in1=TC[:, 0:L]),
                     reads=[rTA, rTC], writes=[rYT[j]])
            if debug is not None and debug[0] == "yc" and l == debug[2]:
                for j in range(8):
                    S.op("dve", lambda e, j=j: e.tensor_copy(out=TA[:, 0:L], in_=YT[:, j, :]), reads=[rYT[j]], writes=[rTA])
                    S.dma("sp", dbg[j * 128:(j + 1) * 128, :], TA[:, 0:L], reads=[rTA], writes=[rDBG])
            merge_branch(2)

        if 3 in branches:
            S.op("dve", lambda e: e.tensor_scalar_mul(out=SMT[:, 0:8], in0=cp("lru_lam"), scalar1=-1.0), reads=[rCP], writes=[rSM])
            softplus_small(C8, SMT[:, 0:8], SMT[:, 8:32], 8, [rSM], [rSM])
            S.op("dve", lambda e: e.tensor_scalar_mul(out=C8, in0=C8, scalar1=-8.0), reads=[rSM], writes=[rSM])
            S.op("dve", lambda e: e.memset(TB[:, 0:4], 0.0), writes=[rTB])
            LWT = [AR.view(MOFF + 6 * 2048 + i * 1024, [128, 2, 128], F32) for i in range(2)]
            rLWT = [Res("lwt0"), Res("lwt1")]
            for j in range(8):
                lwt, lwr = LWT[j % 2], rLWT[j % 2]
                S.dma("sp", lwt, lruw_in[l, :, j].rearrange("g p c -> p g c"), writes=[lwr])
                proj_fm(l, D_X + j * 128, 128, evac_copy(TB, 3, rTB))
                cw = cp("lru_cw")
                S.op("dve", lambda e, j=j: e.tensor_scalar(out=TA[:, 0:L], in0=TB[:, 0:L], scalar1=cw[:, j * 4:j * 4 + 1],
                                                          scalar2=cp("lru_cb")[:, j:j + 1], op0=ALU.mult, op1=ALU.add),
                     reads=[rTB, rCP], writes=[rTA])
                for kk in (1, 2, 3):
                    S.op("dve", lambda e, j=j, kk=kk: e.scalar_tensor_tensor(
                        out=TA[:, 0:L], in0=TB[:, kk:kk + L], scalar=cw[:, j * 4 + kk:j * 4 + kk + 1], in1=TA[:, 0:L],
                        op0=ALU.mult, op1=ALU.add), reads=[rTB, rTA, rCP], writes=[rTA])
                for gi, (bname, dstT, dres) in enumerate((("lru_ba", TB, rTB), ("lru_bx", TC, rTC))):
                    for tt in range(4):
                        b = 4 + (gi * 4 + tt) % 4
                        S.op("pe", lambda e, tt=tt, b=b, gi=gi, lwt=lwt: e.matmul(
                            PB[b], lhsT=lwt[:, gi, :], rhs=TA[:, tt * 512:(tt + 1) * 512],
                            start=True, stop=True), reads=[lwr, rTA], writes=[PR[b]])
                        S.op("act", lambda e, tt=tt, b=b, bname=bname, dstT=dstT, j=j: e.activation(
                            out=dstT[:, tt * 512:(tt + 1) * 512], in_=PB[b], func=AF.Sigmoid, bias=cp(bname)[:, j:j + 1]),
                            reads=[PR[b], rCP], writes=[dres], acc=(tt > 0))
                S.op("act", lambda e, j=j: e.activation(out=TB[:, 0:L], in_=TB[:, 0:L], func=AF.Exp, scale=C8[:, j:j + 1]),
                     reads=[rTB, rSM], writes=[rTB])
                S.op("dve", lambda e: e.tensor_mul(out=TA[:, 0:L], in0=TA[:, 0:L], in1=TC[:, 0:L]), reads=[rTA, rTC], writes=[rTA])
                S.op("act", lambda e: e.activation(out=TC[:, 0:L], in_=TB[:, 0:L], func=AF.Square), reads=[rTB, rTC], writes=[rTC])
                S.op("act", lambda e: e.activation(out=TC[:, 0:L], in_=TC[:, 0:L], func=AF.Sqrt, scale=-1.0, bias=1.0),
                     reads=[rTC], writes=[rTC])
                S.op("dve", lambda e: e.tensor_mul(out=TA[:, 0:L], in0=TA[:, 0:L], in1=TC[:, 0:L]), reads=[rTA, rTC], writes=[rTA])
                S.op("dve", lambda e: e.tensor_tensor_scan(out=TC[:, 0:L], data0=TB[:, 0:L], data1=TA[:, 0:L], initial=0.0,
                                                          op0=ALU.mult, op1=ALU.add), reads=[rTA, rTB], writes=[rTC])
                proj_fm(l, D_G + j * 128, 128, evac_act(TA, 0, rTA, AF.Silu))
                S.op("dve", lambda e, j=j: e.tensor_mul(out=YT[:, j, :], in0=TA[:, 0:L], in1=TC[:, 0:L]),
                     reads=[rTA, rTC], writes=[rYT[j]])
                S.op("dve", lambda e: e.memset(TB[:, 0:4], 0.0), reads=[rTB], writes=[rTB])
            if debug is not None and debug[0] == "yd" and l == debug[2]:
                for j in range(8):
                    S.op("dve", lambda e, j=j: e.tensor_copy(out=TA[:, 0:L], in_=YT[:, j, :]), reads=[rYT[j]], writes=[rTA])
                    S.dma("sp", dbg[j * 128:(j + 1) * 128, :], TA[:, 0:L], reads=[rTA], writes=[rDBG])
            merge_branch(3)

        if first_branch[0]:
            ZT = AR.view(O_T, [128, L], F32)
            S.op("dve", lambda e: e.memset(ZT, 0.0), writes=[rT[0]])
            for dc in range(NKC):
                S.dma("sp", mscr[dc], ZT, reads=[rT[0]], writes=[rMT[dc]])
        S.barrier()

        WO = AR.view(O_HT, [128, NKC, D], BF16)
        rWOk = [Res("wo%d" % kc) for kc in range(NKC)]
        for kc in range(NKC):
            S.dma("pool", WO[:, kc, :], w_out[l, kc * 128:(kc + 1) * 128, :], writes=[rWOk[kc]])
        GR = AR.view(O_Y, [128, D], F32)
        LW = AR.view(O_Y + 8192, [128, D], F32)
        LB = AR.view(O_Y + 16384, [128, D], F32)
        rGR = Res("gr")
        S.dma("sp", GR, gscr, reads=[rG], writes=[rGR])
        S.dma("sp", LW, ln_wb[l, 0].partition_broadcast(128), writes=[rGR])
        S.dma("sp", LB, ln_wb[l, 1].partition_broadcast(128), writes=[rGR])
        for kc in range(NKC):
            S.op("dve", lambda e, kc=kc: e.tensor_mul(out=WO[:, kc, :], in0=WO[:, kc, :], in1=GR), reads=[rWOk[kc], rGR], writes=[rWOk[kc]])
        XT = [AR.view(O_T + i * 8192, [128, D], F32) for i in range(2)]
        RSB = [AR.view(O_T + (2 + i) * 8192, [128, D], F32) for i in range(2)]
        MTT = [AR.view(O_T + 4 * 8192 + i * 4096, [128, NKC, 128], BF16) for i in range(2)]
        rMTT = [Res("mtt0"), Res("mtt1")]
        rXT = [rT[0], rT[1]]
        rRSB = [rT[2], rT[3]]
        ST = SM[:, 256:320]
        rST4 = [Res("st4a"), Res("st4b")]
        def p4_mm(t16):
            xt, xr = XT[t16 % 2], rXT[t16 % 2]
            RS, rRS = RSB[t16 % 2], rRSB[t16 % 2]
            MT1, rMT1 = MTT[t16 % 2], rMTT[t16 % 2]
            S.dma("sp", xt, xin[t16 * 128:(t16 + 1) * 128, :], reads=[rX1] if l > 0 else [], writes=[xr])
            S.dma("pool", MT1, mscr[:, :, t16 * 128:(t16 + 1) * 128].rearrange("dc p t -> p dc t"), reads=rMT, writes=[rMT1])
            for nb in range(4):
                b = (t16 * 4 + nb) % 8
                for kc in range(NKC):
                    S.op("pe", lambda e, kc=kc, nb=nb, b=b, MT1=MT1: e.matmul(
                        PB[b], lhsT=MT1[:, kc, :], rhs=WO[:, kc, nb * 512:(nb + 1) * 512],
                        start=(kc == 0), stop=(kc == NKC - 1)), reads=[rMT1, rWOk[kc]], writes=[PR[b]], acc=(kc > 0))
                S.op("dve", lambda e, nb=nb, b=b, RS=RS, xt=xt: e.scalar_tensor_tensor(
                    out=RS[:, nb * 512:(nb + 1) * 512], in0=xt[:, nb * 512:(nb + 1) * 512], scalar=ALPHA, in1=PB[b],
                    op0=ALU.mult, op1=ALU.add), reads=[PR[b], xr], writes=[rRS], acc=(nb > 0))
                yield

        def p4_ln(t16):
            xt, xr = XT[t16 % 2], rXT[t16 % 2]
            RS, rRS = RSB[t16 % 2], rRSB[t16 % 2]
            rst = rST4[t16 % 2]
            c0 = (t16 % 2) * 8
            mean, nb_, ssq, rstd, msq = (ST[:, c0 + i:c0 + i + 1] for i in range(5))
            S.op("act", lambda e: e.activation(out=xt, in_=RS, func=AF.Copy, accum_out=mean), reads=[rRS], writes=[xr, rst])
            yield
            S.op("act", lambda e: e.activation(out=xt, in_=RS, func=AF.Square, accum_out=ssq), reads=[rRS], writes=[xr, rst])
            yield
            S.op("dve", lambda e: e.tensor_scalar_mul(out=mean, in0=mean, scalar1=1.0 / D), reads=[rst], writes=[rst])
            S.op("dve", lambda e: e.tensor_mul(out=msq, in0=mean, in1=mean), reads=[rst], writes=[rst])
            yield
            S.op("dve", lambda e: e.scalar_tensor_tensor(out=rstd, in0=ssq, scalar=1.0 / D, in1=msq, op0=ALU.mult, op1=ALU.subtract),
                 reads=[rst], writes=[rst])
            S.op("dve", lambda e: e.tensor_scalar_add(out=rstd, in0=rstd, scalar1=LN_EPS), reads=[rst], writes=[rst])
            S.op("act", lambda e: e.activation(out=rstd, in_=rstd, func=AF.Sqrt), reads=[rst], writes=[rst])
            yield
            S.op("dve", lambda e: e.reciprocal(out=rstd, in_=rstd), reads=[rst], writes=[rst])
            S.op("dve", lambda e: e.scalar_tensor_tensor(out=nb_, in0=mean, scalar=-1.0, in1=rstd, op0=ALU.mult, op1=ALU.mult),
                 reads=[rst], writes=[rst])
            S.op("act", lambda e: e.activation(out=xt, in_=RS, func=AF.Identity, scale=rstd, bias=nb_), reads=[rRS, rst], writes=[xr])
            yield
            S.op("dve", lambda e: e.tensor_mul(out=xt, in0=xt, in1=LW), reads=[xr, rGR], writes=[xr])
            yield
            S.op("dve", lambda e: e.tensor_add(out=xt, in0=xt, in1=LB), reads=[xr, rGR], writes=[xr])
            S.dma("sp", xout[t16 * 128:(t16 + 1) * 128, :], xt, reads=[xr], writes=[rX1 if xout is x1 else rOUT])
            yield

        run_gens([(p4_mm(0), 1)])
        for t16 in range(16):
            gens = [(p4_ln(t16), 2)]
            if t16 + 1 < 16:
                gens.insert(0, (p4_mm(t16 + 1), 1))
            run_gens(gens)
        S.barrier()

    S.finish("sp")
    return nc, S


def _fm(v, chunks):
    v = np.asarray(v)
    return np.ascontiguousarray(np.moveaxis(v.reshape((chunks, 128) + v.shape[1:]), 0, 1))


def _pack_cp(inp, l):
    cp = np.zeros((128, NCP), np.float32)

    def put(name, arr):
        a, w = CP[name]
        cp[:, a:a + w] = np.asarray(arr, np.float32).reshape(128, w)
    put("ssd_cw", _fm(np.asarray(inp["ssd_conv_w"][l]).T, 16))
    put("ssd_cb", _fm(inp["ssd_conv_b"][l], 16))
    put("ssd_nw", _fm(inp["ssd_norm_w"][l], 8))
    put("ssd_d", _fm(np.repeat(np.asarray(inp["ssd_d"][l]), 64), 8))
    put("sconv_w", _fm(np.asarray(inp["sconv_w"][l]).T, 8))
    put("lru_cw", _fm(np.asarray(inp["lru_conv_w"][l]).T, 8))
    put("lru_cb", _fm(inp["lru_conv_b"][l], 8))
    put("lru_ba", _fm(inp["lru_b_a"][l], 8))
    put("lru_bx", _fm(inp["lru_b_x"][l], 8))
    put("lru_lam", _fm(inp["lru_lambda"][l], 8))
    put("b_gate", np.stack([_fm(np.asarray(inp["b_gate"][l][k]), 16) for k in range(4)], axis=1))
    put("sinks", np.broadcast_to(np.asarray(inp["attn_sinks"][l])[None, :], (128, 16)))
    put("dt_bias", np.broadcast_to(np.asarray(inp["ssd_dt_bias"][l])[None, :], (128, 16)))
    put("a_log", np.broadcast_to(np.asarray(inp["ssd_a_log"][l])[None, :], (128, 16)))
    return cp


def _pack_lruw(inp):
    out = np.zeros((DEPTH, 2, 8, 128, 128), np.float32)
    for l in range(DEPTH):
        for gi, key in enumerate(("lru_w_a", "lru_w_x")):
            w = np.asarray(inp[key][l])
            for j in range(8):
                out[l, gi, j, 0:64, 0:64] = w[2 * j]
                out[l, gi, j, 64:128, 64:128] = w[2 * j + 1]
    return out


def _konst():
    k = np.zeros((128, NKO), np.float32)

    def put(name, arr):
        a, w = KO[name]
        k[:, a:a + w] = arr
    put("ident", np.eye(128, dtype=np.float32))
    put("triu", np.triu(np.ones((128, 128), np.float32)))
    put("ones", np.ones((128, 128), np.float32))
    rot = np.zeros((128, 128), np.float32)
    for p in range(128):
        if p % 64 < 32:
            rot[p + 32, p] = -1.0
        else:
            rot[p - 32, p] = 1.0
    put("rot", rot)
    half = 32
    invf = (10000.0 ** (-np.arange(half, dtype=np.float32) / half)).astype(np.float32)
    put("invf", np.tile(invf, 4)[:, None])
    qi = np.arange(128)[:, None]
    sj = np.arange(256)[None, :]
    rel = qi + 128 - sj
    band = (rel >= 0) & (rel < 128)
    put("amask", np.where(band, 0.0, -30000.0).astype(np.float32))
    put("amask0", np.where(band & (sj >= 128), 0.0, -30000.0).astype(np.float32))
    put("halfpi", np.full((128, 1), np.pi / 2, np.float32))
    s_ = np.arange(128)[:, None]
    l_ = np.arange(128)[None, :]
    put("smask", np.where(l_ >= s_, 0.0, -30000.0).astype(np.float32))
    return k


_CACHE = {}


def make_in_maps(inp, cores):
    f = lambda k: np.ascontiguousarray(np.asarray(inp[k], dtype=np.float32))
    shared = {
        "w_ada": f("w_ada"), "b_ada": f("b_ada").reshape(DEPTH, 1, 3 * D), "w_in": f("w_in"),
        "w_branch": f("w_branch"), "w_out": f("w_out"),
        "ln_wb": np.ascontiguousarray(np.stack([f("ln_w"), f("ln_b")], axis=1).reshape(DEPTH, 2, 1, D)),
        "cp": np.stack([_pack_cp(inp, l) for l in range(DEPTH)], axis=0),
        "konst": _konst(), "lruw": _pack_lruw(inp),
    }
    x = np.asarray(inp["x"], dtype=np.float32)
    c = np.asarray(inp["c"], dtype=np.float32)
    pos = np.asarray(inp["positions"]).astype(np.int32)
    maps = []
    for b in cores:
        m = dict(shared)
        m["x"] = np.ascontiguousarray(x[b])
        m["c"] = np.ascontiguousarray(c[b].reshape(NKC, 128).T)
        m["pos"] = np.ascontiguousarray(pos[b].reshape(1, L))
        maps.append(m)
    return maps


def kernel(**inputs):
    if "nc" not in _CACHE:
        _CACHE["nc"] = build_program()[0]
    nc = _CACHE["nc"]
    maps = make_in_maps(inputs, list(range(8)))
    res = run_bass_kernel_spmd(nc, maps, core_ids=list(range(8)))
    return np.stack([np.asarray(r["y"], dtype=np.float32) for r in res.results], axis=0)
```

```python
import numpy as np
import concourse.bass as bass
import concourse.mybir as mybir
from concourse.bass_utils import run_bass_kernel_spmd

F32 = mybir.dt.float32
BF16 = mybir.dt.bfloat16
I32 = mybir.dt.int32
U8 = mybir.dt.uint8
AF = mybir.ActivationFunctionType
ALU = mybir.AluOpType
AX = mybir.AxisListType

D = 2048
L = 2048
DEPTH = 2
NKC = 16
BW = 1024
IN_COLS = 19984
A_Z, A_X, A_B, A_C, A_DT = 0, 1024, 2048, 2560, 3072
B_Q, B_K, B_V, B_G = 3088, 4112, 4368, 4624
C_B, C_C, C_X, C_G = 5648, 6672, 7696, 8720
D_X, D_G = 9744, 10768
MERGE = 11792
ALPHA = (2.0 * DEPTH) ** 0.25
LN_EPS = 1e-5
RMS_EPS = 1e-5

CP = {}
_o = 0
for _n, _w in [("ssd_cw", 64), ("ssd_cb", 16), ("ssd_nw", 8), ("ssd_d", 8), ("sconv_w", 24), ("lru_cw", 32),
               ("lru_cb", 8), ("lru_ba", 8), ("lru_bx", 8), ("lru_lam", 8), ("b_gate", 64), ("sinks", 16),
               ("dt_bias", 16), ("a_log", 16)]:
    CP[_n] = (_o, _w)
    _o += _w
NCP = _o
KO = {}
_o = 0
for _n, _w in [("ident", 128), ("triu", 128), ("ones", 128), ("rot", 128), ("invf", 1), ("amask", 256),
               ("smask", 128), ("amask0", 256), ("halfpi", 1)]:
    KO[_n] = (_o, _w)
    _o += _w
NKO = _o


class Res:
    __slots__ = ("name", "w", "r", "excl")

    def __init__(self, name, excl=False):
        self.name = name
        self.w = {}
        self.r = {}
        self.excl = excl


class Sched:
    NDS = 12

    def __init__(self, nc):
        self.nc = nc
        self.eng = {"pe": nc.tensor, "act": nc.scalar, "dve": nc.vector, "pool": nc.gpsimd, "sp": nc.sync}
        self.sem = {k: nc.alloc_semaphore("s_" + k) for k in self.eng}
        self.cnt = {k: 0 for k in self.eng}
        self.seen = {k: {} for k in self.eng}
        self.dsems = {q: [[nc.alloc_semaphore("d_%s_%d" % (q, i)), 0] for i in range(self.NDS)]
                      for q in ("sp", "pool", "act")}
        self.dptr = {q: 0 for q in self.dsems}
        self.all_dma = {}
        self.ninst = 0

    SAME_ENGINE_WAITS = ("pool", "pe", "sp", "act", "dve")

    def _wait(self, e, deps):
        need = {}
        own = self.sem[e].name
        for d in deps:
            for nm, (sem, val) in d.items():
                if nm == own and e not in self.SAME_ENGINE_WAITS:
                    continue
                if self.seen[e].get(nm, 0) < val and need.get(nm, (None, 0))[1] < val:
                    need[nm] = (sem, val)
        for nm, (sem, val) in need.items():
            self.eng[e].wait_ge(sem, val)
            self.seen[e][nm] = val

    def _deps(self, reads, writes, acc):
        deps = []
        for r in reads:
            deps.append(r.w)
            if r.excl:
                deps.append(r.r)
        if not acc:
            for w in writes:
                deps.append(w.w)
                deps.append(w.r)
        return deps

    def _commit(self, reads, writes, sem, val):
        nm = sem.name
        for w in writes:
            w.w = {nm: (sem, val)}
            w.r = {}
        for r in reads:
            r.r[nm] = (sem, val)

    def op(self, e, fn, reads=(), writes=(), acc=False):
        self._wait(e, self._deps(reads, writes, acc))
        ins = fn(self.eng[e])
        self.cnt[e] += 1
        ins.then_inc(self.sem[e], 1)
        self.seen[e][self.sem[e].name] = max(self.seen[e].get(self.sem[e].name, 0), 0)
        self._commit(reads, writes, self.sem[e], self.cnt[e])
        self.ninst += 1

    def dma(self, q, out, in_, reads=(), writes=(), **kw):
        slot = self.dsems[q][self.dptr[q] % self.NDS]
        self.dptr[q] += 1
        sem, uses = slot
        deps = self._deps(reads, writes, False)
        if uses > 0:
            deps.append({sem.name: (sem, 16 * uses)})
        self._wait(q, deps)
        self.eng[q].dma_start(out=out, in_=in_, **kw).then_inc(sem, 16)
        slot[1] = uses + 1
        self._commit(reads, writes, sem, 16 * (uses + 1))
        self.all_dma[sem.name] = (sem, 16 * (uses + 1))
        self.ninst += 1
        return sem, 16 * (uses + 1)

    def barrier(self):
        allt = {}
        for k in self.eng:
            if self.cnt[k] > 0:
                allt[self.sem[k].name] = (self.sem[k], self.cnt[k])
        allt.update(self.all_dma)
        for k in self.eng:
            self._wait(k, [allt])

    def finish(self, e="sp"):
        allt = dict(self.all_dma)
        for k in self.eng:
            if self.cnt[k] > 0:
                allt[self.sem[k].name] = (self.sem[k], self.cnt[k])
        self._wait(e, [allt])


def run_gens(gens):
    live = [[g, n] for g, n in gens]
    while live:
        for item in list(live):
            g, n = item
            for _ in range(n):
                try:
                    next(g)
                except StopIteration:
                    live.remove(item)
                    break


def rr_gen(gens):
    live = [[g, n] for g, n in gens]
    while live:
        for item in list(live):
            g, n = item
            for _ in range(n):
                try:
                    next(g)
                except StopIteration:
                    live.remove(item)
                    break
            yield


class Arena:
    def __init__(self, nc, nbytes):
        self.t = nc.alloc_sbuf_tensor("arena", [128, nbytes], U8)
        self.ap = self.t.ap()
        self.nbytes = nbytes

    def view(self, off, shape, dtype):
        esz = mybir.dt.size(dtype)
        n = 1
        for s in shape[1:]:
            n *= s
        assert off % 4 == 0 and off + n * esz <= self.nbytes, (off, shape, self.nbytes)
        v = self.ap[:, off:off + n * esz].bitcast(dtype)
        if len(shape) == 3:
            v = v.rearrange("p (a b) -> p a b", b=shape[2])
        elif len(shape) == 4:
            v = v.rearrange("p (a b c) -> p a b c", b=shape[2], c=shape[3])
        return v


ATTN_STOP = 9


def build_program(n_layers=DEPTH, branches=(0, 1, 2, 3), debug=None):
    nc = bass.Bass("TRN2", target_bir_lowering=False)
    dt_in = lambda name, shape, dt=F32: nc.dram_tensor(name, list(shape), dt, kind="ExternalInput").ap()
    x_in = dt_in("x", [L, D])
    c_in = dt_in("c", [128, NKC])
    pos_in = dt_in("pos", [1, L], I32)
    w_ada = dt_in("w_ada", [DEPTH, D, 3 * D])
    b_ada = dt_in("b_ada", [DEPTH, 1, 3 * D])
    w_in = dt_in("w_in", [DEPTH, D, IN_COLS])
    w_br = dt_in("w_branch", [DEPTH, 4, BW, D])
    w_out = dt_in("w_out", [DEPTH, D, D])
    ln_wb = dt_in("ln_wb", [DEPTH, 2, 1, D])
    cp_in = dt_in("cp", [DEPTH, 128, NCP])
    lruw_in = dt_in("lruw", [DEPTH, 2, 8, 128, 128])
    ko_in = dt_in("konst", [128, NKO])
    y_out = nc.dram_tensor("y", [L, D], F32, kind="ExternalOutput").ap()
    x1 = nc.dram_tensor("x1_scr", [L, D], F32).ap()
    gscr = nc.dram_tensor("gate_scr", [128, D], F32).ap()
    mscr = nc.dram_tensor("m_scr", [NKC, 128, L], F32).ap()
    dbg = None
    if debug is not None:
        dbg = nc.dram_tensor("dbg", list(debug[1]), F32, kind="ExternalOutput").ap()

    S = Sched(nc)
    AR = Arena(nc, 206 * 1024)
    psum_t = nc.alloc_psum_tensor("ps", [128, 4096], F32)
    PS = psum_t.ap()
    PB = [PS[:, b * 512:(b + 1) * 512] for b in range(8)]
    PR = [Res("psum%d" % b, excl=True) for b in range(8)]

    o = 0
    def take(n):
        nonlocal o
        r = o
        o += (n + 31) // 32 * 32
        return r
    O_KO = take(NKO * 4)
    O_CP = take(NCP * 4)
    O_SMALL = take(4096)
    O_HT = take(NKC * L * 2)
    O_Y = take(8 * L * 2)
    O_W = take(4 * NKC * 128 * 2)
    O_WB = take(3 * 8 * 128 * 2)
    O_T = take(0)
    T_BYTES = AR.nbytes - O_T
    assert T_BYTES >= 64 * 1024, T_BYTES

    KOt = AR.view(O_KO, [128, NKO], F32)
    CPt = AR.view(O_CP, [128, NCP], F32)
    SM = AR.view(O_SMALL, [128, 1024], F32)
    HT = AR.view(O_HT, [128, NKC, L], BF16)
    YT = AR.view(O_Y, [128, 8, L], BF16)
    WR = [AR.view(O_W + i * NKC * 128 * 2, [128, NKC, 128], BF16) for i in range(4)]
    WRr = [Res("wr%d" % i) for i in range(4)]
    WBR = [AR.view(O_WB + i * 8 * 128 * 2, [128, 8, 128], BF16) for i in range(3)]
    WBRr = [Res("wbr%d" % i) for i in range(3)]
    rKO, rCP, rSM = Res("ko"), Res("cp"), Res("sm")
    rHT = [Res("ht%d" % i) for i in range(4)]
    rMT = [Res("mt%d" % i) for i in range(NKC)]
    rYT = [Res("yt%d" % i) for i in range(8)]
    rT = [Res("t%d" % i) for i in range(8)]
    rX1, rG, rDBG, rOUT = Res("x1"), Res("gscr"), Res("dbg"), Res("out")

    def ko(name, rows=128):
        a, w = KO[name]
        return KOt[0:rows, a:a + w]

    def cp(name, j=None, w=None):
        a, ww = CP[name]
        if j is None:
            return CPt[:, a:a + ww]
        w = w or 1
        return CPt[:, a + j * w:a + (j + 1) * w]

    SH = SM[:, 0:16]
    SC1 = SM[:, 16:32]
    C8 = SM[:, 32:40]
    CACT = SM[:, 40:56]
    SMT = SM[:, 64:256]

    S.dma("sp", KOt, ko_in, writes=[rKO])
    S.dma("sp", CACT, c_in, writes=[rSM])
    S.op("act", lambda e: e.activation(out=CACT, in_=CACT, func=AF.Silu), reads=[rSM], writes=[rSM])

    wr_i = [0]

    def proj_fm_g(l, col0, ncols, consumer, banks=(0, 1, 2, 3), wload=None):
        i = wr_i[0] % 3
        wr_i[0] += 1
        wt, wres = WR[i], WRr[i]
        if wload is None:
            S.dma("pool", wt[:, :, 0:ncols], w_in[l, :, col0:col0 + ncols].rearrange("(kc p) c -> p kc c", p=128),
                  writes=[wres])
        else:
            wload(wt, wres)
        for tt in range(4):
            b = banks[tt % len(banks)]
            for kc in range(NKC):
                S.op("pe", lambda e, kc=kc, tt=tt, b=b: e.matmul(
                    PB[b][0:ncols, :], lhsT=wt[:, kc, 0:ncols], rhs=HT[:, kc, tt * 512:(tt + 1) * 512],
                    start=(kc == 0), stop=(kc == NKC - 1)),
                    reads=[wres, rHT[tt]], writes=[PR[b]], acc=(kc > 0))
                if kc % 8 == 7:
                    yield
            consumer(tt, PB[b][0:ncols, :], PR[b])
            yield

    def proj_fm(*a, **kw):
        for _ in proj_fm_g(*a, **kw):
            pass

    def softplus_small(dst, src, tmp, n, reads, writes):
        t0, t1, t2 = tmp[:, 0:n], tmp[:, n:2 * n], tmp[:, 2 * n:3 * n]
        rw = dict(reads=reads, writes=writes)
        S.op("dve", lambda e: e.tensor_scalar_mul(out=t1, in0=src, scalar1=-1.0), **rw)
        S.op("dve", lambda e: e.tensor_max(out=t0, in0=src, in1=t1), **rw)
        S.op("act", lambda e: e.activation(out=t0, in_=t0, func=AF.Exp, scale=-1.0), **rw)
        S.op("dve", lambda e: e.tensor_scalar_add(out=t1, in0=t0, scalar1=2.0), **rw)
        S.op("dve", lambda e: e.reciprocal(out=t1, in_=t1), **rw)
        S.op("dve", lambda e: e.tensor_mul(out=t0, in0=t0, in1=t1), **rw)
        S.op("dve", lambda e: e.tensor_mul(out=t1, in0=t0, in1=t0), **rw)
        S.op("dve", lambda e: e.tensor_scalar(out=t2, in0=t1, scalar1=1.0 / 9.0, scalar2=1.0 / 7.0,
                                              op0=ALU.mult, op1=ALU.add), **rw)
        for cst in (1.0 / 5.0, 1.0 / 3.0, 1.0):
            S.op("dve", lambda e: e.tensor_mul(out=t2, in0=t2, in1=t1), **rw)
            S.op("dve", lambda e, cst=cst: e.tensor_scalar_add(out=t2, in0=t2, scalar1=cst), **rw)
        S.op("dve", lambda e: e.tensor_mul(out=t2, in0=t2, in1=t0), **rw)
        S.op("dve", lambda e: e.tensor_scalar_max(out=t0, in0=src, scalar1=0.0), **rw)
        S.op("dve", lambda e: e.scalar_tensor_tensor(out=dst, in0=t2, scalar=2.0, in1=t0,
                                                     op0=ALU.mult, op1=ALU.add), **rw)

    for l in range(n_layers):
        xin = x_in if l == 0 else x1
        xout = y_out if l == n_layers - 1 else x1
        S.barrier()
        S.dma("sp", CPt, cp_in[l], writes=[rCP])

        CB = AR.view(O_T, [128, NKC, 128], F32)
        ADA = AR.view(O_T + 8192, [128, 3 * D], F32)
        WA = [AR.view(O_T + 8192 + 24576 + i * 16384, [128, 8, 512], F32) for i in range(2)]
        rCB = Res("cb")
        rADAb = [Res("ada%d" % nb) for nb in range(12)]
        rWA = [Res("wa0"), Res("wa1")]
        S.op("dve", lambda e: e.tensor_copy(out=CB, in_=CACT.unsqueeze(2).to_broadcast([128, NKC, 128])),
             reads=[rSM], writes=[rCB])
        S.dma("sp", ADA, b_ada[l].partition_broadcast(128), writes=rADAb)
        XT = [AR.view(O_Y + i * 8192, [128, D], F32) for i in range(2)]
        HB = [AR.view(O_Y + 16384 + i * 4096, [128, D], BF16) for i in range(2)]
        IDB1 = AR.view(O_Y + 24576, [128, 128], BF16)
        rXT = [Res("xt0"), Res("xt1")]
        rHB = [Res("hb0"), Res("hb1")]
        rIDB1 = Res("idb1")
        S.op("dve", lambda e: e.tensor_copy(out=IDB1, in_=ko("ident")), reads=[rKO], writes=[rIDB1])
        wi = 0
        for nb in range(12):
            b = nb % 2
            for half in range(2):
                wt, wr = WA[wi % 2], rWA[wi % 2]
                wi += 1
                S.dma("sp" if half == 0 else "act", wt,
                      w_ada[l, half * 1024:(half + 1) * 1024, nb * 512:(nb + 1) * 512].rearrange(
                          "(kc p) c -> p kc c", p=128), writes=[wr])
                for k8 in range(8):
                    kc = half * 8 + k8
                    S.op("pe", lambda e, kc=kc, k8=k8, wt=wt, b=b: e.matmul(
                        PB[b], lhsT=CB[:, kc, :], rhs=wt[:, k8, :], start=(kc == 0), stop=(kc == NKC - 1)),
                        reads=[rCB, wr], writes=[PR[b]], acc=(kc > 0))
            if 4 <= nb < 8:
                S.op("dve", lambda e, nb=nb, b=b: e.scalar_tensor_tensor(
                    out=ADA[:, nb * 512:(nb + 1) * 512], in0=PB[b], scalar=1.0, in1=ADA[:, nb * 512:(nb + 1) * 512],
                    op0=ALU.add, op1=ALU.add), reads=[PR[b], rADAb[nb]], writes=[rADAb[nb]])
            else:
                S.op("dve", lambda e, nb=nb, b=b: e.tensor_add(out=ADA[:, nb * 512:(nb + 1) * 512],
                                                               in0=PB[b], in1=ADA[:, nb * 512:(nb + 1) * 512]),
                     reads=[PR[b], rADAb[nb]], writes=[rADAb[nb]])
        S.dma("sp", gscr, ADA[:, 2 * D:3 * D], reads=rADAb[8:12], writes=[rG])

        SHR = ADA[:, 0:D]
        SCR = ADA[:, D:2 * D]
        for t16 in range(16):
            xt, xr = XT[t16 % 2], rXT[t16 % 2]
            hb, hr = HB[t16 % 2], rHB[t16 % 2]
            S.dma("sp", xt, xin[t16 * 128:(t16 + 1) * 128, :], reads=[rX1] if l > 0 else [], writes=[xr])
            S.op("dve", lambda e, xt=xt: e.tensor_mul(out=xt, in0=xt, in1=SCR), reads=[xr] + rADAb[4:8], writes=[xr])
            S.op("dve", lambda e, xt=xt, hb=hb: e.tensor_add(out=hb, in0=xt, in1=SHR), reads=[xr] + rADAb[0:4], writes=[hr])
            pb0 = 2 * (t16 % 2)
            ptp = PS[:, pb0 * 512:(pb0 + 2) * 512].bitcast(BF16)
            for kc in range(NKC):
                S.op("pe", lambda e, kc=kc, hb=hb, ptp=ptp: e.transpose(out=ptp[:, kc * 128:(kc + 1) * 128],
                                                                       in_=hb[:, kc * 128:(kc + 1) * 128], identity=IDB1),
                     reads=[hr, rIDB1], writes=[PR[pb0 + kc // 8]], acc=(kc % 8 > 0))
            for bk in range(2):
                S.op("act", lambda e, bk=bk, ptp=ptp, t16=t16: e.activation(
                    out=HT[:, 8 * bk:8 * bk + 8, t16 * 128:(t16 + 1) * 128],
                    in_=ptp[:, bk * 1024:(bk + 1) * 1024].rearrange("p (a b) -> p a b", b=128), func=AF.Copy),
                    reads=[PR[pb0 + bk]], writes=[rHT[t16 // 4]], acc=True)
        if debug is not None and debug[0] == "ht" and l == debug[2]:
            DT_ = AR.view(O_T, [128, L], F32)
            for kc in range(NKC):
                S.op("dve", lambda e, kc=kc: e.tensor_copy(out=DT_, in_=HT[:, kc, :]), reads=rHT, writes=[rT[0]])
                S.dma("sp", dbg[kc * 128:(kc + 1) * 128, :], DT_, reads=[rT[0]], writes=[rDBG])
        S.barrier()

        TA = AR.view(O_T, [128, L + 32], F32)
        TB = AR.view(O_T + 8320, [128, L + 32], F32)
        TC = AR.view(O_T + 2 * 8320, [128, L + 32], F32)
        rTA, rTB, rTC = rT[0], rT[1], rT[2]
        first_branch = [True]

        def evac_copy(dst, off, res):
            def f(tt, ps, pres):
                S.op("act", lambda e: e.activation(out=dst[:, off + tt * 512: off + (tt + 1) * 512], in_=ps, func=AF.Copy),
                     reads=[pres], writes=[res], acc=(tt > 0))
            return f

        def evac_act(dst, off, res, func, bias=None):
            def f(tt, ps, pres):
                kw = {} if bias is None else {"bias": bias}
                S.op("act", lambda e: e.activation(out=dst[:, off + tt * 512: off + (tt + 1) * 512], in_=ps, func=func, **kw),
                     reads=[pres, rCP], writes=[res], acc=(tt > 0))
            return f

        def evac_mul(dst, doff, src, soff, res):
            def f(tt, ps, pres):
                S.op("dve", lambda e: e.tensor_mul(out=dst[:, doff + tt * 512: doff + (tt + 1) * 512], in0=ps,
                                                   in1=src[:, soff + tt * 512: soff + (tt + 1) * 512]),
                     reads=[pres, res], writes=[res], acc=(tt > 0))
            return f

        MOFF = O_T + 3 * 8320
        SGB = [AR.view(MOFF + i * 2048, [128, 512], F32) for i in range(3)]
        PVB = [AR.view(MOFF + 3 * 2048 + i * 2048, [128, 512], F32) for i in range(3)]
        rSGB = [Res("sg%d" % i) for i in range(3)]
        rPVB = [Res("pv%d" % i) for i in range(3)]

        def merge_branch(k, rstd_row=None, rstd_res=None):
            first = first_branch[0]
            it = 0
            pending = None
            for dc in range(NKC):
                wi_ = wr_i[0] % 3
                wr_i[0] += 1
                wg, wgr = WR[wi_], WRr[wi_]
                S.dma("pool", wg, w_in[l, :, MERGE + k * D + dc * 128: MERGE + k * D + (dc + 1) * 128].rearrange(
                    "(kc p) c -> p kc c", p=128), writes=[wgr])
                wb, wbr = WBR[dc % 3], WBRr[dc % 3]
                S.dma("pool", wb, w_br[l, k, :, dc * 128:(dc + 1) * 128].rearrange("(c p) d -> p c d", p=128),
                      writes=[wbr])
                for tt in range(4):
                    bg, bb = (it % 2) * 2, (it % 2) * 2 + 1
                    sg, sgr = SGB[it % 3], rSGB[it % 3]
                    pv, pvr = PVB[it % 3], rPVB[it % 3]
                    it += 1
                    if not first:
                        S.dma("sp", pv, mscr[dc, :, tt * 512:(tt + 1) * 512], reads=[rMT[dc]], writes=[pvr])
                    for kc in range(NKC):
                        S.op("pe", lambda e, kc=kc, tt=tt, bg=bg, wg=wg: e.matmul(
                            PB[bg], lhsT=wg[:, kc, :], rhs=HT[:, kc, tt * 512:(tt + 1) * 512],
                            start=(kc == 0), stop=(kc == NKC - 1)), reads=[wgr, rHT[tt]], writes=[PR[bg]], acc=(kc > 0))
                    for c in range(8):
                        S.op("pe", lambda e, c=c, tt=tt, bb=bb, wb=wb: e.matmul(
                            PB[bb], lhsT=wb[:, c, :], rhs=YT[:, c, tt * 512:(tt + 1) * 512],
                            start=(c == 0), stop=(c == 7)), reads=[wbr, rYT[c]], writes=[PR[bb]], acc=(c > 0))
                    bgc = cp("b_gate")[:, k * 16 + dc:k * 16 + dc + 1]
                    S.op("act", lambda e, bg=bg, sg=sg, bgc=bgc: e.activation(out=sg, in_=PB[bg], func=AF.Sigmoid, bias=bgc),
                         reads=[PR[bg], rCP], writes=[sgr])
                    if pending is not None:
                        pending()
                    S.op("dve", lambda e, bb=bb, sg=sg: e.tensor_mul(out=sg, in0=sg, in1=PB[bb]),
                         reads=[PR[bb], sgr], writes=[sgr])
                    if rstd_row is not None:
                        S.op("dve", lambda e, sg=sg, tt=tt: e.tensor_mul(out=sg, in0=sg, in1=rstd_row[:, tt * 512:(tt + 1) * 512]),
                             reads=[sgr, rstd_res], writes=[sgr])
                    if not first:
                        S.op("dve", lambda e, sg=sg, pv=pv: e.tensor_add(out=sg, in0=sg, in1=pv), reads=[sgr, pvr], writes=[sgr])
                    pending = (lambda sg=sg, sgr=sgr, dc=dc, tt=tt: S.dma(
                        "sp", mscr[dc, :, tt * 512:(tt + 1) * 512], sg, reads=[sgr], writes=[rMT[dc]]))
            pending()
            first_branch[0] = False


        if 0 in branches:
            S.barrier()
            o2 = [O_T]
            def tk(n):
                r = o2[0]
                o2[0] += (n + 31) // 32 * 32
                return r
            SSQ = AR.view(tk(8192), [128, L], F32)
            DTT = AR.view(tk(1024), [128, 16, 16], F32)
            DA = AR.view(tk(1024), [128, 16, 16], F32)
            CUMC = AR.view(tk(1024), [128, 16, 16], F32)
            CDEC = AR.view(tk(1024), [128, 16, 16], F32)
            DTDE = AR.view(tk(1024), [128, 16, 16], F32)
            SPT = AR.view(tk(3072), [128, 768], F32)
            XS = [AR.view(tk(8192), [128, L], F32) for _ in range(2)]
            CV = AR.view(tk(8320), [128, L + 32], F32)
            BT = AR.view(tk(4096), [128, L], BF16)
            CT = AR.view(tk(4096), [128, L], BF16)
            ZS = [AR.view(tk(4096), [128, L], BF16) for _ in range(2)]
            XDTc = [AR.view(tk(512), [128, 4, 64], BF16) for _ in range(2)]
            XDEc = [AR.view(tk(512), [128, 4, 64], BF16) for _ in range(2)]
            Bc = [AR.view(tk(256), [128, 128], BF16) for _ in range(2)]
            RR = AR.view(tk(2048), [128, 4, 128], F32)
            SEG = AR.view(tk(2048), [128, 4, 128], F32)
            MT2 = [AR.view(tk(1024), [128, 4, 128], BF16) for _ in range(2)]
            EC = AR.view(tk(2048), [128, 4, 128], F32)
            CE2 = [AR.view(tk(1024), [128, 4, 128], BF16) for _ in range(2)]
            rMT2 = [Res("mtc0"), Res("mtc1")]
            rCE2 = [Res("cec0"), Res("cec1")]
            STATE = AR.view(tk(1024), [128, 4, 64], F32)
            STATEB = AR.view(tk(512), [128, 256], BF16)
            IDB = AR.view(tk(256), [128, 128], BF16)
            assert o2[0] <= AR.nbytes, o2[0]
            (rSSQ, rDT, rXS0, rXS1, rCV, rBT, rCT, rZS0, rZS1, rRR, rSEG, rMTc, rEC, rCEc, rST, rSTB, rIDB, rSPT) = (
                Res(n) for n in ("ssq", "dt", "xs0", "xs1", "cv", "bt", "ct", "zs0", "zs1", "rr", "seg", "mtc", "ec", "cec",
                                 "st", "stb", "idb", "spt"))
            rXS = [rXS0, rXS1]
            rZS = [rZS0, rZS1]
            rXDT = [Res("xdt0"), Res("xdt1")]
            rXDE = [Res("xde0"), Res("xde1")]
            rBc = [Res("bc0"), Res("bc1")]
            S.op("dve", lambda e: e.tensor_copy(out=IDB, in_=ko("ident")), reads=[rKO], writes=[rIDB])
            S.op("dve", lambda e: e.memset(SSQ, 0.0), writes=[rSSQ])
            S.op("dve", lambda e: e.memset(CV[:, 0:4], 0.0), writes=[rCV])
            wi_ = wr_i[0] % 3
            wr_i[0] += 1
            wdt, wdtr = WR[wi_], WRr[wi_]
            S.dma("pool", wdt[:, :, 0:16], w_in[l, :, A_DT:A_DT + 16].rearrange("(kc p) c -> p kc c", p=128), writes=[wdtr])
            for c in range(16):
                for kc in range(NKC):
                    S.op("pe", lambda e, c=c, kc=kc: e.matmul(PB[0][:, c * 16:(c + 1) * 16], lhsT=HT[:, kc, c * 128:(c + 1) * 128],
                                                             rhs=wdt[:, kc, 0:16], start=(kc == 0), stop=(kc == NKC - 1)),
                         reads=[wdtr, rHT[c // 4]], writes=[PR[0]], acc=(c > 0 or kc > 0))
            p0v = PB[0][:, 0:256].rearrange("p (c h) -> p c h", h=16)
            S.op("dve", lambda e: e.tensor_add(out=DTT, in0=p0v, in1=cp("dt_bias").unsqueeze(1).to_broadcast([128, 16, 16])),
                 reads=[PR[0], rCP], writes=[rDT])
            DTTf = DTT.rearrange("p c h -> p (c h)")
            DAf = DA.rearrange("p c h -> p (c h)")
            CUMCf = CUMC.rearrange("p c h -> p (c h)")
            CDECf = CDEC.rearrange("p c h -> p (c h)")
            DTDEf = DTDE.rearrange("p c h -> p (c h)")
            softplus_small(DTTf, DTTf, SPT, 256, [rDT, rSPT], [rDT, rSPT])
            S.op("act", lambda e: e.activation(out=SMT[:, 32:48], in_=cp("a_log"), func=AF.Exp), reads=[rCP], writes=[rSM])
            S.op("dve", lambda e: e.scalar_tensor_tensor(out=DA, in0=DTT, scalar=-1.0,
                                                         in1=SMT[:, 32:48].unsqueeze(1).to_broadcast([128, 16, 16]),
                                                         op0=ALU.mult, op1=ALU.mult), reads=[rDT, rSM], writes=[rDT])
            S.op("pe", lambda e: e.matmul(PB[1][:, 0:256], lhsT=ko("triu"), rhs=DAf, start=True, stop=True),
                 reads=[rKO, rDT], writes=[PR[1]])
            S.op("pe", lambda e: e.matmul(PB[2][:, 0:256], lhsT=ko("ones"), rhs=DAf, start=True, stop=True),
                 reads=[rKO, rDT], writes=[PR[2]])
            S.op("dve", lambda e: e.tensor_copy(out=CUMCf, in_=PB[1][:, 0:256]), reads=[PR[1]], writes=[rDT])
            S.op("act", lambda e: e.activation(out=CDECf, in_=PB[2][:, 0:256], func=AF.Exp), reads=[PR[2]], writes=[rDT])
            S.op("dve", lambda e: e.tensor_sub(out=DTDEf, in0=PB[2][:, 0:256], in1=CUMCf), reads=[PR[2], rDT], writes=[rDT])
            S.op("act", lambda e: e.activation(out=DTDEf, in_=DTDEf, func=AF.Exp), reads=[rDT], writes=[rDT])
            S.op("dve", lambda e: e.tensor_mul(out=DTDEf, in0=DTDEf, in1=DTTf), reads=[rDT], writes=[rDT])

            def conv_silu(chunk16, dst, dres, out_bf16):
                cw = cp("ssd_cw")
                acc_t = dst if not out_bf16 else SEGL
                acc_r = dres if not out_bf16 else rSEGL
                S.op("dve", lambda e: e.tensor_scalar(out=acc_t, in0=CV[:, 0:L], scalar1=cw[:, chunk16 * 4:chunk16 * 4 + 1],
                                                      scalar2=cp("ssd_cb")[:, chunk16:chunk16 + 1], op0=ALU.mult, op1=ALU.add),
                     reads=[rCV, rCP], writes=[acc_r])
                for kk in (1, 2, 3):
                    S.op("dve", lambda e, kk=kk: e.scalar_tensor_tensor(
                        out=acc_t, in0=CV[:, kk:kk + L], scalar=cw[:, chunk16 * 4 + kk:chunk16 * 4 + kk + 1], in1=acc_t,
                        op0=ALU.mult, op1=ALU.add), reads=[rCV, acc_r, rCP], writes=[acc_r])
                S.op("act", lambda e: e.activation(out=dst, in_=acc_t, func=AF.Silu), reads=[acc_r], writes=[dres, acc_r])

            SEGL = AR.view(O_T + 8192 + 5 * 1024 + 3072 + 2 * 8192 + 8320 + 16384 + 1536, [128, L], F32) if False else None
            rSEGL = None
            for g in range(4):
                SEGL, rSEGL = XS[1], rXS[1]
                proj_fm(l, A_B + g * 128, 128, evac_copy(CV, 3, rCV))
                conv_silu(8 + g, BT, rBT, True)
                proj_fm(l, A_C + g * 128, 128, evac_copy(CV, 3, rCV))
                conv_silu(12 + g, CT, rCT, True)
                for i in range(2):
                    proj_fm(l, A_X + (2 * g + i) * 128, 128, evac_copy(CV, 3, rCV))
                    conv_silu(2 * g + i, XS[i], rXS[i], False)
                    proj_fm(l, A_Z + (2 * g + i) * 128, 128, evac_act(ZS[i], 0, rZS[i], AF.Silu))
                S.op("dve", lambda e: e.memset(STATE, 0.0), writes=[rST])
                hs4 = slice(4 * g, 4 * g + 4)

                def ssd_P(c):
                    tok = slice(c * 128, (c + 1) * 128)
                    xd, rxd = XDTc[c % 2], rXDT[c % 2]
                    xe, rxe = XDEc[c % 2], rXDE[c % 2]
                    bc, rbc = Bc[c % 2], rBc[c % 2]
                    mt, rmt = MT2[c % 2], rMT2[c % 2]
                    ce, rce = CE2[c % 2], rCE2[c % 2]
                    for i in range(2):
                        b = 4 + i
                        S.op("pe", lambda e, i=i, b=b: e.transpose(out=PB[b][:, 0:128], in_=XS[i][:, tok], identity=ko("ident")),
                             reads=[rXS[i], rKO], writes=[PR[b]])
                        yield
                        pv = PB[b][:, 0:128].rearrange("p (h d) -> p h d", d=64)
                        hs = slice(4 * g + 2 * i, 4 * g + 2 * i + 2)
                        S.op("dve", lambda e, i=i, pv=pv, hs=hs: e.tensor_mul(
                            out=xd[:, 2 * i:2 * i + 2, :], in0=pv, in1=DTT[:, c, hs].unsqueeze(2).to_broadcast([128, 2, 64])),
                            reads=[PR[b], rDT], writes=[rxd], acc=(i > 0))
                        S.op("dve", lambda e, i=i, pv=pv, hs=hs: e.tensor_mul(
                            out=xe[:, 2 * i:2 * i + 2, :], in0=pv, in1=DTDE[:, c, hs].unsqueeze(2).to_broadcast([128, 2, 64])),
                            reads=[PR[b], rDT], writes=[rxe], acc=(i > 0))
                        yield
                    p6 = PB[6].bitcast(BF16)
                    S.op("pe", lambda e: e.transpose(out=p6[:, 0:128], in_=BT[:, tok], identity=IDB),
                         reads=[rBT, rIDB], writes=[PR[6]])
                    S.op("act", lambda e: e.activation(out=bc, in_=p6[:, 0:128], func=AF.Copy), reads=[PR[6]], writes=[rbc])
                    yield
                    S.op("pe", lambda e: e.matmul(PB[7][:, 0:128], lhsT=BT[:, tok], rhs=CT[:, tok], start=True, stop=True),
                         reads=[rBT, rCT], writes=[PR[7]])
                    S.op("dve", lambda e: e.tensor_mul(
                        out=RR, in0=DA[:, c, hs4].unsqueeze(2).to_broadcast([128, 4, 128]),
                        in1=ko("triu").unsqueeze(1).to_broadcast([128, 4, 128])), reads=[rDT, rKO], writes=[rRR])
                    yield
                    S.op("pe", lambda e: e.matmul(PB[3], lhsT=ko("ones"), rhs=RR.rearrange("p h l -> p (h l)"), start=True, stop=True),
                         reads=[rKO, rRR], writes=[PR[3]])
                    p3v = PB[3].rearrange("p (h l) -> p h l", l=128)
                    yield
                    S.op("dve", lambda e: e.tensor_sub(
                        out=SEG, in0=p3v, in1=CUMC[:, c, hs4].unsqueeze(2).to_broadcast([128, 4, 128])),
                        reads=[PR[3], rDT], writes=[rSEG])
                    if c > 0:
                        S.op("act", lambda e: e.activation(out=EC, in_=p3v, func=AF.Exp), reads=[PR[3], rSEG], writes=[rEC])
                    yield
                    S.op("dve", lambda e: e.tensor_add(out=SEG, in0=SEG, in1=ko("smask").unsqueeze(1).to_broadcast([128, 4, 128])),
                         reads=[rSEG, rKO], writes=[rSEG])
                    yield
                    S.op("act", lambda e: e.activation(out=SEG, in_=SEG, func=AF.Exp), reads=[rSEG], writes=[rSEG])
                    if c > 0:
                        S.op("dve", lambda e: e.tensor_mul(out=ce, in0=EC, in1=CT[:, tok].unsqueeze(1).to_broadcast([128, 4, 128])),
                             reads=[rEC, rCT], writes=[rce])
                    yield
                    S.op("dve", lambda e: e.tensor_mul(out=mt, in0=SEG, in1=PB[7][:, 0:128].unsqueeze(1).to_broadcast([128, 4, 128])),
                         reads=[rSEG, PR[7]], writes=[rmt])
                    yield

                def ssd_Q(c):
                    tok = slice(c * 128, (c + 1) * 128)
                    xd, rxd = XDTc[c % 2], rXDT[c % 2]
                    xe, rxe = XDEc[c % 2], rXDE[c % 2]
                    bc, rbc = Bc[c % 2], rBc[c % 2]
                    mt, rmt = MT2[c % 2], rMT2[c % 2]
                    ce, rce = CE2[c % 2], rCE2[c % 2]
                    for i in range(2):
                        b = 0 + i
                        for h2 in range(2):
                            h = 2 * i + h2
                            pr = slice(64 * h2, 64 * h2 + 64)
                            S.op("pe", lambda e, b=b, pr=pr, h=h: e.matmul(
                                PB[b][pr, 0:128], lhsT=xd[:, h, :], rhs=mt[:, h, :], start=True, stop=(c == 0)),
                                reads=[rxd, rmt], writes=[PR[b]], acc=(h2 > 0))
                            if c > 0:
                                S.op("pe", lambda e, b=b, pr=pr, h=h: e.matmul(
                                    PB[b][pr, 0:128], lhsT=STATEB[:, h * 64:(h + 1) * 64], rhs=ce[:, h, :], start=False, stop=True),
                                    reads=[rSTB, rce], writes=[PR[b]], acc=True)
                        yield
                        j = 2 * g + i
                        S.op("dve", lambda e, i=i, b=b, j=j: e.scalar_tensor_tensor(
                            out=XS[i][:, tok], in0=XS[i][:, tok], scalar=cp("ssd_d")[:, j:j + 1], in1=PB[b][:, 0:128],
                            op0=ALU.mult, op1=ALU.add), reads=[PR[b], rXS[i], rCP], writes=[rXS[i]])
                        yield
                        S.op("dve", lambda e, i=i: e.tensor_mul(out=XS[i][:, tok], in0=XS[i][:, tok], in1=ZS[i][:, tok]),
                             reads=[rXS[i], rZS[i]], writes=[rXS[i]])
                        yield
                    if c < 15:
                        S.op("pe", lambda e: e.matmul(PB[2][:, 0:256], lhsT=bc, rhs=xe.rearrange("p h d -> p (h d)"),
                                                      start=True, stop=True), reads=[rbc, rxe], writes=[PR[2]])
                        S.op("dve", lambda e: e.tensor_mul(
                            out=STATE, in0=STATE, in1=CDEC[:, c, hs4].unsqueeze(2).to_broadcast([128, 4, 64])),
                            reads=[rST, rDT], writes=[rST])
                        yield
                        S.op("dve", lambda e: e.tensor_add(out=STATE, in0=STATE, in1=PB[2][:, 0:256].rearrange("p (h d) -> p h d", d=64)),
                             reads=[rST, PR[2]], writes=[rST])
                        yield
                        S.op("act", lambda e: e.activation(out=STATEB, in_=STATE.rearrange("p h d -> p (h d)"), func=AF.Copy),
                             reads=[rST], writes=[rSTB])
                        yield

                run_gens([(ssd_P(0), 1)])
                for c in range(16):
                    gens = [(ssd_Q(c), 1)]
                    if c + 1 < 16:
                        gens.insert(0, (ssd_P(c + 1), 1))
                    run_gens(gens)
                for i in range(2):
                    j = 2 * g + i
                    S.op("act", lambda e, i=i: e.activation(out=CV[:, 4:4 + L], in_=XS[i], func=AF.Square), reads=[rXS[i], rCV], writes=[rCV])
                    for tt in range(4):
                        b = 4 + tt
                        S.op("pe", lambda e, tt=tt, b=b: e.matmul(PB[b], lhsT=ko("ones"), rhs=CV[:, 4 + tt * 512:4 + (tt + 1) * 512],
                                                                 start=True, stop=True), reads=[rKO, rCV], writes=[PR[b]])
                        S.op("dve", lambda e, tt=tt, b=b: e.tensor_add(out=SSQ[:, tt * 512:(tt + 1) * 512], in0=SSQ[:, tt * 512:(tt + 1) * 512],
                                                                      in1=PB[b]), reads=[PR[b], rSSQ], writes=[rSSQ])
                    S.op("dve", lambda e, i=i, j=j: e.tensor_scalar_mul(out=YT[:, j, :], in0=XS[i], scalar1=cp("ssd_nw")[:, j:j + 1]),
                         reads=[rXS[i], rCP], writes=[rYT[j]])
                S.op("dve", lambda e: e.memset(CV[:, 0:4], 0.0), reads=[rCV], writes=[rCV])
            S.op("dve", lambda e: e.tensor_scalar(out=SSQ, in0=SSQ, scalar1=1.0 / BW, scalar2=RMS_EPS, op0=ALU.mult, op1=ALU.add),
                 reads=[rSSQ], writes=[rSSQ])
            S.op("act", lambda e: e.activation(out=SSQ, in_=SSQ, func=AF.Sqrt), reads=[rSSQ], writes=[rSSQ])
            S.op("dve", lambda e: e.reciprocal(out=SSQ, in_=SSQ), reads=[rSSQ], writes=[rSSQ])
            if debug is not None and debug[0] == "ya" and l == debug[2]:
                S.barrier()
                DT_ = AR.view(O_T + 8192, [128, L], F32)
                for j in range(8):
                    S.op("dve", lambda e, j=j: e.tensor_mul(out=DT_, in0=YT[:, j, :], in1=SSQ), reads=[rYT[j], rSSQ], writes=[rT[0]])
                    S.dma("sp", dbg[j * 128:(j + 1) * 128, :], DT_, reads=[rT[0]], writes=[rDBG])
            S.barrier()
            merge_branch(0, rstd_row=SSQ, rstd_res=rSSQ)
            S.barrier()

        if 1 in branches:
            S.barrier()
            LP = L + 128
            COS = AR.view(O_T, [128, L], F32)
            SIN = AR.view(O_T + 8192, [128, L], F32)
            KT2 = AR.view(O_T + 16384, [128, 4, LP], BF16)
            VT = AR.view(O_T + 33792, [128, 17, 256], BF16)
            QF = AR.view(O_T + 42496, [128, L], F32)
            QR = AR.view(O_T + 50688, [128, L], F32)
            QI = AR.view(O_T + 50688, [128, L], I32)
            SMB = AR.view(O_T + 50688, [128, 8, 256], F32)
            WV = AR.view(O_T + 50688, [128, NKC, 256], BF16)
            QT = AR.view(O_T + 58880, [128, L], BF16)
            GS = AR.view(O_T + 62976, [128, L], BF16)
            PBF = AR.view(O_T + 67072, [128, 8, 256], BF16)
            PTS = AR.view(O_T + 71168, [128, 2048], BF16)
            IDB = AR.view(O_T + 75264, [128, 128], BF16)
            assert O_T + 75264 + 256 <= AR.nbytes
            rCOS, rSIN, rKT2, rVT, rQF, rQR, rQT, rGS, rIDB, rPBF, rPTS, rAT, rPBFb = (Res(n) for n in
                ("cos", "sin", "kt2", "vt", "qf", "qr", "qt", "gs", "idb", "pbf", "pts", "at", "pbfb"))
            AT = SM[:, 320:384]
            MX = AT[:, 0:8]
            RS8 = AT[:, 8:16]
            ES = AT[:, 16:24]
            NMX = AT[:, 24:32]
            S.op("dve", lambda e: e.tensor_copy(out=IDB, in_=ko("ident")), reads=[rKO], writes=[rIDB])
            for kh in range(4):
                S.op("dve", lambda e, kh=kh: e.memset(KT2[:, kh, 0:128], 0.0), writes=[rKT2], acc=(kh > 0))
            S.op("dve", lambda e: e.memset(VT[:, 0, :], 0.0), writes=[rVT])
            S.dma("sp", QI, pos_in.partition_broadcast(128), writes=[rQR])
            S.op("dve", lambda e: e.tensor_copy(out=QF, in_=QI), reads=[rQR], writes=[rQF])
            S.op("dve", lambda e: e.tensor_scalar_mul(out=QF, in0=QF, scalar1=ko("invf")), reads=[rQF, rKO], writes=[rQF])
            S.op("dve", lambda e: e.tensor_scalar_mul(out=COS, in0=QF, scalar1=float(1.0 / (2 * np.pi))), reads=[rQF], writes=[rCOS])
            S.op("dve", lambda e: e.tensor_copy(out=QI, in_=COS), reads=[rCOS], writes=[rQR])
            S.op("dve", lambda e: e.tensor_copy(out=COS, in_=QI), reads=[rQR], writes=[rCOS])
            S.op("dve", lambda e: e.scalar_tensor_tensor(out=SIN, in0=COS, scalar=-6.28125, in1=QF, op0=ALU.mult, op1=ALU.add),
                 reads=[rCOS, rQF], writes=[rSIN])
            S.op("dve", lambda e: e.scalar_tensor_tensor(out=SIN, in0=COS, scalar=-0.0019353071795864769, in1=SIN,
                                                         op0=ALU.mult, op1=ALU.add), reads=[rCOS, rSIN], writes=[rSIN])
            S.op("dve", lambda e: e.tensor_scalar(out=SIN, in0=SIN, scalar1=-3.141592, scalar2=3.141592, op0=ALU.max, op1=ALU.min),
                 reads=[rSIN], writes=[rSIN])
            S.op("dve", lambda e: e.tensor_scalar_mul(out=QF, in0=SIN, scalar1=-1.0), reads=[rSIN], writes=[rQF])
            S.op("dve", lambda e: e.tensor_max(out=QF, in0=QF, in1=SIN), reads=[rSIN, rQF], writes=[rQF])
            S.op("act", lambda e: e.activation(out=COS, in_=QF, func=AF.Sin, scale=-1.0, bias=ko("halfpi")),
                 reads=[rQF, rKO], writes=[rCOS])
            S.op("act", lambda e: e.activation(out=SIN, in_=SIN, func=AF.Sin), reads=[rSIN], writes=[rSIN])

            def rope_chunk(dst, dres, qscale):
                for tt in range(4):
                    b = 4 + tt
                    S.op("pe", lambda e, tt=tt, b=b: e.matmul(PB[b], lhsT=ko("rot"), rhs=QF[:, tt * 512:(tt + 1) * 512],
                                                             start=True, stop=True), reads=[rKO, rQF], writes=[PR[b]])
                    S.op("dve", lambda e, tt=tt, b=b: e.scalar_tensor_tensor(
                        out=QR[:, tt * 512:(tt + 1) * 512], in0=PB[b], scalar=qscale, in1=SIN[:, tt * 512:(tt + 1) * 512],
                        op0=ALU.mult, op1=ALU.mult), reads=[PR[b], rSIN], writes=[rQR], acc=(tt > 0))
                S.op("dve", lambda e: e.scalar_tensor_tensor(out=QF, in0=QF, scalar=qscale, in1=COS, op0=ALU.mult, op1=ALU.mult),
                     reads=[rQF, rCOS], writes=[rQF])
                S.op("dve", lambda e: e.tensor_add(out=dst, in0=QF, in1=QR), reads=[rQF, rQR], writes=[dres])

            S.dma("pool", WV, w_in[l, :, B_V:B_V + 256].rearrange("(kc p) c -> p kc c", p=128), reads=[rQR], writes=[rQR])
            for n in range(16):
                b = n % 4
                for kc in range(NKC):
                    S.op("pe", lambda e, kc=kc, n=n, b=b: e.matmul(PB[b][:, 0:256], lhsT=HT[:, kc, n * 128:(n + 1) * 128],
                                                                  rhs=WV[:, kc, :], start=(kc == 0), stop=(kc == NKC - 1)),
                         reads=[rQR, rHT[n // 4]], writes=[PR[b]], acc=(kc > 0))
                S.op("act", lambda e, n=n, b=b: e.activation(out=VT[:, n + 1, :], in_=PB[b][:, 0:256], func=AF.Copy),
                     reads=[PR[b]], writes=[rVT], acc=True)
            for kh in range(4):
                def wl(wt, wres, kh=kh):
                    src = w_in[l, :, B_K + kh * 64:B_K + (kh + 1) * 64].rearrange("(kc p) c -> p kc c", p=128)
                    S.dma("pool", wt[:, :, 0:64], src, writes=[wres])
                    keep_w = dict(wres.w)
                    sem2, val2 = S.dma("pool", wt[:, :, 64:128], src, reads=[wres], writes=[])
                    wres.r = {}
                    wres.w = keep_w
                    wres.w[sem2.name] = (sem2, val2)
                proj_fm(l, 0, 128, evac_copy(QF, 0, rQF), wload=wl)
                rope_chunk(KT2[:, kh, 128:LP], rKT2, 1.0)
            PSG = PS[:, 0:2048].rearrange("p (i s) -> p i s", s=256)
            PTP = PS[:, 2048:3072].bitcast(BF16)
            rPSG = Res("psg")
            rPTP = Res("ptp")

            def s_matmuls(jq, ng):
                kh = jq // 2
                for nbi in range(4):
                    n = ng * 4 + nbi
                    for h2 in range(2):
                        i = h2 * 4 + nbi
                        pr = slice(64 * h2, 64 * h2 + 64)
                        if ATTN_STOP == 2.6:
                            S.op("pe", lambda e, pr=pr, n=n, i=i, kh=kh: e.matmul(
                                PB[i % 4][:, 0:256], lhsT=QT[pr, n * 128:(n + 1) * 128], rhs=KT2[pr, kh, n * 128:n * 128 + 256],
                                start=True, stop=True), reads=[rQT, rKT2], writes=[PR[i % 4]])
                            continue
                        S.op("pe", lambda e, pr=pr, n=n, i=i, kh=kh: e.matmul(
                            PSG[:, i, :], lhsT=QT[pr, n * 128:(n + 1) * 128], rhs=KT2[pr, kh, n * 128:n * 128 + 256],
                            start=True, stop=True), reads=[rQT, rKT2], writes=[PR[0], PR[1], PR[2], PR[3]], acc=(nbi > 0 or h2 > 0))

            QTB = [QT, AR.view(O_WB, [128, L], BF16)]
            GSB_ = [GS, AR.view(O_W + 3 * NKC * 128 * 2, [128, L], BF16)]
            rQTB = [rQT, Res("qt2")]
            rGSB_ = [rGS, Res("gs2")]

            def att_pre(jq):
                qt, rqt = QTB[jq % 2], rQTB[jq % 2]
                gs, rgs = GSB_[jq % 2], rGSB_[jq % 2]
                yield from proj_fm_g(l, B_Q + jq * 128, 128, evac_copy(QF, 0, rQF), banks=(7,))
                for tt in range(4):
                    cs = slice(tt * 512, (tt + 1) * 512)
                    S.op("pe", lambda e, cs=cs: e.matmul(PB[7], lhsT=ko("rot"), rhs=QF[:, cs], start=True, stop=True),
                         reads=[rKO, rQF], writes=[PR[7]])
                    yield
                    S.op("dve", lambda e, cs=cs: e.scalar_tensor_tensor(out=QF[:, cs], in0=QF[:, cs], scalar=0.125, in1=COS[:, cs],
                                                                        op0=ALU.mult, op1=ALU.mult), reads=[rQF, rCOS], writes=[rQF])
                    yield
                    S.op("dve", lambda e, cs=cs: e.tensor_mul(out=PB[7], in0=PB[7], in1=SIN[:, cs]), reads=[PR[7], rSIN], writes=[PR[7]])
                    yield
                    S.op("dve", lambda e, cs=cs: e.scalar_tensor_tensor(out=qt[:, cs], in0=PB[7], scalar=0.125, in1=QF[:, cs],
                                                                        op0=ALU.mult, op1=ALU.add),
                         reads=[PR[7], rQF], writes=[rqt], acc=(tt > 0))
                    yield
                yield from proj_fm_g(l, B_G + jq * 128, 128, evac_act(gs, 0, rgs, AF.Silu), banks=(7,))

            def att_core(jq):
                kh = jq // 2
                QT, rQT = QTB[jq % 2], rQTB[jq % 2]
                GS, rGS = GSB_[jq % 2], rGSB_[jq % 2]
                sinkb = cp("sinks")[:, 2 * jq:2 * jq + 2].unsqueeze(2).to_broadcast([128, 2, 4])
                v42 = lambda t: t.rearrange("p (a b) -> p a b", b=4)

                def s_matmuls(jq, ng):
                    for nbi in range(4):
                        n = ng * 4 + nbi
                        for h2 in range(2):
                            i = h2 * 4 + nbi
                            pr = slice(64 * h2, 64 * h2 + 64)
                            S.op("pe", lambda e, pr=pr, n=n, i=i: e.matmul(
                                PSG[:, i, :], lhsT=QT[pr, n * 128:(n + 1) * 128], rhs=KT2[pr, kh, n * 128:n * 128 + 256],
                                start=True, stop=True), reads=[rQT, rKT2], writes=[PR[0], PR[1], PR[2], PR[3]], acc=(nbi > 0 or h2 > 0))

                PBF2 = [PBF, AR.view(O_T + 75520, [128, 8, 256], BF16)]
                rPBF2 = [rPBF, rPBFb]

                def att_H1(ng, jq=jq, sinkb=sinkb, v42=v42):
                    pbf, rpbf = PBF2[ng % 2], rPBF2[ng % 2]
                    for bk in range(4):
                        S.op("dve", lambda e, bk=bk: e.tensor_add(out=SMB[:, 2 * bk:2 * bk + 2, :], in0=PSG[:, 2 * bk:2 * bk + 2, :],
                                                                 in1=ko("amask").unsqueeze(1).to_broadcast([128, 2, 256])),
                             reads=[PR[bk], rKO], writes=[rQR], acc=(bk > 0))
                        if bk % 2 == 1:
                            yield
                    if ng < 3:
                        s_matmuls(jq, ng + 1)
                    yield
                    if ng == 0:
                        for i0 in (0, 4):
                            S.op("dve", lambda e, i0=i0: e.tensor_scalar_add(out=SMB[:, i0, 0:128], in0=SMB[:, i0, 0:128], scalar1=-30000.0),
                                 reads=[rQR], writes=[rQR])
                    S.op("dve", lambda e: e.reduce_max(out=MX, in_=SMB, axis=AX.X), reads=[rQR], writes=[rAT])
                    yield
                    S.op("dve", lambda e: e.tensor_max(out=v42(MX), in0=v42(MX), in1=sinkb), reads=[rAT, rCP], writes=[rAT])
                    S.op("dve", lambda e: e.tensor_scalar_mul(out=NMX, in0=MX, scalar1=-1.0), reads=[rAT], writes=[rAT])
                    yield
                    for i in range(8):
                        S.op("act", lambda e, i=i: e.activation(out=SMB[:, i, :], in_=SMB[:, i, :], func=AF.Exp, bias=NMX[:, i:i + 1],
                                                               accum_out=RS8[:, i:i + 1]), reads=[rQR, rAT], writes=[rQR, rAT], acc=(i > 0))
                        if i % 2 == 1:
                            yield
                    S.op("dve", lambda e: e.tensor_sub(out=v42(ES), in0=sinkb, in1=v42(MX)), reads=[rAT, rCP], writes=[rAT])
                    S.op("act", lambda e: e.activation(out=ES, in_=ES, func=AF.Exp), reads=[rAT], writes=[rAT])
                    yield
                    S.op("dve", lambda e: e.tensor_add(out=RS8, in0=RS8, in1=ES), reads=[rAT], writes=[rAT])
                    S.op("dve", lambda e: e.reciprocal(out=RS8, in_=RS8), reads=[rAT], writes=[rAT])
                    yield
                    S.op("dve", lambda e: e.tensor_mul(out=pbf, in0=SMB, in1=RS8.unsqueeze(2).to_broadcast([128, 8, 256])),
                         reads=[rQR, rAT], writes=[rpbf])
                    yield

                def att_H2(ng, jq=jq, kh=kh):
                    pbf, rpbf = PBF2[ng % 2], rPBF2[ng % 2]
                    bo = 6
                    for i in range(8):
                        for blk in range(2):
                            S.op("pe", lambda e, i=i, blk=blk: e.transpose(
                                out=PTP[:, (i * 2 + blk) * 128:(i * 2 + blk + 1) * 128], in_=pbf[:, i, blk * 128:(blk + 1) * 128],
                                identity=IDB), reads=[rpbf, rIDB], writes=[PR[4], PR[5]], acc=(i > 0 or blk > 0))
                        if i % 2 == 1:
                            yield
                    for bk in range(2):
                        S.op("act", lambda e, bk=bk: e.activation(out=PTS[:, bk * 1024:(bk + 1) * 1024], in_=PTP[:, bk * 1024:(bk + 1) * 1024],
                                                                 func=AF.Copy), reads=[PR[4 + bk]], writes=[rPTS], acc=(bk > 0))
                    yield
                    for nbi in range(4):
                        n = ng * 4 + nbi
                        for h2 in range(2):
                            i = h2 * 4 + nbi
                            pr = slice(64 * h2, 64 * h2 + 64)
                            for blk in range(2):
                                S.op("pe", lambda e, i=i, blk=blk, pr=pr, n=n, nbi=nbi: e.matmul(
                                    PB[bo][pr, nbi * 128:(nbi + 1) * 128], lhsT=VT[:, n + blk, kh * 64:(kh + 1) * 64],
                                    rhs=PTS[:, (i * 2 + blk) * 128:(i * 2 + blk + 1) * 128], start=(blk == 0), stop=(blk == 1)),
                                    reads=[rVT, rPTS], writes=[PR[bo]], acc=(nbi > 0 or h2 > 0 or blk > 0))
                        yield
                    S.op("dve", lambda e: e.tensor_mul(out=YT[:, jq, ng * 512:(ng + 1) * 512], in0=PB[bo],
                                                       in1=GS[:, ng * 512:(ng + 1) * 512]),
                         reads=[PR[bo], rGS], writes=[rYT[jq]], acc=True)
                    yield

                s_matmuls(jq, 0)
                yield from att_H1(0)
                for ng in range(4):
                    gens = [(att_H2(ng), 1)]
                    if ng < 3:
                        gens.insert(0, (att_H1(ng + 1), 1))
                    yield from rr_gen(gens)

            run_gens([(att_pre(0), 1)])
            for jq in range(8):
                gens = [(att_core(jq), 3)]
                if jq + 1 < 8:
                    gens.append((att_pre(jq + 1), 1))
                run_gens(gens)
            if debug is not None and debug[0] == "yb" and l == debug[2]:
                S.barrier()
                DT_ = AR.view(O_T, [128, L], F32)
                for j in range(8):
                    S.op("dve", lambda e, j=j: e.tensor_copy(out=DT_, in_=YT[:, j, :]), reads=[rYT[j]], writes=[rT[0]])
                    S.dma("sp", dbg[j * 128:(j + 1) * 128, :], DT_, reads=[rT[0]], writes=[rDBG])
            S.barrier()
            merge_branch(1)
            S.barrier()

        if 2 in branches:
            S.op("dve", lambda e: e.memset(TB[:, 0:4], 0.0), writes=[rTB])
            for j in range(8):
                proj_fm(l, C_C + j * 128, 128, evac_copy(TA, 0, rTA))
                proj_fm(l, C_X + j * 128, 128, evac_mul(TB, 2, TA, 0, rTB) if False else
                        (lambda tt, ps, pres: S.op("dve", lambda e: e.tensor_mul(
                            out=TB[:, 2 + tt * 512: 2 + (tt + 1) * 512], in0=ps, in1=TA[:, tt * 512:(tt + 1) * 512]),
                            reads=[pres, rTA], writes=[rTB], acc=(tt > 0))))
                wv = cp("sconv_w")
                S.op("dve", lambda e, j=j: e.tensor_scalar_mul(out=TA[:, 0:L], in0=TB[:, 0:L], scalar1=wv[:, j * 3:j * 3 + 1]),
                     reads=[rTB, rCP], writes=[rTA])
                for kk in (1, 2):
                    S.op("dve", lambda e, j=j, kk=kk: e.scalar_tensor_tensor(
                        out=TA[:, 0:L], in0=TB[:, kk:kk + L], scalar=wv[:, j * 3 + kk:j * 3 + kk + 1], in1=TA[:, 0:L],
                        op0=ALU.mult, op1=ALU.add), reads=[rTB, rTA, rCP], writes=[rTA])
                proj_fm(l, C_B + j * 128, 128, lambda tt, ps, pres: S.op("dve", lambda e: e.tensor_mul(
                    out=TA[:, tt * 512:(tt + 1) * 512], in0=ps, in1=TA[:, tt * 512:(tt + 1) * 512]),
                    reads=[pres, rTA], writes=[rTA], acc=(tt > 0)))
                proj_fm(l, C_G + j * 128, 128, evac_act(TC, 0, rTC, AF.Silu))
                S.op("dve", lambda e, j=j: e.tensor_mul(out=YT[:, j, :], in0=TA[:, 0:L], in1=TC[:, 0:L]),
                     reads=[rTA, rTC], writes=[rYT[j]])
            if debug is not None and debug[0] == "yc" and l == debug[2]:
                for j in range(8):
                    S.op("dve", lambda e, j=j: e.tensor_copy(out=TA[:, 0:L], in_=YT[:, j, :]), reads=[rYT[j]], writes=[rTA])
                    S.dma("sp", dbg[j * 128:(j + 1) * 128, :], TA[:, 0:L], reads=[rTA], writes=[rDBG])
            merge_branch(2)

        if 3 in branches:
            S.op("dve", lambda e: e.tensor_scalar_mul(out=SMT[:, 0:8], in0=cp("lru_lam"), scalar1=-1.0), reads=[rCP], writes=[rSM])
            softplus_small(C8, SMT[:, 0:8], SMT[:, 8:32], 8, [rSM], [rSM])
            S.op("dve", lambda e: e.tensor_scalar_mul(out=C8, in0=C8, scalar1=-8.0), reads=[rSM], writes=[rSM])
            S.op("dve", lambda e: e.memset(TB[:, 0:4], 0.0), writes=[rTB])
            LWT = [AR.view(MOFF + 6 * 2048 + i * 1024, [128, 2, 128], F32) for i in range(2)]
            rLWT = [Res("lwt0"), Res("lwt1")]
            for j in range(8):
                lwt, lwr = LWT[j % 2], rLWT[j % 2]
                S.dma("sp", lwt, lruw_in[l, :, j].rearrange("g p c -> p g c"), writes=[lwr])
                proj_fm(l, D_X + j * 128, 128, evac_copy(TB, 3, rTB))
                cw = cp("lru_cw")
                S.op("dve", lambda e, j=j: e.tensor_scalar(out=TA[:, 0:L], in0=TB[:, 0:L], scalar1=cw[:, j * 4:j * 4 + 1],
                                                          scalar2=cp("lru_cb")[:, j:j + 1], op0=ALU.mult, op1=ALU.add),
                     reads=[rTB, rCP], writes=[rTA])
                for kk in (1, 2, 3):
                    S.op("dve", lambda e, j=j, kk=kk: e.scalar_tensor_tensor(
                        out=TA[:, 0:L], in0=TB[:, kk:kk + L], scalar=cw[:, j * 4 + kk:j * 4 + kk + 1], in1=TA[:, 0:L],
                        op0=ALU.mult, op1=ALU.add), reads=[rTB, rTA, rCP], writes=[rTA])
                for gi, (bname, dstT, dres) in enumerate((("lru_ba", TB, rTB), ("lru_bx", TC, rTC))):
                    for tt in range(4):
                        b = 4 + (gi * 4 + tt) % 4
                        S.op("pe", lambda e, tt=tt, b=b, gi=gi, lwt=lwt: e.matmul(
                            PB[b], lhsT=lwt[:, gi, :], rhs=TA[:, tt * 512:(tt + 1) * 512],
                            start=True, stop=True), reads=[lwr, rTA], writes=[PR[b]])
                        S.op("act", lambda e, tt=tt, b=b, bname=bname, dstT=dstT, j=j: e.activation(
                            out=dstT[:, tt * 512:(tt + 1) * 512], in_=PB[b], func=AF.Sigmoid, bias=cp(bname)[:, j:j + 1]),
                            reads=[PR[b], rCP], writes=[dres], acc=(tt > 0))
                S.op("act", lambda e, j=j: e.activation(out=TB[:, 0:L], in_=TB[:, 0:L], func=AF.Exp, scale=C8[:, j:j + 1]),
                     reads=[rTB, rSM], writes=[rTB])
                S.op("dve", lambda e: e.tensor_mul(out=TA[:, 0:L], in0=TA[:, 0:L], in1=TC[:, 0:L]), reads=[rTA, rTC], writes=[rTA])
                S.op("act", lambda e: e.activation(out=TC[:, 0:L], in_=TB[:, 0:L], func=AF.Square), reads=[rTB, rTC], writes=[rTC])
                S.op("act", lambda e: e.activation(out=TC[:, 0:L], in_=TC[:, 0:L], func=AF.Sqrt, scale=-1.0, bias=1.0),
                     reads=[rTC], writes=[rTC])
                S.op("dve", lambda e: e.tensor_mul(out=TA[:, 0:L], in0=TA[:, 0:L], in1=TC[:, 0:L]), reads=[rTA, rTC], writes=[rTA])
                S.op("dve", lambda e: e.tensor_tensor_scan(out=TC[:, 0:L], data0=TB[:, 0:L], data1=TA[:, 0:L], initial=0.0,
                                                          op0=ALU.mult, op1=ALU.add), reads=[rTA, rTB], writes=[rTC])
                proj_fm(l, D_G + j * 128, 128, evac_act(TA, 0, rTA, AF.Silu))
                S.op("dve", lambda e, j=j: e.tensor_mul(out=YT[:, j, :], in0=TA[:, 0:L], in1=TC[:, 0:L]),
                     reads=[rTA, rTC], writes=[rYT[j]])
                S.op("dve", lambda e: e.memset(TB[:, 0:4], 0.0), reads=[rTB], writes=[rTB])
            if debug is not None and debug[0] == "yd" and l == debug[2]:
                for j in range(8):
                    S.op("dve", lambda e, j=j: e.tensor_copy(out=TA[:, 0:L], in_=YT[:, j, :]), reads=[rYT[j]], writes=[rTA])
                    S.dma("sp", dbg[j * 128:(j + 1) * 128, :], TA[:, 0:L], reads=[rTA], writes=[rDBG])
            merge_branch(3)

        if first_branch[0]:
            ZT = AR.view(O_T, [128, L], F32)
            S.op("dve", lambda e: e.memset(ZT, 0.0), writes=[rT[0]])
            for dc in range(NKC):
                S.dma("sp", mscr[dc], ZT, reads=[rT[0]], writes=[rMT[dc]])
        S.barrier()

        WO = AR.view(O_HT, [128, NKC, D], BF16)
        rWOk = [Res("wo%d" % kc) for kc in range(NKC)]
        for kc in range(NKC):
            S.dma("pool", WO[:, kc, :], w_out[l, kc * 128:(kc + 1) * 128, :], writes=[rWOk[kc]])
        GR = AR.view(O_Y, [128, D], F32)
        LW = AR.view(O_Y + 8192, [128, D], F32)
        LB = AR.view(O_Y + 16384, [128, D], F32)
        rGR = Res("gr")
        S.dma("sp", GR, gscr, reads=[rG], writes=[rGR])
        S.dma("sp", LW, ln_wb[l, 0].partition_broadcast(128), writes=[rGR])
        S.dma("sp", LB, ln_wb[l, 1].partition_broadcast(128), writes=[rGR])
        for kc in range(NKC):
            S.op("dve", lambda e, kc=kc: e.tensor_mul(out=WO[:, kc, :], in0=WO[:, kc, :], in1=GR), reads=[rWOk[kc], rGR], writes=[rWOk[kc]])
        XT = [AR.view(O_T + i * 8192, [128, D], F32) for i in range(2)]
        RSB = [AR.view(O_T + (2 + i) * 8192, [128, D], F32) for i in range(2)]
        MTT = [AR.view(O_T + 4 * 8192 + i * 4096, [128, NKC, 128], BF16) for i in range(2)]
        rMTT = [Res("mtt0"), Res("mtt1")]
        rXT = [rT[0], rT[1]]
        rRSB = [rT[2], rT[3]]
        ST = SM[:, 256:320]
        rST4 = [Res("st4a"), Res("st4b")]
        def p4_mm(t16):
            xt, xr = XT[t16 % 2], rXT[t16 % 2]
            RS, rRS = RSB[t16 % 2], rRSB[t16 % 2]
            MT1, rMT1 = MTT[t16 % 2], rMTT[t16 % 2]
            S.dma("sp", xt, xin[t16 * 128:(t16 + 1) * 128, :], reads=[rX1] if l > 0 else [], writes=[xr])
            S.dma("pool", MT1, mscr[:, :, t16 * 128:(t16 + 1) * 128].rearrange("dc p t -> p dc t"), reads=rMT, writes=[rMT1])
            for nb in range(4):
                b = (t16 * 4 + nb) % 8
                for kc in range(NKC):
                    S.op("pe", lambda e, kc=kc, nb=nb, b=b, MT1=MT1: e.matmul(
                        PB[b], lhsT=MT1[:, kc, :], rhs=WO[:, kc, nb * 512:(nb + 1) * 512],
                        start=(kc == 0), stop=(kc == NKC - 1)), reads=[rMT1, rWOk[kc]], writes=[PR[b]], acc=(kc > 0))
                S.op("dve", lambda e, nb=nb, b=b, RS=RS, xt=xt: e.scalar_tensor_tensor(
                    out=RS[:, nb * 512:(nb + 1) * 512], in0=xt[:, nb * 512:(nb + 1) * 512], scalar=ALPHA, in1=PB[b],
                    op0=ALU.mult, op1=ALU.add), reads=[PR[b], xr], writes=[rRS], acc=(nb > 0))
                yield

        def p4_ln(t16):
            xt, xr = XT[t16 % 2], rXT[t16 % 2]
            RS, rRS = RSB[t16 % 2], rRSB[t16 % 2]
            rst = rST4[t16 % 2]
            c0 = (t16 % 2) * 8
            mean, nb_, ssq, rstd, msq = (ST[:, c0 + i:c0 + i + 1] for i in range(5))
            S.op("act", lambda e: e.activation(out=xt, in_=RS, func=AF.Copy, accum_out=mean), reads=[rRS], writes=[xr, rst])
            yield
            S.op("act", lambda e: e.activation(out=xt, in_=RS, func=AF.Square, accum_out=ssq), reads=[rRS], writes=[xr, rst])
            yield
            S.op("dve", lambda e: e.tensor_scalar_mul(out=mean, in0=mean, scalar1=1.0 / D), reads=[rst], writes=[rst])
            S.op("dve", lambda e: e.tensor_mul(out=msq, in0=mean, in1=mean), reads=[rst], writes=[rst])
            yield
            S.op("dve", lambda e: e.scalar_tensor_tensor(out=rstd, in0=ssq, scalar=1.0 / D, in1=msq, op0=ALU.mult, op1=ALU.subtract),
                 reads=[rst], writes=[rst])
            S.op("dve", lambda e: e.tensor_scalar_add(out=rstd, in0=rstd, scalar1=LN_EPS), reads=[rst], writes=[rst])
            S.op("act", lambda e: e.activation(out=rstd, in_=rstd, func=AF.Sqrt), reads=[rst], writes=[rst])
            yield
            S.op("dve", lambda e: e.reciprocal(out=rstd, in_=rstd), reads=[rst], writes=[rst])
            S.op("dve", lambda e: e.scalar_tensor_tensor(out=nb_, in0=mean, scalar=-1.0, in1=rstd, op0=ALU.mult, op1=ALU.mult),
                 reads=[rst], writes=[rst])
            S.op("act", lambda e: e.activation(out=xt, in_=RS, func=AF.Identity, scale=rstd, bias=nb_), reads=[rRS, rst], writes=[xr])
            yield
            S.op("dve", lambda e: e.tensor_mul(out=xt, in0=xt, in1=LW), reads=[xr, rGR], writes=[xr])
            yield
            S.op("dve", lambda e: e.tensor_add(out=xt, in0=xt, in1=LB), reads=[xr, rGR], writes=[xr])
            S.dma("sp", xout[t16 * 128:(t16 + 1) * 128, :], xt, reads=[xr], writes=[rX1 if xout is x1 else rOUT])
            yield

        run_gens([(p4_mm(0), 1)])
        for t16 in range(16):
            gens = [(p4_ln(t16), 2)]
            if t16 + 1 < 16:
                gens.insert(0, (p4_mm(t16 + 1), 1))
            run_gens(gens)
        S.barrier()

    S.finish("sp")
    return nc, S


def _fm(v, chunks):
    v = np.asarray(v)
    return np.ascontiguousarray(np.moveaxis(v.reshape((chunks, 128) + v.shape[1:]), 0, 1))


def _pack_cp(inp, l):
    cp = np.zeros((128, NCP), np.float32)

    def put(name, arr):
        a, w = CP[name]
        cp[:, a:a + w] = np.asarray(arr, np.float32).reshape(128, w)
    put("ssd_cw", _fm(np.asarray(inp["ssd_conv_w"][l]).T, 16))
    put("ssd_cb", _fm(inp["ssd_conv_b"][l], 16))
    put("ssd_nw", _fm(inp["ssd_norm_w"][l], 8))
    put("ssd_d", _fm(np.repeat(np.asarray(inp["ssd_d"][l]), 64), 8))
    put("sconv_w", _fm(np.asarray(inp["sconv_w"][l]).T, 8))
    put("lru_cw", _fm(np.asarray(inp["lru_conv_w"][l]).T, 8))
    put("lru_cb", _fm(inp["lru_conv_b"][l], 8))
    put("lru_ba", _fm(inp["lru_b_a"][l], 8))
    put("lru_bx", _fm(inp["lru_b_x"][l], 8))
    put("lru_lam", _fm(inp["lru_lambda"][l], 8))
    put("b_gate", np.stack([_fm(np.asarray(inp["b_gate"][l][k]), 16) for k in range(4)], axis=1))
    put("sinks", np.broadcast_to(np.asarray(inp["attn_sinks"][l])[None, :], (128, 16)))
    put("dt_bias", np.broadcast_to(np.asarray(inp["ssd_dt_bias"][l])[None, :], (128, 16)))
    put("a_log", np.broadcast_to(np.asarray(inp["ssd_a_log"][l])[None, :], (128, 16)))
    return cp


def _pack_lruw(inp):
    out = np.zeros((DEPTH, 2, 8, 128, 128), np.float32)
    for l in range(DEPTH):
        for gi, key in enumerate(("lru_w_a", "lru_w_x")):
            w = np.asarray(inp[key][l])
            for j in range(8):
                out[l, gi, j, 0:64, 0:64] = w[2 * j]
                out[l, gi, j, 64:128, 64:128] = w[2 * j + 1]
    return out


def _konst():
    k = np.zeros((128, NKO), np.float32)

    def put(name, arr):
        a, w = KO[name]
        k[:, a:a + w] = arr
    put("ident", np.eye(128, dtype=np.float32))
    put("triu", np.triu(np.ones((128, 128), np.float32)))
    put("ones", np.ones((128, 128), np.float32))
    rot = np.zeros((128, 128), np.float32)
    for p in range(128):
        if p % 64 < 32:
            rot[p + 32, p] = -1.0
        else:
            rot[p - 32, p] = 1.0
    put("rot", rot)
    half = 32
    invf = (10000.0 ** (-np.arange(half, dtype=np.float32) / half)).astype(np.float32)
    put("invf", np.tile(invf, 4)[:, None])
    qi = np.arange(128)[:, None]
    sj = np.arange(256)[None, :]
    rel = qi + 128 - sj
    band = (rel >= 0) & (rel < 128)
    put("amask", np.where(band, 0.0, -30000.0).astype(np.float32))
    put("amask0", np.where(band & (sj >= 128), 0.0, -30000.0).astype(np.float32))
    put("halfpi", np.full((128, 1), np.pi / 2, np.float32))
    s_ = np.arange(128)[:, None]
    l_ = np.arange(128)[None, :]
    put("smask", np.where(l_ >= s_, 0.0, -30000.0).astype(np.float32))
    return k


_CACHE = {}


def make_in_maps(inp, cores):
    f = lambda k: np.ascontiguousarray(np.asarray(inp[k], dtype=np.float32))
    shared = {
        "w_ada": f("w_ada"), "b_ada": f("b_ada").reshape(DEPTH, 1, 3 * D), "w_in": f("w_in"),
        "w_branch": f("w_branch"), "w_out": f("w_out"),
        "ln_wb": np.ascontiguousarray(np.stack([f("ln_w"), f("ln_b")], axis=1).reshape(DEPTH, 2, 1, D)),
        "cp": np.stack([_pack_cp(inp, l) for l in range(DEPTH)], axis=0),
        "konst": _konst(), "lruw": _pack_lruw(inp),
    }
    x = np.asarray(inp["x"], dtype=np.float32)
    c = np.asarray(inp["c"], dtype=np.float32)
    pos = np.asarray(inp["positions"]).astype(np.int32)
    maps = []
    for b in cores:
        m = dict(shared)
        m["x"] = np.ascontiguousarray(x[b])
        m["c"] = np.ascontiguousarray(c[b].reshape(NKC, 128).T)
        m["pos"] = np.ascontiguousarray(pos[b].reshape(1, L))
        maps.append(m)
    return maps


def kernel(**inputs):
    if "nc" not in _CACHE:
        _CACHE["nc"] = build_program()[0]
    nc = _CACHE["nc"]
    maps = make_in_maps(inputs, list(range(8)))
    res = run_bass_kernel_spmd(nc, maps, core_ids=list(range(8)))
    return np.stack([np.asarray(r["y"], dtype=np.float32) for r in res.results], axis=0)
```

```python
import numpy as np
import concourse.bass as bass
import concourse.mybir as mybir
from concourse.bass_utils import run_bass_kernel_spmd

F32 = mybir.dt.float32
BF16 = mybir.dt.bfloat16
I32 = mybir.dt.int32
U8 = mybir.dt.uint8
AF = mybir.ActivationFunctionType
ALU = mybir.AluOpType
AX = mybir.AxisListType

D = 2048
L = 2048
DEPTH = 2
NKC = 16
BW = 1024
IN_COLS = 19984
A_Z, A_X, A_B, A_C, A_DT = 0, 1024, 2048, 2560, 3072
B_Q, B_K, B_V, B_G = 3088, 4112, 4368, 4624
C_B, C_C, C_X, C_G = 5648, 6672, 7696, 8720
D_X, D_G = 9744, 10768
MERGE = 11792
ALPHA = (2.0 * DEPTH) ** 0.25
LN_EPS = 1e-5
RMS_EPS = 1e-5

CP = {}
_o = 0
for _n, _w in [("ssd_cw", 64), ("ssd_cb", 16), ("ssd_nw", 8), ("ssd_d", 8), ("sconv_w", 24), ("lru_cw", 32),
               ("lru_cb", 8), ("lru_ba", 8), ("lru_bx", 8), ("lru_lam", 8), ("b_gate", 64), ("sinks", 16),
               ("dt_bias", 16), ("a_log", 16)]:
    CP[_n] = (_o, _w)
    _o += _w
NCP = _o
KO = {}
_o = 0
for _n, _w in [("ident", 128), ("triu", 128), ("ones", 128), ("rot", 128), ("invf", 1), ("amask", 256),
               ("smask", 128), ("amask0", 256), ("halfpi", 1)]:
    KO[_n] = (_o, _w)
    _o += _w
NKO = _o


class Res:
    __slots__ = ("name", "w", "r", "excl")

    def __init__(self, name, excl=False):
        self.name = name
        self.w = {}
        self.r = {}
        self.excl = excl


class Sched:
    NDS = 12

    def __init__(self, nc):
        self.nc = nc
        self.eng = {"pe": nc.tensor, "act": nc.scalar, "dve": nc.vector, "pool": nc.gpsimd, "sp": nc.sync}
        self.sem = {k: nc.alloc_semaphore("s_" + k) for k in self.eng}
        self.cnt = {k: 0 for k in self.eng}
        self.seen = {k: {} for k in self.eng}
        self.dsems = {q: [[nc.alloc_semaphore("d_%s_%d" % (q, i)), 0] for i in range(self.NDS)]
                      for q in ("sp", "pool", "act")}
        self.dptr = {q: 0 for q in self.dsems}
        self.all_dma = {}
        self.ninst = 0

    SAME_ENGINE_WAITS = ("pool", "pe", "sp", "act", "dve")

    def _wait(self, e, deps):
        need = {}
        own = self.sem[e].name
        for d in deps:
            for nm, (sem, val) in d.items():
                if nm == own and e not in self.SAME_ENGINE_WAITS:
                    continue
                if self.seen[e].get(nm, 0) < val and need.get(nm, (None, 0))[1] < val:
                    need[nm] = (sem, val)
        for nm, (sem, val) in need.items():
            self.eng[e].wait_ge(sem, val)
            self.seen[e][nm] = val

    def _deps(self, reads, writes, acc):
        deps = []
        for r in reads:
            deps.append(r.w)
            if r.excl:
                deps.append(r.r)
        if not acc:
            for w in writes:
                deps.append(w.w)
                deps.append(w.r)
        return deps

    def _commit(self, reads, writes, sem, val):
        nm = sem.name
        for w in writes:
            w.w = {nm: (sem, val)}
            w.r = {}
        for r in reads:
            r.r[nm] = (sem, val)

    def op(self, e, fn, reads=(), writes=(), acc=False):
        self._wait(e, self._deps(reads, writes, acc))
        ins = fn(self.eng[e])
        self.cnt[e] += 1
        ins.then_inc(self.sem[e], 1)
        self.seen[e][self.sem[e].name] = max(self.seen[e].get(self.sem[e].name, 0), 0)
        self._commit(reads, writes, self.sem[e], self.cnt[e])
        self.ninst += 1

    def dma(self, q, out, in_, reads=(), writes=(), **kw):
        slot = self.dsems[q][self.dptr[q] % self.NDS]
        self.dptr[q] += 1
        sem, uses = slot
        deps = self._deps(reads, writes, False)
        if uses > 0:
            deps.append({sem.name: (sem, 16 * uses)})
        self._wait(q, deps)
        self.eng[q].dma_start(out=out, in_=in_, **kw).then_inc(sem, 16)
        slot[1] = uses + 1
        self._commit(reads, writes, sem, 16 * (uses + 1))
        self.all_dma[sem.name] = (sem, 16 * (uses + 1))
        self.ninst += 1
        return sem, 16 * (uses + 1)

    def barrier(self):
        allt = {}
        for k in self.eng:
            if self.cnt[k] > 0:
                allt[self.sem[k].name] = (self.sem[k], self.cnt[k])
        allt.update(self.all_dma)
        for k in self.eng:
            self._wait(k, [allt])

    def finish(self, e="sp"):
        allt = dict(self.all_dma)
        for k in self.eng:
            if self.cnt[k] > 0:
                allt[self.sem[k].name] = (self.sem[k], self.cnt[k])
        self._wait(e, [allt])


def run_gens(gens):
    live = [[g, n] for g, n in gens]
    while live:
        for item in list(live):
            g, n = item
            for _ in range(n):
                try:
                    next(g)
                except StopIteration:
                    live.remove(item)
                    break


class Arena:
    def __init__(self, nc, nbytes):
        self.t = nc.alloc_sbuf_tensor("arena", [128, nbytes], U8)
        self.ap = self.t.ap()
        self.nbytes = nbytes

    def view(self, off, shape, dtype):
        esz = mybir.dt.size(dtype)
        n = 1
        for s in shape[1:]:
            n *= s
        assert off % 4 == 0 and off + n * esz <= self.nbytes, (off, shape, self.nbytes)
        v = self.ap[:, off:off + n * esz].bitcast(dtype)
        if len(shape) == 3:
            v = v.rearrange("p (a b) -> p a b", b=shape[2])
        elif len(shape) == 4:
            v = v.rearrange("p (a b c) -> p a b c", b=shape[2], c=shape[3])
        return v


ATTN_STOP = 9


def build_program(n_layers=DEPTH, branches=(0, 1, 2, 3), debug=None):
    nc = bass.Bass("TRN2", target_bir_lowering=False)
    dt_in = lambda name, shape, dt=F32: nc.dram_tensor(name, list(shape), dt, kind="ExternalInput").ap()
    x_in = dt_in("x", [L, D])
    c_in = dt_in("c", [128, NKC])
    pos_in = dt_in("pos", [1, L], I32)
    w_ada = dt_in("w_ada", [DEPTH, D, 3 * D])
    b_ada = dt_in("b_ada", [DEPTH, 1, 3 * D])
    w_in = dt_in("w_in", [DEPTH, D, IN_COLS])
    w_br = dt_in("w_branch", [DEPTH, 4, BW, D])
    w_out = dt_in("w_out", [DEPTH, D, D])
    ln_wb = dt_in("ln_wb", [DEPTH, 2, 1, D])
    cp_in = dt_in("cp", [DEPTH, 128, NCP])
    lruw_in = dt_in("lruw", [DEPTH, 2, 8, 128, 128])
    ko_in = dt_in("konst", [128, NKO])
    y_out = nc.dram_tensor("y", [L, D], F32, kind="ExternalOutput").ap()
    x1 = nc.dram_tensor("x1_scr", [L, D], F32).ap()
    gscr = nc.dram_tensor("gate_scr", [128, D], F32).ap()
    mscr = nc.dram_tensor("m_scr", [NKC, 128, L], F32).ap()
    dbg = None
    if debug is not None:
        dbg = nc.dram_tensor("dbg", list(debug[1]), F32, kind="ExternalOutput").ap()

    S = Sched(nc)
    AR = Arena(nc, 206 * 1024)
    psum_t = nc.alloc_psum_tensor("ps", [128, 4096], F32)
    PS = psum_t.ap()
    PB = [PS[:, b * 512:(b + 1) * 512] for b in range(8)]
    PR = [Res("psum%d" % b, excl=True) for b in range(8)]

    o = 0
    def take(n):
        nonlocal o
        r = o
        o += (n + 31) // 32 * 32
        return r
    O_KO = take(NKO * 4)
    O_CP = take(NCP * 4)
    O_SMALL = take(4096)
    O_HT = take(NKC * L * 2)
    O_Y = take(8 * L * 2)
    O_W = take(4 * NKC * 128 * 2)
    O_WB = take(3 * 8 * 128 * 2)
    O_T = take(0)
    T_BYTES = AR.nbytes - O_T
    assert T_BYTES >= 64 * 1024, T_BYTES

    KOt = AR.view(O_KO, [128, NKO], F32)
    CPt = AR.view(O_CP, [128, NCP], F32)
    SM = AR.view(O_SMALL, [128, 1024], F32)
    HT = AR.view(O_HT, [128, NKC, L], BF16)
    YT = AR.view(O_Y, [128, 8, L], BF16)
    WR = [AR.view(O_W + i * NKC * 128 * 2, [128, NKC, 128], BF16) for i in range(4)]
    WRr = [Res("wr%d" % i) for i in range(4)]
    WBR = [AR.view(O_WB + i * 8 * 128 * 2, [128, 8, 128], BF16) for i in range(3)]
    WBRr = [Res("wbr%d" % i) for i in range(3)]
    rKO, rCP, rSM = Res("ko"), Res("cp"), Res("sm")
    rHT = [Res("ht%d" % i) for i in range(4)]
    rMT = [Res("mt%d" % i) for i in range(NKC)]
    rYT = [Res("yt%d" % i) for i in range(8)]
    rT = [Res("t%d" % i) for i in range(8)]
    rX1, rG, rDBG, rOUT = Res("x1"), Res("gscr"), Res("dbg"), Res("out")

    def ko(name, rows=128):
        a, w = KO[name]
        return KOt[0:rows, a:a + w]

    def cp(name, j=None, w=None):
        a, ww = CP[name]
        if j is None:
            return CPt[:, a:a + ww]
        w = w or 1
        return CPt[:, a + j * w:a + (j + 1) * w]

    SH = SM[:, 0:16]
    SC1 = SM[:, 16:32]
    C8 = SM[:, 32:40]
    CACT = SM[:, 40:56]
    SMT = SM[:, 64:256]

    S.dma("sp", KOt, ko_in, writes=[rKO])
    S.dma("sp", CACT, c_in, writes=[rSM])
    S.op("act", lambda e: e.activation(out=CACT, in_=CACT, func=AF.Silu), reads=[rSM], writes=[rSM])

    wr_i = [0]

    def proj_fm(l, col0, ncols, consumer, banks=(0, 1, 2, 3), wload=None):
        i = wr_i[0] % 4
        wr_i[0] += 1
        wt, wres = WR[i], WRr[i]
        if wload is None:
            S.dma("pool", wt[:, :, 0:ncols], w_in[l, :, col0:col0 + ncols].rearrange("(kc p) c -> p kc c", p=128),
                  writes=[wres])
        else:
            wload(wt, wres)
        for tt in range(4):
            b = banks[tt % len(banks)]
            for kc in range(NKC):
                S.op("pe", lambda e, kc=kc, tt=tt, b=b: e.matmul(
                    PB[b][0:ncols, :], lhsT=wt[:, kc, 0:ncols], rhs=HT[:, kc, tt * 512:(tt + 1) * 512],
                    start=(kc == 0), stop=(kc == NKC - 1)),
                    reads=[wres, rHT[tt]], writes=[PR[b]], acc=(kc > 0))
            consumer(tt, PB[b][0:ncols, :], PR[b])

    def softplus_small(dst, src, tmp, n, reads, writes):
        t0, t1, t2 = tmp[:, 0:n], tmp[:, n:2 * n], tmp[:, 2 * n:3 * n]
        rw = dict(reads=reads, writes=writes)
        S.op("dve", lambda e: e.tensor_scalar_mul(out=t1, in0=src, scalar1=-1.0), **rw)
        S.op("dve", lambda e: e.tensor_max(out=t0, in0=src, in1=t1), **rw)
        S.op("act", lambda e: e.activation(out=t0, in_=t0, func=AF.Exp, scale=-1.0), **rw)
        S.op("dve", lambda e: e.tensor_scalar_add(out=t1, in0=t0, scalar1=2.0), **rw)
        S.op("dve", lambda e: e.reciprocal(out=t1, in_=t1), **rw)
        S.op("dve", lambda e: e.tensor_mul(out=t0, in0=t0, in1=t1), **rw)
        S.op("dve", lambda e: e.tensor_mul(out=t1, in0=t0, in1=t0), **rw)
        S.op("dve", lambda e: e.tensor_scalar(out=t2, in0=t1, scalar1=1.0 / 9.0, scalar2=1.0 / 7.0,
                                              op0=ALU.mult, op1=ALU.add), **rw)
        for cst in (1.0 / 5.0, 1.0 / 3.0, 1.0):
            S.op("dve", lambda e: e.tensor_mul(out=t2, in0=t2, in1=t1), **rw)
            S.op("dve", lambda e, cst=cst: e.tensor_scalar_add(out=t2, in0=t2, scalar1=cst), **rw)
        S.op("dve", lambda e: e.tensor_mul(out=t2, in0=t2, in1=t0), **rw)
        S.op("dve", lambda e: e.tensor_scalar_max(out=t0, in0=src, scalar1=0.0), **rw)
        S.op("dve", lambda e: e.scalar_tensor_tensor(out=dst, in0=t2, scalar=2.0, in1=t0,
                                                     op0=ALU.mult, op1=ALU.add), **rw)

    for l in range(n_layers):
        xin = x_in if l == 0 else x1
        xout = y_out if l == n_layers - 1 else x1
        S.barrier()
        S.dma("sp", CPt, cp_in[l], writes=[rCP])

        CB = AR.view(O_T, [128, NKC, 128], F32)
        ADA = AR.view(O_T + 8192, [128, 3 * D], F32)
        WA = [AR.view(O_T + 8192 + 24576 + i * 16384, [128, 8, 512], F32) for i in range(2)]
        rCB = Res("cb")
        rADAb = [Res("ada%d" % nb) for nb in range(12)]
        rWA = [Res("wa0"), Res("wa1")]
        S.op("dve", lambda e: e.tensor_copy(out=CB, in_=CACT.unsqueeze(2).to_broadcast([128, NKC, 128])),
             reads=[rSM], writes=[rCB])
        S.dma("sp", ADA, b_ada[l].partition_broadcast(128), writes=rADAb)
        XT = [AR.view(O_Y + i * 8192, [128, D], F32) for i in range(2)]
        HB = [AR.view(O_Y + 16384 + i * 4096, [128, D], BF16) for i in range(2)]
        IDB1 = AR.view(O_Y + 24576, [128, 128], BF16)
        rXT = [Res("xt0"), Res("xt1")]
        rHB = [Res("hb0"), Res("hb1")]
        rIDB1 = Res("idb1")
        S.op("dve", lambda e: e.tensor_copy(out=IDB1, in_=ko("ident")), reads=[rKO], writes=[rIDB1])
        wi = 0
        for nb in range(12):
            b = nb % 2
            for half in range(2):
                wt, wr = WA[wi % 2], rWA[wi % 2]
                wi += 1
                S.dma("sp" if half == 0 else "act", wt,
                      w_ada[l, half * 1024:(half + 1) * 1024, nb * 512:(nb + 1) * 512].rearrange(
                          "(kc p) c -> p kc c", p=128), writes=[wr])
                for k8 in range(8):
                    kc = half * 8 + k8
                    S.op("pe", lambda e, kc=kc, k8=k8, wt=wt, b=b: e.matmul(
                        PB[b], lhsT=CB[:, kc, :], rhs=wt[:, k8, :], start=(kc == 0), stop=(kc == NKC - 1)),
                        reads=[rCB, wr], writes=[PR[b]], acc=(kc > 0))
            if 4 <= nb < 8:
                S.op("dve", lambda e, nb=nb, b=b: e.scalar_tensor_tensor(
                    out=ADA[:, nb * 512:(nb + 1) * 512], in0=PB[b], scalar=1.0, in1=ADA[:, nb * 512:(nb + 1) * 512],
                    op0=ALU.add, op1=ALU.add), reads=[PR[b], rADAb[nb]], writes=[rADAb[nb]])
            else:
                S.op("dve", lambda e, nb=nb, b=b: e.tensor_add(out=ADA[:, nb * 512:(nb + 1) * 512],
                                                               in0=PB[b], in1=ADA[:, nb * 512:(nb + 1) * 512]),
                     reads=[PR[b], rADAb[nb]], writes=[rADAb[nb]])
        S.dma("sp", gscr, ADA[:, 2 * D:3 * D], reads=rADAb[8:12], writes=[rG])

        SHR = ADA[:, 0:D]
        SCR = ADA[:, D:2 * D]
        for t16 in range(16):
            xt, xr = XT[t16 % 2], rXT[t16 % 2]
            hb, hr = HB[t16 % 2], rHB[t16 % 2]
            S.dma("sp", xt, xin[t16 * 128:(t16 + 1) * 128, :], reads=[rX1] if l > 0 else [], writes=[xr])
            S.op("dve", lambda e, xt=xt: e.tensor_mul(out=xt, in0=xt, in1=SCR), reads=[xr] + rADAb[4:8], writes=[xr])
            S.op("dve", lambda e, xt=xt, hb=hb: e.tensor_add(out=hb, in0=xt, in1=SHR), reads=[xr] + rADAb[0:4], writes=[hr])
            pb0 = 2 * (t16 % 2)
            ptp = PS[:, pb0 * 512:(pb0 + 2) * 512].bitcast(BF16)
            for kc in range(NKC):
                S.op("pe", lambda e, kc=kc, hb=hb, ptp=ptp: e.transpose(out=ptp[:, kc * 128:(kc + 1) * 128],
                                                                       in_=hb[:, kc * 128:(kc + 1) * 128], identity=IDB1),
                     reads=[hr, rIDB1], writes=[PR[pb0 + kc // 8]], acc=(kc % 8 > 0))
            for bk in range(2):
                S.op("act", lambda e, bk=bk, ptp=ptp, t16=t16: e.activation(
                    out=HT[:, 8 * bk:8 * bk + 8, t16 * 128:(t16 + 1) * 128],
                    in_=ptp[:, bk * 1024:(bk + 1) * 1024].rearrange("p (a b) -> p a b", b=128), func=AF.Copy),
                    reads=[PR[pb0 + bk]], writes=[rHT[t16 // 4]], acc=True)
        if debug is not None and debug[0] == "ht" and l == debug[2]:
            DT_ = AR.view(O_T, [128, L], F32)
            for kc in range(NKC):
                S.op("dve", lambda e, kc=kc: e.tensor_copy(out=DT_, in_=HT[:, kc, :]), reads=rHT, writes=[rT[0]])
                S.dma("sp", dbg[kc * 128:(kc + 1) * 128, :], DT_, reads=[rT[0]], writes=[rDBG])
        S.barrier()

        TA = AR.view(O_T, [128, L + 32], F32)
        TB = AR.view(O_T + 8320, [128, L + 32], F32)
        TC = AR.view(O_T + 2 * 8320, [128, L + 32], F32)
        rTA, rTB, rTC = rT[0], rT[1], rT[2]
        first_branch = [True]

        def evac_copy(dst, off, res):
            def f(tt, ps, pres):
                S.op("act", lambda e: e.activation(out=dst[:, off + tt * 512: off + (tt + 1) * 512], in_=ps, func=AF.Copy),
                     reads=[pres], writes=[res], acc=(tt > 0))
            return f

        def evac_act(dst, off, res, func, bias=None):
            def f(tt, ps, pres):
                kw = {} if bias is None else {"bias": bias}
                S.op("act", lambda e: e.activation(out=dst[:, off + tt * 512: off + (tt + 1) * 512], in_=ps, func=func, **kw),
                     reads=[pres, rCP], writes=[res], acc=(tt > 0))
            return f

        def evac_mul(dst, doff, src, soff, res):
            def f(tt, ps, pres):
                S.op("dve", lambda e: e.tensor_mul(out=dst[:, doff + tt * 512: doff + (tt + 1) * 512], in0=ps,
                                                   in1=src[:, soff + tt * 512: soff + (tt + 1) * 512]),
                     reads=[pres, res], writes=[res], acc=(tt > 0))
            return f

        MOFF = O_T + 3 * 8320
        SGB = [AR.view(MOFF + i * 2048, [128, 512], F32) for i in range(3)]
        PVB = [AR.view(MOFF + 3 * 2048 + i * 2048, [128, 512], F32) for i in range(3)]
        rSGB = [Res("sg%d" % i) for i in range(3)]
        rPVB = [Res("pv%d" % i) for i in range(3)]

        def merge_branch(k, rstd_row=None, rstd_res=None):
            first = first_branch[0]
            it = 0
            pending = None
            for dc in range(NKC):
                wi_ = wr_i[0] % 4
                wr_i[0] += 1
                wg, wgr = WR[wi_], WRr[wi_]
                S.dma("pool", wg, w_in[l, :, MERGE + k * D + dc * 128: MERGE + k * D + (dc + 1) * 128].rearrange(
                    "(kc p) c -> p kc c", p=128), writes=[wgr])
                wb, wbr = WBR[dc % 3], WBRr[dc % 3]
                S.dma("pool", wb, w_br[l, k, :, dc * 128:(dc + 1) * 128].rearrange("(c p) d -> p c d", p=128),
                      writes=[wbr])
                for tt in range(4):
                    bg, bb = (it % 2) * 2, (it % 2) * 2 + 1
                    sg, sgr = SGB[it % 3], rSGB[it % 3]
                    pv, pvr = PVB[it % 3], rPVB[it % 3]
                    it += 1
                    if not first:
                        S.dma("sp", pv, mscr[dc, :, tt * 512:(tt + 1) * 512], reads=[rMT[dc]], writes=[pvr])
                    for kc in range(NKC):
                        S.op("pe", lambda e, kc=kc, tt=tt, bg=bg, wg=wg: e.matmul(
                            PB[bg], lhsT=wg[:, kc, :], rhs=HT[:, kc, tt * 512:(tt + 1) * 512],
                            start=(kc == 0), stop=(kc == NKC - 1)), reads=[wgr, rHT[tt]], writes=[PR[bg]], acc=(kc > 0))
                    for c in range(8):
                        S.op("pe", lambda e, c=c, tt=tt, bb=bb, wb=wb: e.matmul(
                            PB[bb], lhsT=wb[:, c, :], rhs=YT[:, c, tt * 512:(tt + 1) * 512],
                            start=(c == 0), stop=(c == 7)), reads=[wbr, rYT[c]], writes=[PR[bb]], acc=(c > 0))
                    bgc = cp("b_gate")[:, k * 16 + dc:k * 16 + dc + 1]
                    S.op("act", lambda e, bg=bg, sg=sg, bgc=bgc: e.activation(out=sg, in_=PB[bg], func=AF.Sigmoid, bias=bgc),
                         reads=[PR[bg], rCP], writes=[sgr])
                    if pending is not None:
                        pending()
                    S.op("dve", lambda e, bb=bb, sg=sg: e.tensor_mul(out=sg, in0=sg, in1=PB[bb]),
                         reads=[PR[bb], sgr], writes=[sgr])
                    if rstd_row is not None:
                        S.op("dve", lambda e, sg=sg, tt=tt: e.tensor_mul(out=sg, in0=sg, in1=rstd_row[:, tt * 512:(tt + 1) * 512]),
                             reads=[sgr, rstd_res], writes=[sgr])
                    if not first:
                        S.op("dve", lambda e, sg=sg, pv=pv: e.tensor_add(out=sg, in0=sg, in1=pv), reads=[sgr, pvr], writes=[sgr])
                    pending = (lambda sg=sg, sgr=sgr, dc=dc, tt=tt: S.dma(
                        "sp", mscr[dc, :, tt * 512:(tt + 1) * 512], sg, reads=[sgr], writes=[rMT[dc]]))
            pending()
            first_branch[0] = False


        if 0 in branches:
            S.barrier()
            o2 = [O_T]
            def tk(n):
                r = o2[0]
                o2[0] += (n + 31) // 32 * 32
                return r
            SSQ = AR.view(tk(8192), [128, L], F32)
            DTT = AR.view(tk(1024), [128, 16, 16], F32)
            DA = AR.view(tk(1024), [128, 16, 16], F32)
            CUMC = AR.view(tk(1024), [128, 16, 16], F32)
            CDEC = AR.view(tk(1024), [128, 16, 16], F32)
            DTDE = AR.view(tk(1024), [128, 16, 16], F32)
            SPT = AR.view(tk(3072), [128, 768], F32)
            XS = [AR.view(tk(8192), [128, L], F32) for _ in range(2)]
            CV = AR.view(tk(8320), [128, L + 32], F32)
            BT = AR.view(tk(4096), [128, L], BF16)
            CT = AR.view(tk(4096), [128, L], BF16)
            ZS = [AR.view(tk(4096), [128, L], BF16) for _ in range(2)]
            XDTc = [AR.view(tk(512), [128, 4, 64], BF16) for _ in range(2)]
            XDEc = [AR.view(tk(512), [128, 4, 64], BF16) for _ in range(2)]
            Bc = [AR.view(tk(256), [128, 128], BF16) for _ in range(2)]
            RR = AR.view(tk(2048), [128, 4, 128], F32)
            SEG = AR.view(tk(2048), [128, 4, 128], F32)
            MT2 = [AR.view(tk(1024), [128, 4, 128], BF16) for _ in range(2)]
            EC = AR.view(tk(2048), [128, 4, 128], F32)
            CE2 = [AR.view(tk(1024), [128, 4, 128], BF16) for _ in range(2)]
            rMT2 = [Res("mtc0"), Res("mtc1")]
            rCE2 = [Res("cec0"), Res("cec1")]
            STATE = AR.view(tk(1024), [128, 4, 64], F32)
            STATEB = AR.view(tk(512), [128, 256], BF16)
            IDB = AR.view(tk(256), [128, 128], BF16)
            assert o2[0] <= AR.nbytes, o2[0]
            (rSSQ, rDT, rXS0, rXS1, rCV, rBT, rCT, rZS0, rZS1, rRR, rSEG, rMTc, rEC, rCEc, rST, rSTB, rIDB, rSPT) = (
                Res(n) for n in ("ssq", "dt", "xs0", "xs1", "cv", "bt", "ct", "zs0", "zs1", "rr", "seg", "mtc", "ec", "cec",
                                 "st", "stb", "idb", "spt"))
            rXS = [rXS0, rXS1]
            rZS = [rZS0, rZS1]
            rXDT = [Res("xdt0"), Res("xdt1")]
            rXDE = [Res("xde0"), Res("xde1")]
            rBc = [Res("bc0"), Res("bc1")]
            S.op("dve", lambda e: e.tensor_copy(out=IDB, in_=ko("ident")), reads=[rKO], writes=[rIDB])
            S.op("dve", lambda e: e.memset(SSQ, 0.0), writes=[rSSQ])
            S.op("dve", lambda e: e.memset(CV[:, 0:4], 0.0), writes=[rCV])
            wi_ = wr_i[0] % 4
            wr_i[0] += 1
            wdt, wdtr = WR[wi_], WRr[wi_]
            S.dma("pool", wdt[:, :, 0:16], w_in[l, :, A_DT:A_DT + 16].rearrange("(kc p) c -> p kc c", p=128), writes=[wdtr])
            for c in range(16):
                for kc in range(NKC):
                    S.op("pe", lambda e, c=c, kc=kc: e.matmul(PB[0][:, c * 16:(c + 1) * 16], lhsT=HT[:, kc, c * 128:(c + 1) * 128],
                                                             rhs=wdt[:, kc, 0:16], start=(kc == 0), stop=(kc == NKC - 1)),
                         reads=[wdtr, rHT[c // 4]], writes=[PR[0]], acc=(c > 0 or kc > 0))
            p0v = PB[0][:, 0:256].rearrange("p (c h) -> p c h", h=16)
            S.op("dve", lambda e: e.tensor_add(out=DTT, in0=p0v, in1=cp("dt_bias").unsqueeze(1).to_broadcast([128, 16, 16])),
                 reads=[PR[0], rCP], writes=[rDT])
            DTTf = DTT.rearrange("p c h -> p (c h)")
            DAf = DA.rearrange("p c h -> p (c h)")
            CUMCf = CUMC.rearrange("p c h -> p (c h)")
            CDECf = CDEC.rearrange("p c h -> p (c h)")
            DTDEf = DTDE.rearrange("p c h -> p (c h)")
            softplus_small(DTTf, DTTf, SPT, 256, [rDT, rSPT], [rDT, rSPT])
            S.op("act", lambda e: e.activation(out=SMT[:, 32:48], in_=cp("a_log"), func=AF.Exp), reads=[rCP], writes=[rSM])
            S.op("dve", lambda e: e.scalar_tensor_tensor(out=DA, in0=DTT, scalar=-1.0,
                                                         in1=SMT[:, 32:48].unsqueeze(1).to_broadcast([128, 16, 16]),
                                                         op0=ALU.mult, op1=ALU.mult), reads=[rDT, rSM], writes=[rDT])
            S.op("pe", lambda e: e.matmul(PB[1][:, 0:256], lhsT=ko("triu"), rhs=DAf, start=True, stop=True),
                 reads=[rKO, rDT], writes=[PR[1]])
            S.op("pe", lambda e: e.matmul(PB[2][:, 0:256], lhsT=ko("ones"), rhs=DAf, start=True, stop=True),
                 reads=[rKO, rDT], writes=[PR[2]])
            S.op("dve", lambda e: e.tensor_copy(out=CUMCf, in_=PB[1][:, 0:256]), reads=[PR[1]], writes=[rDT])
            S.op("act", lambda e: e.activation(out=CDECf, in_=PB[2][:, 0:256], func=AF.Exp), reads=[PR[2]], writes=[rDT])
            S.op("dve", lambda e: e.tensor_sub(out=DTDEf, in0=PB[2][:, 0:256], in1=CUMCf), reads=[PR[2], rDT], writes=[rDT])
            S.op("act", lambda e: e.activation(out=DTDEf, in_=DTDEf, func=AF.Exp), reads=[rDT], writes=[rDT])
            S.op("dve", lambda e: e.tensor_mul(out=DTDEf, in0=DTDEf, in1=DTTf), reads=[rDT], writes=[rDT])

            def conv_silu(chunk16, dst, dres, out_bf16):
                cw = cp("ssd_cw")
                acc_t = dst if not out_bf16 else SEGL
                acc_r = dres if not out_bf16 else rSEGL
                S.op("dve", lambda e: e.tensor_scalar(out=acc_t, in0=CV[:, 0:L], scalar1=cw[:, chunk16 * 4:chunk16 * 4 + 1],
                                                      scalar2=cp("ssd_cb")[:, chunk16:chunk16 + 1], op0=ALU.mult, op1=ALU.add),
                     reads=[rCV, rCP], writes=[acc_r])
                for kk in (1, 2, 3):
                    S.op("dve", lambda e, kk=kk: e.scalar_tensor_tensor(
                        out=acc_t, in0=CV[:, kk:kk + L], scalar=cw[:, chunk16 * 4 + kk:chunk16 * 4 + kk + 1], in1=acc_t,
                        op0=ALU.mult, op1=ALU.add), reads=[rCV, acc_r, rCP], writes=[acc_r])
                S.op("act", lambda e: e.activation(out=dst, in_=acc_t, func=AF.Silu), reads=[acc_r], writes=[dres, acc_r])

            SEGL = AR.view(O_T + 8192 + 5 * 1024 + 3072 + 2 * 8192 + 8320 + 16384 + 1536, [128, L], F32) if False else None
            rSEGL = None
            for g in range(4):
                SEGL, rSEGL = XS[1], rXS[1]
                proj_fm(l, A_B + g * 128, 128, evac_copy(CV, 3, rCV))
                conv_silu(8 + g, BT, rBT, True)
                proj_fm(l, A_C + g * 128, 128, evac_copy(CV, 3, rCV))
                conv_silu(12 + g, CT, rCT, True)
                for i in range(2):
                    proj_fm(l, A_X + (2 * g + i) * 128, 128, evac_copy(CV, 3, rCV))
                    conv_silu(2 * g + i, XS[i], rXS[i], False)
                    proj_fm(l, A_Z + (2 * g + i) * 128, 128, evac_act(ZS[i], 0, rZS[i], AF.Silu))
                S.op("dve", lambda e: e.memset(STATE, 0.0), writes=[rST])
                hs4 = slice(4 * g, 4 * g + 4)

                def ssd_P(c):
                    tok = slice(c * 128, (c + 1) * 128)
                    xd, rxd = XDTc[c % 2], rXDT[c % 2]
                    xe, rxe = XDEc[c % 2], rXDE[c % 2]
                    bc, rbc = Bc[c % 2], rBc[c % 2]
                    mt, rmt = MT2[c % 2], rMT2[c % 2]
                    ce, rce = CE2[c % 2], rCE2[c % 2]
                    S.op("dve", lambda e: e.tensor_mul(
                        out=RR, in0=DA[:, c, hs4].unsqueeze(2).to_broadcast([128, 4, 128]),
                        in1=ko("triu").unsqueeze(1).to_broadcast([128, 4, 128])), reads=[rDT, rKO], writes=[rRR])
                    for i in range(2):
                        b = 4 + i
                        S.op("pe", lambda e, i=i, b=b: e.transpose(out=PB[b][:, 0:128], in_=XS[i][:, tok], identity=ko("ident")),
                             reads=[rXS[i], rKO], writes=[PR[b]])
                    yield
                    S.op("pe", lambda e: e.matmul(PB[3], lhsT=ko("ones"), rhs=RR.rearrange("p h l -> p (h l)"), start=True, stop=True),
                         reads=[rKO, rRR], writes=[PR[3]])
                    p3v = PB[3].rearrange("p (h l) -> p h l", l=128)
                    yield
                    for i in range(2):
                        b = 4 + i
                        pv = PB[b][:, 0:128].rearrange("p (h d) -> p h d", d=64)
                        hs = slice(4 * g + 2 * i, 4 * g + 2 * i + 2)
                        S.op("dve", lambda e, i=i, pv=pv, hs=hs: e.tensor_mul(
                            out=xd[:, 2 * i:2 * i + 2, :], in0=pv, in1=DTT[:, c, hs].unsqueeze(2).to_broadcast([128, 2, 64])),
                            reads=[PR[b], rDT], writes=[rxd], acc=(i > 0))
                        S.op("dve", lambda e, i=i, pv=pv, hs=hs: e.tensor_mul(
                            out=xe[:, 2 * i:2 * i + 2, :], in0=pv, in1=DTDE[:, c, hs].unsqueeze(2).to_broadcast([128, 2, 64])),
                            reads=[PR[b], rDT], writes=[rxe], acc=(i > 0))
                        yield
                    p6 = PB[6].bitcast(BF16)
                    S.op("pe", lambda e: e.transpose(out=p6[:, 0:128], in_=BT[:, tok], identity=IDB),
                         reads=[rBT, rIDB], writes=[PR[6]])
                    S.op("act", lambda e: e.activation(out=bc, in_=p6[:, 0:128], func=AF.Copy), reads=[PR[6]], writes=[rbc])
                    yield
                    S.op("pe", lambda e: e.matmul(PB[7][:, 0:128], lhsT=BT[:, tok], rhs=CT[:, tok], start=True, stop=True),
                         reads=[rBT, rCT], writes=[PR[7]])
                    yield
                    S.op("dve", lambda e: e.tensor_sub(
                        out=SEG, in0=p3v, in1=CUMC[:, c, hs4].unsqueeze(2).to_broadcast([128, 4, 128])),
                        reads=[PR[3], rDT], writes=[rSEG])
                    if c > 0:
                        S.op("act", lambda e: e.activation(out=EC, in_=p3v, func=AF.Exp), reads=[PR[3], rSEG], writes=[rEC])
                    yield
                    S.op("dve", lambda e: e.tensor_add(out=SEG, in0=SEG, in1=ko("smask").unsqueeze(1).to_broadcast([128, 4, 128])),
                         reads=[rSEG, rKO], writes=[rSEG])
                    yield
                    S.op("act", lambda e: e.activation(out=SEG, in_=SEG, func=AF.Exp), reads=[rSEG], writes=[rSEG])
                    if c > 0:
                        S.op("dve", lambda e: e.tensor_mul(out=ce, in0=EC, in1=CT[:, tok].unsqueeze(1).to_broadcast([128, 4, 128])),
                             reads=[rEC, rCT], writes=[rce])
                    yield
                    S.op("dve", lambda e: e.tensor_mul(out=mt, in0=SEG, in1=PB[7][:, 0:128].unsqueeze(1).to_broadcast([128, 4, 128])),
                         reads=[rSEG, PR[7]], writes=[rmt])
                    yield

                def ssd_Q(c):
                    tok = slice(c * 128, (c + 1) * 128)
                    xd, rxd = XDTc[c % 2], rXDT[c % 2]
                    xe, rxe = XDEc[c % 2], rXDE[c % 2]
                    bc, rbc = Bc[c % 2], rBc[c % 2]
                    mt, rmt = MT2[c % 2], rMT2[c % 2]
                    ce, rce = CE2[c % 2], rCE2[c % 2]
                    for i in range(2):
                        b = 0 + i
                        for h2 in range(2):
                            h = 2 * i + h2
                            pr = slice(64 * h2, 64 * h2 + 64)
                            S.op("pe", lambda e, b=b, pr=pr, h=h: e.matmul(
                                PB[b][pr, 0:128], lhsT=xd[:, h, :], rhs=mt[:, h, :], start=True, stop=(c == 0)),
                                reads=[rxd, rmt], writes=[PR[b]], acc=(h2 > 0))
                            if c > 0:
                                S.op("pe", lambda e, b=b, pr=pr, h=h: e.matmul(
                                    PB[b][pr, 0:128], lhsT=STATEB[:, h * 64:(h + 1) * 64], rhs=ce[:, h, :], start=False, stop=True),
                                    reads=[rSTB, rce], writes=[PR[b]], acc=True)
                        yield
                        j = 2 * g + i
                        S.op("dve", lambda e, i=i, b=b, j=j: e.scalar_tensor_tensor(
                            out=XS[i][:, tok], in0=XS[i][:, tok], scalar=cp("ssd_d")[:, j:j + 1], in1=PB[b][:, 0:128],
                            op0=ALU.mult, op1=ALU.add), reads=[PR[b], rXS[i], rCP], writes=[rXS[i]])
                        yield
                        S.op("dve", lambda e, i=i: e.tensor_mul(out=XS[i][:, tok], in0=XS[i][:, tok], in1=ZS[i][:, tok]),
                             reads=[rXS[i], rZS[i]], writes=[rXS[i]])
                        yield
                    if c < 15:
                        S.op("pe", lambda e: e.matmul(PB[2][:, 0:256], lhsT=bc, rhs=xe.rearrange("p h d -> p (h d)"),
                                                      start=True, stop=True), reads=[rbc, rxe], writes=[PR[2]])
                        S.op("dve", lambda e: e.tensor_mul(
                            out=STATE, in0=STATE, in1=CDEC[:, c, hs4].unsqueeze(2).to_broadcast([128, 4, 64])),
                            reads=[rST, rDT], writes=[rST])
                        yield
                        S.op("dve", lambda e: e.tensor_add(out=STATE, in0=STATE, in1=PB[2][:, 0:256].rearrange("p (h d) -> p h d", d=64)),
                             reads=[rST, PR[2]], writes=[rST])
                        yield
                        S.op("act", lambda e: e.activation(out=STATEB, in_=STATE.rearrange("p h d -> p (h d)"), func=AF.Copy),
                             reads=[rST], writes=[rSTB])
                        yield

                run_gens([(ssd_P(0), 1)])
                for c in range(16):
                    gens = [(ssd_Q(c), 1)]
                    if c + 1 < 16:
                        gens.insert(0, (ssd_P(c + 1), 1))
                    run_gens(gens)
                for i in range(2):
                    j = 2 * g + i
                    S.op("act", lambda e, i=i: e.activation(out=CV[:, 4:4 + L], in_=XS[i], func=AF.Square), reads=[rXS[i], rCV], writes=[rCV])
                    for tt in range(4):
                        b = 4 + tt
                        S.op("pe", lambda e, tt=tt, b=b: e.matmul(PB[b], lhsT=ko("ones"), rhs=CV[:, 4 + tt * 512:4 + (tt + 1) * 512],
                                                                 start=True, stop=True), reads=[rKO, rCV], writes=[PR[b]])
                        S.op("dve", lambda e, tt=tt, b=b: e.tensor_add(out=SSQ[:, tt * 512:(tt + 1) * 512], in0=SSQ[:, tt * 512:(tt + 1) * 512],
                                                                      in1=PB[b]), reads=[PR[b], rSSQ], writes=[rSSQ])
                    S.op("dve", lambda e, i=i, j=j: e.tensor_scalar_mul(out=YT[:, j, :], in0=XS[i], scalar1=cp("ssd_nw")[:, j:j + 1]),
                         reads=[rXS[i], rCP], writes=[rYT[j]])
                S.op("dve", lambda e: e.memset(CV[:, 0:4], 0.0), reads=[rCV], writes=[rCV])
            S.op("dve", lambda e: e.tensor_scalar(out=SSQ, in0=SSQ, scalar1=1.0 / BW, scalar2=RMS_EPS, op0=ALU.mult, op1=ALU.add),
                 reads=[rSSQ], writes=[rSSQ])
            S.op("act", lambda e: e.activation(out=SSQ, in_=SSQ, func=AF.Sqrt), reads=[rSSQ], writes=[rSSQ])
            S.op("dve", lambda e: e.reciprocal(out=SSQ, in_=SSQ), reads=[rSSQ], writes=[rSSQ])
            if debug is not None and debug[0] == "ya" and l == debug[2]:
                S.barrier()
                DT_ = AR.view(O_T + 8192, [128, L], F32)
                for j in range(8):
                    S.op("dve", lambda e, j=j: e.tensor_mul(out=DT_, in0=YT[:, j, :], in1=SSQ), reads=[rYT[j], rSSQ], writes=[rT[0]])
                    S.dma("sp", dbg[j * 128:(j + 1) * 128, :], DT_, reads=[rT[0]], writes=[rDBG])
            S.barrier()
            merge_branch(0, rstd_row=SSQ, rstd_res=rSSQ)
            S.barrier()

        if 1 in branches:
            S.barrier()
            LP = L + 128
            COS = AR.view(O_T, [128, L], F32)
            SIN = AR.view(O_T + 8192, [128, L], F32)
            KT2 = AR.view(O_T + 16384, [128, 4, LP], BF16)
            VT = AR.view(O_T + 33792, [128, 17, 256], BF16)
            QF = AR.view(O_T + 42496, [128, L], F32)
            QR = AR.view(O_T + 50688, [128, L], F32)
            QI = AR.view(O_T + 50688, [128, L], I32)
            SMB = AR.view(O_T + 50688, [128, 8, 256], F32)
            WV = AR.view(O_T + 50688, [128, NKC, 256], BF16)
            QT = AR.view(O_T + 58880, [128, L], BF16)
            GS = AR.view(O_T + 62976, [128, L], BF16)
            PBF = AR.view(O_T + 67072, [128, 8, 256], BF16)
            PTS = AR.view(O_T + 71168, [128, 2048], BF16)
            IDB = AR.view(O_T + 75264, [128, 128], BF16)
            assert O_T + 75264 + 256 <= AR.nbytes
            rCOS, rSIN, rKT2, rVT, rQF, rQR, rQT, rGS, rIDB, rPBF, rPTS, rAT, rPBFb = (Res(n) for n in
                ("cos", "sin", "kt2", "vt", "qf", "qr", "qt", "gs", "idb", "pbf", "pts", "at", "pbfb"))
            AT = SM[:, 320:384]
            MX = AT[:, 0:8]
            RS8 = AT[:, 8:16]
            ES = AT[:, 16:24]
            NMX = AT[:, 24:32]
            S.op("dve", lambda e: e.tensor_copy(out=IDB, in_=ko("ident")), reads=[rKO], writes=[rIDB])
            for kh in range(4):
                S.op("dve", lambda e, kh=kh: e.memset(KT2[:, kh, 0:128], 0.0), writes=[rKT2], acc=(kh > 0))
            S.op("dve", lambda e: e.memset(VT[:, 0, :], 0.0), writes=[rVT])
            S.dma("sp", QI, pos_in.partition_broadcast(128), writes=[rQR])
            S.op("dve", lambda e: e.tensor_copy(out=QF, in_=QI), reads=[rQR], writes=[rQF])
            S.op("dve", lambda e: e.tensor_scalar_mul(out=QF, in0=QF, scalar1=ko("invf")), reads=[rQF, rKO], writes=[rQF])
            S.op("dve", lambda e: e.tensor_scalar_mul(out=COS, in0=QF, scalar1=float(1.0 / (2 * np.pi))), reads=[rQF], writes=[rCOS])
            S.op("dve", lambda e: e.tensor_copy(out=QI, in_=COS), reads=[rCOS], writes=[rQR])
            S.op("dve", lambda e: e.tensor_copy(out=COS, in_=QI), reads=[rQR], writes=[rCOS])
            S.op("dve", lambda e: e.scalar_tensor_tensor(out=SIN, in0=COS, scalar=-6.28125, in1=QF, op0=ALU.mult, op1=ALU.add),
                 reads=[rCOS, rQF], writes=[rSIN])
            S.op("dve", lambda e: e.scalar_tensor_tensor(out=SIN, in0=COS, scalar=-0.0019353071795864769, in1=SIN,
                                                         op0=ALU.mult, op1=ALU.add), reads=[rCOS, rSIN], writes=[rSIN])
            S.op("dve", lambda e: e.tensor_scalar(out=SIN, in0=SIN, scalar1=-3.141592, scalar2=3.141592, op0=ALU.max, op1=ALU.min),
                 reads=[rSIN], writes=[rSIN])
            S.op("dve", lambda e: e.tensor_scalar_mul(out=QF, in0=SIN, scalar1=-1.0), reads=[rSIN], writes=[rQF])
            S.op("dve", lambda e: e.tensor_max(out=QF, in0=QF, in1=SIN), reads=[rSIN, rQF], writes=[rQF])
            S.op("act", lambda e: e.activation(out=COS, in_=QF, func=AF.Sin, scale=-1.0, bias=ko("halfpi")),
                 reads=[rQF, rKO], writes=[rCOS])
            S.op("act", lambda e: e.activation(out=SIN, in_=SIN, func=AF.Sin), reads=[rSIN], writes=[rSIN])

            def rope_chunk(dst, dres, qscale):
                for tt in range(4):
                    b = 4 + tt
                    S.op("pe", lambda e, tt=tt, b=b: e.matmul(PB[b], lhsT=ko("rot"), rhs=QF[:, tt * 512:(tt + 1) * 512],
                                                             start=True, stop=True), reads=[rKO, rQF], writes=[PR[b]])
                    S.op("dve", lambda e, tt=tt, b=b: e.scalar_tensor_tensor(
                        out=QR[:, tt * 512:(tt + 1) * 512], in0=PB[b], scalar=qscale, in1=SIN[:, tt * 512:(tt + 1) * 512],
                        op0=ALU.mult, op1=ALU.mult), reads=[PR[b], rSIN], writes=[rQR], acc=(tt > 0))
                S.op("dve", lambda e: e.scalar_tensor_tensor(out=QF, in0=QF, scalar=qscale, in1=COS, op0=ALU.mult, op1=ALU.mult),
                     reads=[rQF, rCOS], writes=[rQF])
                S.op("dve", lambda e: e.tensor_add(out=dst, in0=QF, in1=QR), reads=[rQF, rQR], writes=[dres])

            S.dma("pool", WV, w_in[l, :, B_V:B_V + 256].rearrange("(kc p) c -> p kc c", p=128), reads=[rQR], writes=[rQR])
            for n in range(16):
                b = n % 4
                for kc in range(NKC):
                    S.op("pe", lambda e, kc=kc, n=n, b=b: e.matmul(PB[b][:, 0:256], lhsT=HT[:, kc, n * 128:(n + 1) * 128],
                                                                  rhs=WV[:, kc, :], start=(kc == 0), stop=(kc == NKC - 1)),
                         reads=[rQR, rHT[n // 4]], writes=[PR[b]], acc=(kc > 0))
                S.op("act", lambda e, n=n, b=b: e.activation(out=VT[:, n + 1, :], in_=PB[b][:, 0:256], func=AF.Copy),
                     reads=[PR[b]], writes=[rVT], acc=True)
            for kh in range(4):
                def wl(wt, wres, kh=kh):
                    src = w_in[l, :, B_K + kh * 64:B_K + (kh + 1) * 64].rearrange("(kc p) c -> p kc c", p=128)
                    S.dma("pool", wt[:, :, 0:64], src, writes=[wres])
                    keep_w = dict(wres.w)
                    sem2, val2 = S.dma("pool", wt[:, :, 64:128], src, reads=[wres], writes=[])
                    wres.r = {}
                    wres.w = keep_w
                    wres.w[sem2.name] = (sem2, val2)
                proj_fm(l, 0, 128, evac_copy(QF, 0, rQF), wload=wl)
                rope_chunk(KT2[:, kh, 128:LP], rKT2, 1.0)
            PSG = PS[:, 0:2048].rearrange("p (i s) -> p i s", s=256)
            PTP = PS[:, 2048:3072].bitcast(BF16)
            rPSG = Res("psg")
            rPTP = Res("ptp")

            def s_matmuls(jq, ng):
                kh = jq // 2
                for nbi in range(4):
                    n = ng * 4 + nbi
                    for h2 in range(2):
                        i = h2 * 4 + nbi
                        pr = slice(64 * h2, 64 * h2 + 64)
                        if ATTN_STOP == 2.6:
                            S.op("pe", lambda e, pr=pr, n=n, i=i, kh=kh: e.matmul(
                                PB[i % 4][:, 0:256], lhsT=QT[pr, n * 128:(n + 1) * 128], rhs=KT2[pr, kh, n * 128:n * 128 + 256],
                                start=True, stop=True), reads=[rQT, rKT2], writes=[PR[i % 4]])
                            continue
                        S.op("pe", lambda e, pr=pr, n=n, i=i, kh=kh: e.matmul(
                            PSG[:, i, :], lhsT=QT[pr, n * 128:(n + 1) * 128], rhs=KT2[pr, kh, n * 128:n * 128 + 256],
                            start=True, stop=True), reads=[rQT, rKT2], writes=[PR[0], PR[1], PR[2], PR[3]], acc=(nbi > 0 or h2 > 0))

            for jq in range(8 if ATTN_STOP >= 2 else 0):
                kh = jq // 2
                proj_fm(l, B_Q + jq * 128, 128, evac_copy(QF, 0, rQF))
                rope_chunk(QT, rQT, 0.125)
                proj_fm(l, B_G + jq * 128, 128, evac_act(GS, 0, rGS, AF.Silu))
                sinkb = cp("sinks")[:, 2 * jq:2 * jq + 2].unsqueeze(2).to_broadcast([128, 2, 4])
                v42 = lambda t: t.rearrange("p (a b) -> p a b", b=4)
                PBF2 = [PBF, AR.view(O_T + 75520, [128, 8, 256], BF16)]
                rPBF2 = [rPBF, rPBFb]
                assert O_T + 75520 + 4096 <= AR.nbytes

                def att_H1(ng, jq=jq, sinkb=sinkb, v42=v42):
                    pbf, rpbf = PBF2[ng % 2], rPBF2[ng % 2]
                    for bk in range(4):
                        S.op("dve", lambda e, bk=bk: e.tensor_add(out=SMB[:, 2 * bk:2 * bk + 2, :], in0=PSG[:, 2 * bk:2 * bk + 2, :],
                                                                 in1=ko("amask").unsqueeze(1).to_broadcast([128, 2, 256])),
                             reads=[PR[bk], rKO], writes=[rQR], acc=(bk > 0))
                        if bk % 2 == 1:
                            yield
                    if ng < 3:
                        s_matmuls(jq, ng + 1)
                    yield
                    if ng == 0:
                        for i0 in (0, 4):
                            S.op("dve", lambda e, i0=i0: e.tensor_scalar_add(out=SMB[:, i0, 0:128], in0=SMB[:, i0, 0:128], scalar1=-30000.0),
                                 reads=[rQR], writes=[rQR])
                    S.op("dve", lambda e: e.reduce_max(out=MX, in_=SMB, axis=AX.X), reads=[rQR], writes=[rAT])
                    yield
                    S.op("dve", lambda e: e.tensor_max(out=v42(MX), in0=v42(MX), in1=sinkb), reads=[rAT, rCP], writes=[rAT])
                    S.op("dve", lambda e: e.tensor_scalar_mul(out=NMX, in0=MX, scalar1=-1.0), reads=[rAT], writes=[rAT])
                    yield
                    for i in range(8):
                        S.op("act", lambda e, i=i: e.activation(out=SMB[:, i, :], in_=SMB[:, i, :], func=AF.Exp, bias=NMX[:, i:i + 1],
                                                               accum_out=RS8[:, i:i + 1]), reads=[rQR, rAT], writes=[rQR, rAT], acc=(i > 0))
                        if i % 2 == 1:
                            yield
                    S.op("dve", lambda e: e.tensor_sub(out=v42(ES), in0=sinkb, in1=v42(MX)), reads=[rAT, rCP], writes=[rAT])
                    S.op("act", lambda e: e.activation(out=ES, in_=ES, func=AF.Exp), reads=[rAT], writes=[rAT])
                    yield
                    S.op("dve", lambda e: e.tensor_add(out=RS8, in0=RS8, in1=ES), reads=[rAT], writes=[rAT])
                    S.op("dve", lambda e: e.reciprocal(out=RS8, in_=RS8), reads=[rAT], writes=[rAT])
                    yield
                    S.op("dve", lambda e: e.tensor_mul(out=pbf, in0=SMB, in1=RS8.unsqueeze(2).to_broadcast([128, 8, 256])),
                         reads=[rQR, rAT], writes=[rpbf])
                    yield

                def att_H2(ng, jq=jq, kh=kh):
                    pbf, rpbf = PBF2[ng % 2], rPBF2[ng % 2]
                    bo = 6 + (ng % 2)
                    for i in range(8):
                        for blk in range(2):
                            S.op("pe", lambda e, i=i, blk=blk: e.transpose(
                                out=PTP[:, (i * 2 + blk) * 128:(i * 2 + blk + 1) * 128], in_=pbf[:, i, blk * 128:(blk + 1) * 128],
                                identity=IDB), reads=[rpbf, rIDB], writes=[PR[4], PR[5]], acc=(i > 0 or blk > 0))
                        if i % 2 == 1:
                            yield
                    for bk in range(2):
                        S.op("act", lambda e, bk=bk: e.activation(out=PTS[:, bk * 1024:(bk + 1) * 1024], in_=PTP[:, bk * 1024:(bk + 1) * 1024],
                                                                 func=AF.Copy), reads=[PR[4 + bk]], writes=[rPTS], acc=(bk > 0))
                    yield
                    for nbi in range(4):
                        n = ng * 4 + nbi
                        for h2 in range(2):
                            i = h2 * 4 + nbi
                            pr = slice(64 * h2, 64 * h2 + 64)
                            for blk in range(2):
                                S.op("pe", lambda e, i=i, blk=blk, pr=pr, n=n, nbi=nbi: e.matmul(
                                    PB[bo][pr, nbi * 128:(nbi + 1) * 128], lhsT=VT[:, n + blk, kh * 64:(kh + 1) * 64],
                                    rhs=PTS[:, (i * 2 + blk) * 128:(i * 2 + blk + 1) * 128], start=(blk == 0), stop=(blk == 1)),
                                    reads=[rVT, rPTS], writes=[PR[bo]], acc=(nbi > 0 or h2 > 0 or blk > 0))
                        yield
                    S.op("dve", lambda e: e.tensor_mul(out=YT[:, jq, ng * 512:(ng + 1) * 512], in0=PB[bo],
                                                       in1=GS[:, ng * 512:(ng + 1) * 512]),
                         reads=[PR[bo], rGS], writes=[rYT[jq]], acc=True)
                    yield

                s_matmuls(jq, 0)
                run_gens([(att_H1(0), 1)])
                for ng in range(4):
                    gens = [(att_H2(ng), 1)]
                    if ng < 3:
                        gens.insert(0, (att_H1(ng + 1), 1))
                    run_gens(gens)
            if debug is not None and debug[0] == "yb" and l == debug[2]:
                S.barrier()
                DT_ = AR.view(O_T, [128, L], F32)
                for j in range(8):
                    S.op("dve", lambda e, j=j: e.tensor_copy(out=DT_, in_=YT[:, j, :]), reads=[rYT[j]], writes=[rT[0]])
                    S.dma("sp", dbg[j * 128:(j + 1) * 128, :], DT_, reads=[rT[0]], writes=[rDBG])
            S.barrier()
            merge_branch(1)
            S.barrier()

        if 2 in branches:
            S.op("dve", lambda e: e.memset(TB[:, 0:4], 0.0), writes=[rTB])
            for j in range(8):
                proj_fm(l, C_C + j * 128, 128, evac_copy(TA, 0, rTA))
                proj_fm(l, C_X + j * 128, 128, evac_mul(TB, 2, TA, 0, rTB) if False else
                        (lambda tt, ps, pres: S.op("dve", lambda e: e.tensor_mul(
                            out=TB[:, 2 + tt * 512: 2 + (tt + 1) * 512], in0=ps, in1=TA[:, tt * 512:(tt + 1) * 512]),
                            reads=[pres, rTA], writes=[rTB], acc=(tt > 0))))
                wv = cp("sconv_w")
                S.op("dve", lambda e, j=j: e.tensor_scalar_mul(out=TA[:, 0:L], in0=TB[:, 0:L], scalar1=wv[:, j * 3:j * 3 + 1]),
                     reads=[rTB, rCP], writes=[rTA])
                for kk in (1, 2):
                    S.op("dve", lambda e, j=j, kk=kk: e.scalar_tensor_tensor(
                        out=TA[:, 0:L], in0=TB[:, kk:kk + L], scalar=wv[:, j * 3 + kk:j * 3 + kk + 1], in1=TA[:, 0:L],
                        op0=ALU.mult, op1=ALU.add), reads=[rTB, rTA, rCP], writes=[rTA])
                proj_fm(l, C_B + j * 128, 128, lambda tt, ps, pres: S.op("dve", lambda e: e.tensor_mul(
                    out=TA[:, tt * 512:(tt + 1) * 512], in0=ps, in1=TA[:, tt * 512:(tt + 1) * 512]),
                    reads=[pres, rTA], writes=[rTA], acc=(tt > 0)))
                proj_fm(l, C_G + j * 128, 128, evac_act(TC, 0, rTC, AF.Silu))
                S.op("dve", lambda e, j=j: e.tensor_mul(out=YT[:, j, :], in0=TA[:, 0:L], in1=TC[:, 0:L]),
                     reads=[rTA, rTC], writes=[rYT[j]])
            if debug is not None and debug[0] == "yc" and l == debug[2]:
                for j in range(8):
                    S.op("dve", lambda e, j=j: e.tensor_copy(out=TA[:, 0:L], in_=YT[:, j, :]), reads=[rYT[j]], writes=[rTA])
                    S.dma("sp", dbg[j * 128:(j + 1) * 128, :], TA[:, 0:L], reads=[rTA], writes=[rDBG])
            merge_branch(2)

        if 3 in branches:
            S.op("dve", lambda e: e.tensor_scalar_mul(out=SMT[:, 0:8], in0=cp("lru_lam"), scalar1=-1.0), reads=[rCP], writes=[rSM])
            softplus_small(C8, SMT[:, 0:8], SMT[:, 8:32], 8, [rSM], [rSM])
            S.op("dve", lambda e: e.tensor_scalar_mul(out=C8, in0=C8, scalar1=-8.0), reads=[rSM], writes=[rSM])
            S.op("dve", lambda e: e.memset(TB[:, 0:4], 0.0), writes=[rTB])
            LWT = [AR.view(MOFF + 6 * 2048 + i * 1024, [128, 2, 128], F32) for i in range(2)]
            rLWT = [Res("lwt0"), Res("lwt1")]
            for j in range(8):
                lwt, lwr = LWT[j % 2], rLWT[j % 2]
                S.dma("sp", lwt, lruw_in[l, :, j].rearrange("g p c -> p g c"), writes=[lwr])
                proj_fm(l, D_X + j * 128, 128, evac_copy(TB, 3, rTB))
                cw = cp("lru_cw")
                S.op("dve", lambda e, j=j: e.tensor_scalar(out=TA[:, 0:L], in0=TB[:, 0:L], scalar1=cw[:, j * 4:j * 4 + 1],
                                                          scalar2=cp("lru_cb")[:, j:j + 1], op0=ALU.mult, op1=ALU.add),
                     reads=[rTB, rCP], writes=[rTA])
                for kk in (1, 2, 3):
                    S.op("dve", lambda e, j=j, kk=kk: e.scalar_tensor_tensor(
                        out=TA[:, 0:L], in0=TB[:, kk:kk + L], scalar=cw[:, j * 4 + kk:j * 4 + kk + 1], in1=TA[:, 0:L],
                        op0=ALU.mult, op1=ALU.add), reads=[rTB, rTA, rCP], writes=[rTA])
                for gi, (bname, dstT, dres) in enumerate((("lru_ba", TB, rTB), ("lru_bx", TC, rTC))):
                    for tt in range(4):
                        b = 4 + (gi * 4 + tt) % 4
                        S.op("pe", lambda e, tt=tt, b=b, gi=gi, lwt=lwt: e.matmul(
                            PB[b], lhsT=lwt[:, gi, :], rhs=TA[:, tt * 512:(tt + 1) * 512],
                            start=True, stop=True), reads=[lwr, rTA], writes=[PR[b]])
                        S.op("act", lambda e, tt=tt, b=b, bname=bname, dstT=dstT, j=j: e.activation(
                            out=dstT[:, tt * 512:(tt + 1) * 512], in_=PB[b], func=AF.Sigmoid, bias=cp(bname)[:, j:j + 1]),
                            reads=[PR[b], rCP], writes=[dres], acc=(tt > 0))
                S.op("act", lambda e, j=j: e.activation(out=TB[:, 0:L], in_=TB[:, 0:L], func=AF.Exp, scale=C8[:, j:j + 1]),
                     reads=[rTB, rSM], writes=[rTB])
                S.op("dve", lambda e: e.tensor_mul(out=TA[:, 0:L], in0=TA[:, 0:L], in1=TC[:, 0:L]), reads=[rTA, rTC], writes=[rTA])
                S.op("act", lambda e: e.activation(out=TC[:, 0:L], in_=TB[:, 0:L], func=AF.Square), reads=[rTB, rTC], writes=[rTC])
                S.op("act", lambda e: e.activation(out=TC[:, 0:L], in_=TC[:, 0:L], func=AF.Sqrt, scale=-1.0, bias=1.0),
                     reads=[rTC], writes=[rTC])
                S.op("dve", lambda e: e.tensor_mul(out=TA[:, 0:L], in0=TA[:, 0:L], in1=TC[:, 0:L]), reads=[rTA, rTC], writes=[rTA])
                S.op("dve", lambda e: e.tensor_tensor_scan(out=TC[:, 0:L], data0=TB[:, 0:L], data1=TA[:, 0:L], initial=0.0,
                                                          op0=ALU.mult, op1=ALU.add), reads=[rTA, rTB], writes=[rTC])
                proj_fm(l, D_G + j * 128, 128, evac_act(TA, 0, rTA, AF.Silu))
                S.op("dve", lambda e, j=j: e.tensor_mul(out=YT[:, j, :], in0=TA[:, 0:L], in1=TC[:, 0:L]),
                     reads=[rTA, rTC], writes=[rYT[j]])
                S.op("dve", lambda e: e.memset(TB[:, 0:4], 0.0), reads=[rTB], writes=[rTB])
            if debug is not None and debug[0] == "yd" and l == debug[2]:
                for j in range(8):
                    S.op("dve", lambda e, j=j: e.tensor_copy(out=TA[:, 0:L], in_=YT[:, j, :]), reads=[rYT[j]], writes=[rTA])
                    S.dma("sp", dbg[j * 128:(j + 1) * 128, :], TA[:, 0:L], reads=[rTA], writes=[rDBG])
            merge_branch(3)

        if first_branch[0]:
            ZT = AR.view(O_T, [128, L], F32)
            S.op("dve", lambda e: e.memset(ZT, 0.0), writes=[rT[0]])
            for dc in range(NKC):
                S.dma("sp", mscr[dc], ZT, reads=[rT[0]], writes=[rMT[dc]])
        S.barrier()

        WO = AR.view(O_HT, [128, NKC, D], BF16)
        rWOk = [Res("wo%d" % kc) for kc in range(NKC)]
        for kc in range(NKC):
            S.dma("pool", WO[:, kc, :], w_out[l, kc * 128:(kc + 1) * 128, :], writes=[rWOk[kc]])
        GR = AR.view(O_Y, [128, D], F32)
        LW = AR.view(O_Y + 8192, [128, D], F32)
        LB = AR.view(O_Y + 16384, [128, D], F32)
        rGR = Res("gr")
        S.dma("sp", GR, gscr, reads=[rG], writes=[rGR])
        S.dma("sp", LW, ln_wb[l, 0].partition_broadcast(128), writes=[rGR])
        S.dma("sp", LB, ln_wb[l, 1].partition_broadcast(128), writes=[rGR])
        for kc in range(NKC):
            S.op("dve", lambda e, kc=kc: e.tensor_mul(out=WO[:, kc, :], in0=WO[:, kc, :], in1=GR), reads=[rWOk[kc], rGR], writes=[rWOk[kc]])
        XT = [AR.view(O_T + i * 8192, [128, D], F32) for i in range(2)]
        RSB = [AR.view(O_T + (2 + i) * 8192, [128, D], F32) for i in range(2)]
        MTT = [AR.view(O_T + 4 * 8192 + i * 4096, [128, NKC, 128], BF16) for i in range(2)]
        rMTT = [Res("mtt0"), Res("mtt1")]
        rXT = [rT[0], rT[1]]
        rRSB = [rT[2], rT[3]]
        ST = SM[:, 256:320]
        rST4 = [Res("st4a"), Res("st4b")]
        def p4_mm(t16):
            xt, xr = XT[t16 % 2], rXT[t16 % 2]
            RS, rRS = RSB[t16 % 2], rRSB[t16 % 2]
            MT1, rMT1 = MTT[t16 % 2], rMTT[t16 % 2]
            S.dma("sp", xt, xin[t16 * 128:(t16 + 1) * 128, :], reads=[rX1] if l > 0 else [], writes=[xr])
            S.dma("pool", MT1, mscr[:, :, t16 * 128:(t16 + 1) * 128].rearrange("dc p t -> p dc t"), reads=rMT, writes=[rMT1])
            for nb in range(4):
                b = (t16 * 4 + nb) % 8
                for kc in range(NKC):
                    S.op("pe", lambda e, kc=kc, nb=nb, b=b, MT1=MT1: e.matmul(
                        PB[b], lhsT=MT1[:, kc, :], rhs=WO[:, kc, nb * 512:(nb + 1) * 512],
                        start=(kc == 0), stop=(kc == NKC - 1)), reads=[rMT1, rWOk[kc]], writes=[PR[b]], acc=(kc > 0))
                S.op("dve", lambda e, nb=nb, b=b, RS=RS, xt=xt: e.scalar_tensor_tensor(
                    out=RS[:, nb * 512:(nb + 1) * 512], in0=xt[:, nb * 512:(nb + 1) * 512], scalar=ALPHA, in1=PB[b],
                    op0=ALU.mult, op1=ALU.add), reads=[PR[b], xr], writes=[rRS], acc=(nb > 0))
                yield

        def p4_ln(t16):
            xt, xr = XT[t16 % 2], rXT[t16 % 2]
            RS, rRS = RSB[t16 % 2], rRSB[t16 % 2]
            rst = rST4[t16 % 2]
            c0 = (t16 % 2) * 8
            mean, nb_, ssq, rstd, msq = (ST[:, c0 + i:c0 + i + 1] for i in range(5))
            S.op("act", lambda e: e.activation(out=xt, in_=RS, func=AF.Copy, accum_out=mean), reads=[rRS], writes=[xr, rst])
            yield
            S.op("act", lambda e: e.activation(out=xt, in_=RS, func=AF.Square, accum_out=ssq), reads=[rRS], writes=[xr, rst])
            yield
            S.op("dve", lambda e: e.tensor_scalar_mul(out=mean, in0=mean, scalar1=1.0 / D), reads=[rst], writes=[rst])
            S.op("dve", lambda e: e.tensor_mul(out=msq, in0=mean, in1=mean), reads=[rst], writes=[rst])
            yield
            S.op("dve", lambda e: e.scalar_tensor_tensor(out=rstd, in0=ssq, scalar=1.0 / D, in1=msq, op0=ALU.mult, op1=ALU.subtract),
                 reads=[rst], writes=[rst])
            S.op("dve", lambda e: e.tensor_scalar_add(out=rstd, in0=rstd, scalar1=LN_EPS), reads=[rst], writes=[rst])
            S.op("act", lambda e: e.activation(out=rstd, in_=rstd, func=AF.Sqrt), reads=[rst], writes=[rst])
            yield
            S.op("dve", lambda e: e.reciprocal(out=rstd, in_=rstd), reads=[rst], writes=[rst])
            S.op("dve", lambda e: e.scalar_tensor_tensor(out=nb_, in0=mean, scalar=-1.0, in1=rstd, op0=ALU.mult, op1=ALU.mult),
                 reads=[rst], writes=[rst])
            S.op("act", lambda e: e.activation(out=xt, in_=RS, func=AF.Identity, scale=rstd, bias=nb_), reads=[rRS, rst], writes=[xr])
            yield
            S.op("dve", lambda e: e.tensor_mul(out=xt, in0=xt, in1=LW), reads=[xr, rGR], writes=[xr])
            yield
            S.op("dve", lambda e: e.tensor_add(out=xt, in0=xt, in1=LB), reads=[xr, rGR], writes=[xr])
            S.dma("sp", xout[t16 * 128:(t16 + 1) * 128, :], xt, reads=[xr], writes=[rX1 if xout is x1 else rOUT])
            yield

        run_gens([(p4_mm(0), 1)])
        for t16 in range(16):
            gens = [(p4_ln(t16), 2)]
            if t16 + 1 < 16:
                gens.insert(0, (p4_mm(t16 + 1), 1))
            run_gens(gens)
        S.barrier()

    S.finish("sp")
    return nc, S


def _fm(v, chunks):
    v = np.asarray(v)
    return np.ascontiguousarray(np.moveaxis(v.reshape((chunks, 128) + v.shape[1:]), 0, 1))


def _pack_cp(inp, l):
    cp = np.zeros((128, NCP), np.float32)

    def put(name, arr):
        a, w = CP[name]
        cp[:, a:a + w] = np.asarray(arr, np.float32).reshape(128, w)
    put("ssd_cw", _fm(np.asarray(inp["ssd_conv_w"][l]).T, 16))
    put("ssd_cb", _fm(inp["ssd_conv_b"][l], 16))
    put("ssd_nw", _fm(inp["ssd_norm_w"][l], 8))
    put("ssd_d", _fm(np.repeat(np.asarray(inp["ssd_d"][l]), 64), 8))
    put("sconv_w", _fm(np.asarray(inp["sconv_w"][l]).T, 8))
    put("lru_cw", _fm(np.asarray(inp["lru_conv_w"][l]).T, 8))
    put("lru_cb", _fm(inp["lru_conv_b"][l], 8))
    put("lru_ba", _fm(inp["lru_b_a"][l], 8))
    put("lru_bx", _fm(inp["lru_b_x"][l], 8))
    put("lru_lam", _fm(inp["lru_lambda"][l], 8))
    put("b_gate", np.stack([_fm(np.asarray(inp["b_gate"][l][k]), 16) for k in range(4)], axis=1))
    put("sinks", np.broadcast_to(np.asarray(inp["attn_sinks"][l])[None, :], (128, 16)))
    put("dt_bias", np.broadcast_to(np.asarray(inp["ssd_dt_bias"][l])[None, :], (128, 16)))
    put("a_log", np.broadcast_to(np.asarray(inp["ssd_a_log"][l])[None, :], (128, 16)))
    return cp


def _pack_lruw(inp):
    out = np.zeros((DEPTH, 2, 8, 128, 128), np.float32)
    for l in range(DEPTH):
        for gi, key in enumerate(("lru_w_a", "lru_w_x")):
            w = np.asarray(inp[key][l])
            for j in range(8):
                out[l, gi, j, 0:64, 0:64] = w[2 * j]
                out[l, gi, j, 64:128, 64:128] = w[2 * j + 1]
    return out


def _konst():
    k = np.zeros((128, NKO), np.float32)

    def put(name, arr):
        a, w = KO[name]
        k[:, a:a + w] = arr
    put("ident", np.eye(128, dtype=np.float32))
    put("triu", np.triu(np.ones((128, 128), np.float32)))
    put("ones", np.ones((128, 128), np.float32))
    rot = np.zeros((128, 128), np.float32)
    for p in range(128):
        if p % 64 < 32:
            rot[p + 32, p] = -1.0
        else:
            rot[p - 32, p] = 1.0
    put("rot", rot)
    half = 32
    invf = (10000.0 ** (-np.arange(half, dtype=np.float32) / half)).astype(np.float32)
    put("invf", np.tile(invf, 4)[:, None])
    qi = np.arange(128)[:, None]
    sj = np.arange(256)[None, :]
    rel = qi + 128 - sj
    band = (rel >= 0) & (rel < 128)
    put("amask", np.where(band, 0.0, -30000.0).astype(np.float32))
    put("amask0", np.where(band & (sj >= 128), 0.0, -30000.0).astype(np.float32))
    put("halfpi", np.full((128, 1), np.pi / 2, np.float32))
    s_ = np.arange(128)[:, None]
    l_ = np.arange(128)[None, :]
    put("smask", np.where(l_ >= s_, 0.0, -30000.0).astype(np.float32))
    return k


_CACHE = {}


def make_in_maps(inp, cores):
    f = lambda k: np.ascontiguousarray(np.asarray(inp[k], dtype=np.float32))
    shared = {
        "w_ada": f("w_ada"), "b_ada": f("b_ada").reshape(DEPTH, 1, 3 * D), "w_in": f("w_in"),
        "w_branch": f("w_branch"), "w_out": f("w_out"),
        "ln_wb": np.ascontiguousarray(np.stack([f("ln_w"), f("ln_b")], axis=1).reshape(DEPTH, 2, 1, D)),
        "cp": np.stack([_pack_cp(inp, l) for l in range(DEPTH)], axis=0),
        "konst": _konst(), "lruw": _pack_lruw(inp),
    }
    x = np.asarray(inp["x"], dtype=np.float32)
    c = np.asarray(inp["c"], dtype=np.float32)
    pos = np.asarray(inp["positions"]).astype(np.int32)
    maps = []
    for b in cores:
        m = dict(shared)
        m["x"] = np.ascontiguousarray(x[b])
        m["c"] = np.ascontiguousarray(c[b].reshape(NKC, 128).T)
        m["pos"] = np.ascontiguousarray(pos[b].reshape(1, L))
        maps.append(m)
    return maps


def kernel(**inputs):
    if "nc" not in _CACHE:
        _CACHE["nc"] = build_program()[0]
    nc = _CACHE["nc"]
    maps = make_in_maps(inputs, list(range(8)))
    res = run_bass_kernel_spmd(nc, maps, core_ids=list(range(8)))
    return np.stack([np.asarray(r["y"], dtype=np.float32) for r in res.results], axis=0)
```
